# Optimizing a Trainium2 kernel written in Bass

```python
import jax, jax.numpy as jnp
from jax import lax
import numpy as np

D_MODEL = 1024
BATCH = 32
SEQ = 2048
DEPTH = 1

N_HEADS = 8
HEAD_DIM = 64
ATTN_WIDTH = N_HEADS * HEAD_DIM
IDX_HEADS = 8
IDX_DIM = 64
TOPK_MAX = 256
Q_BLOCK = 64
POOL_WIDTH = D_MODEL - ATTN_WIDTH
POOL_WINDOWS = (2, 4, 8, 16)
N_POOL_GROUPS = len(POOL_WINDOWS)
POOL_GROUP_DIM = POOL_WIDTH // N_POOL_GROUPS
D_FF = -(-8 * D_MODEL // (3 * 256)) * 256
ROPE_THETA = 10000.0
LN_EPS = 1e-5
ALPHA = (2.0 * DEPTH) ** 0.25
BETA = (8.0 * DEPTH) ** -0.25
N_MOD = 6
IN_SPLITS = (ATTN_WIDTH, ATTN_WIDTH, ATTN_WIDTH, POOL_WIDTH,
             IDX_HEADS * IDX_DIM, IDX_DIM, IDX_HEADS)
W_IN_COLS = sum(IN_SPLITS)

kernel_name = "hymba_dsa_pool_deepnorm_adaln"


def layer_norm(x, g, b):
    xf = x.astype(jnp.float32)
    mu = jnp.mean(xf, axis=-1, keepdims=True)
    var = jnp.mean(jnp.square(xf - mu), axis=-1, keepdims=True)
    y = (xf - mu) * lax.rsqrt(var + LN_EPS)
    return (y * g.astype(jnp.float32) + b.astype(jnp.float32)).astype(x.dtype)


def rope(x, pos):
    d = x.shape[-1]
    inv_freq = ROPE_THETA ** (-jnp.arange(0, d, 2, dtype=jnp.float32) / d)
    ang = pos[:, None] * inv_freq[None, :]
    cos = jnp.cos(ang)[None, :, None, :]
    sin = jnp.sin(ang)[None, :, None, :]
    xf = x.astype(jnp.float32)
    x1, x2 = jnp.split(xf, 2, axis=-1)
    out = jnp.concatenate([x1 * cos - x2 * sin, x2 * cos + x1 * sin], axis=-1)
    return out.astype(x.dtype)


def dsa_sparse_attention(q, k, v, iq, ik, iw):
    B, S = q.shape[0], q.shape[1]
    k_sel = min(TOPK_MAX, S // 4)
    nblk = S // Q_BLOCK
    pos_k = jnp.arange(S)
    ik_f = ik.astype(jnp.float32)

    def to_blocks(a):
        return a.reshape((B, nblk, Q_BLOCK) + a.shape[2:]).swapaxes(0, 1)

    def one_block(args):
        qb, iqb, iwb, t0 = args
        pos_q = t0 + jnp.arange(Q_BLOCK)
        causal = pos_k[None, :] <= pos_q[:, None]
        logits = jnp.einsum('bqhd,bsd->bqhs', iqb.astype(jnp.float32), ik_f) * (IDX_DIM ** -0.5)
        w = iwb.astype(jnp.float32) * (IDX_HEADS ** -0.5)
        score = jnp.einsum('bqh,bqhs->bqs', w, jax.nn.relu(logits))
        score = jnp.where(causal[None], score, -jnp.inf)
        _, idx = lax.top_k(score, k_sel)
        kg = jax.vmap(lambda kb, ib: kb[ib])(k, idx)
        vg = jax.vmap(lambda vb, ib: vb[ib])(v, idx)
        valid = idx <= pos_q[None, :, None]
        s = jnp.einsum('bqhd,bqkhd->bhqk', qb.astype(jnp.float32),
                       kg.astype(jnp.float32)) * (HEAD_DIM ** -0.5)
        s = jnp.where(valid[:, None], s, -jnp.inf)
        p = jax.nn.softmax(s, axis=-1)
        o = jnp.einsum('bhqk,bqkhd->bqhd', p, vg.astype(jnp.float32))
        return o.astype(q.dtype)

    starts = jnp.arange(nblk, dtype=jnp.int32) * Q_BLOCK
    out = lax.map(one_block, (to_blocks(q), to_blocks(iq), to_blocks(iw), starts))
    return out.swapaxes(0, 1).reshape(B, S, N_HEADS * HEAD_DIM)


def causal_multiscale_pool(u):
    B, S = u.shape[0], u.shape[1]
    ug = u.reshape(B, S, N_POOL_GROUPS, POOL_GROUP_DIM).astype(jnp.float32)
    cs = jnp.concatenate([jnp.zeros((B, 1, N_POOL_GROUPS, POOL_GROUP_DIM), jnp.float32),
                          jnp.cumsum(ug, axis=1)], axis=1)
    t = jnp.arange(S)
    outs = []
    for g, win in enumerate(POOL_WINDOWS):
        start = jnp.maximum(t + 1 - win, 0)
        sums = cs[:, t + 1, g] - cs[:, start, g]
        cnt = (t + 1 - start).astype(jnp.float32)
        outs.append(sums / cnt[None, :, None])
    pooled = jnp.stack(outs, axis=2)
    return (pooled - ug).astype(u.dtype)


def hybrid_mixer(h, w_in, w_pool, pool_scale, w_o):
    B, S = h.shape[0], h.shape[1]
    proj = h @ w_in
    offs = list(np.cumsum(IN_SPLITS)[:-1])
    q, k, v, u, iq, ik, iw = jnp.split(proj, offs, axis=-1)
    pos = jnp.arange(S, dtype=jnp.float32)
    q = rope(q.reshape(B, S, N_HEADS, HEAD_DIM), pos)
    k = rope(k.reshape(B, S, N_HEADS, HEAD_DIM), pos)
    v = v.reshape(B, S, N_HEADS, HEAD_DIM)
    iq = rope(iq.reshape(B, S, IDX_HEADS, IDX_DIM), pos)
    ik = rope(ik[:, :, None, :], pos)[:, :, 0, :]
    attn = dsa_sparse_attention(q, k, v, iq, ik, iw)
    pooled = causal_multiscale_pool(u)
    pool_out = jnp.einsum('bsgc,gcd->bsgd', pooled, w_pool).reshape(B, S, POOL_WIDTH) * pool_scale
    return jnp.concatenate([attn, pool_out], axis=-1) @ w_o


def swiglu(h, w_gate, w_up, w_down):
    return (jax.nn.silu(h @ w_gate) * (h @ w_up)) @ w_down


def setup_inputs(seed: int = 0) -> dict:
    key = jax.random.key(seed)
    ks = jax.random.split(key, 16)
    f32 = jnp.float32
    x = jax.random.normal(ks[0], (BATCH, SEQ, D_MODEL), f32)
    c = jax.random.normal(ks[1], (BATCH, D_MODEL), f32)
    w_mod = jax.random.normal(ks[2], (DEPTH, D_MODEL, N_MOD * D_MODEL), f32) * (0.5 * D_MODEL ** -0.5)
    b_mod = 0.02 * jax.random.normal(ks[3], (DEPTH, N_MOD * D_MODEL), f32)
    col_scale = np.concatenate([
        np.full(ATTN_WIDTH, 1.0), np.full(ATTN_WIDTH, 1.0),
        np.full(ATTN_WIDTH, BETA), np.full(POOL_WIDTH, BETA),
        np.full(IDX_HEADS * IDX_DIM, 1.0), np.full(IDX_DIM, 1.0), np.full(IDX_HEADS, 1.0)
    ]).astype(np.float32) * (D_MODEL ** -0.5)
    w_in = jax.random.normal(ks[4], (DEPTH, D_MODEL, W_IN_COLS), f32) * jnp.asarray(col_scale)
    w_pool = jax.random.normal(ks[5], (DEPTH, N_POOL_GROUPS, POOL_GROUP_DIM, POOL_GROUP_DIM), f32) * (POOL_GROUP_DIM ** -0.5)
    pool_scale = 1.0 + 0.1 * jax.random.normal(ks[6], (DEPTH, POOL_WIDTH), f32)
    w_o = jax.random.normal(ks[7], (DEPTH, D_MODEL, D_MODEL), f32) * (BETA * D_MODEL ** -0.5)
    ln1_g = 1.0 + 0.02 * jax.random.normal(ks[8], (DEPTH, D_MODEL), f32)
    ln1_b = 0.02 * jax.random.normal(ks[9], (DEPTH, D_MODEL), f32)
    w_gate = jax.random.normal(ks[10], (DEPTH, D_MODEL, D_FF), f32) * (D_MODEL ** -0.5)
    w_up = jax.random.normal(ks[11], (DEPTH, D_MODEL, D_FF), f32) * (BETA * D_MODEL ** -0.5)
    w_down = jax.random.normal(ks[12], (DEPTH, D_FF, D_MODEL), f32) * (BETA * D_FF ** -0.5)
    ln2_g = 1.0 + 0.02 * jax.random.normal(ks[13], (DEPTH, D_MODEL), f32)
    ln2_b = 0.02 * jax.random.normal(ks[14], (DEPTH, D_MODEL), f32)
    return {"x": x, "c": c, "w_mod": w_mod, "b_mod": b_mod, "w_in": w_in,
            "w_pool": w_pool, "pool_scale": pool_scale, "w_o": w_o,
            "ln1_g": ln1_g, "ln1_b": ln1_b, "w_gate": w_gate, "w_up": w_up,
            "w_down": w_down, "ln2_g": ln2_g, "ln2_b": ln2_b}


def reference(x, c, w_mod, b_mod, w_in, w_pool, pool_scale, w_o,
              ln1_g, ln1_b, w_gate, w_up, w_down, ln2_g, ln2_b):
    for l in range(DEPTH):
        mod = jax.nn.silu(c) @ w_mod[l] + b_mod[l]
        shift1, scale1, gate1, shift2, scale2, gate2 = [m[:, None, :] for m in jnp.split(mod, N_MOD, axis=-1)]
        h = x * (1.0 + scale1) + shift1
        mix = hybrid_mixer(h, w_in[l], w_pool[l], pool_scale[l], w_o[l])
        x = layer_norm(ALPHA * x + gate1 * mix, ln1_g[l], ln1_b[l])
        h = x * (1.0 + scale2) + shift2
        ffn = swiglu(h, w_gate[l], w_up[l], w_down[l])
        x = layer_norm(ALPHA * x + gate2 * ffn, ln2_g[l], ln2_b[l])
    return x
```

```python
import numpy as np
import concourse.bass as bass
import concourse.mybir as mybir
from concourse.bass_utils import run_bass_kernel_spmd

F32 = mybir.dt.float32
BF16 = mybir.dt.bfloat16
U8 = mybir.dt.uint8
AF = mybir.ActivationFunctionType
ALU = mybir.AluOpType

D = 1024
KC = 8
DFF = 2816
FC = 22
WIN = 2632
NMOD = 6
ALPHA = float(2.0 ** 0.25)
EPS = 1e-5
BIG = 32768.0
N_CORES = 8
POOL_WINDOWS = (2, 4, 8, 16)
SAME_WAIT = True
NO_SELF_WAIT = ("tensor",)
STOP = 99
DEBUG = 0


class StopBuild(Exception):
    pass


def stop_at(k):
    if STOP == k:
        raise StopBuild()


class Tok:
    __slots__ = ("sem", "val")

    def __init__(self, sem, val):
        self.sem = sem
        self.val = val


class Sem:
    def __init__(self, h):
        self.h = h
        self.count = 0
        self.last = None


class Buf:
    __slots__ = ("name", "w", "r")

    def __init__(self, name=""):
        self.name = name
        self.w = {}
        self.r = {}


class Eng:
    def __init__(self, name, sem):
        self.name = name
        self.sem = sem
        self.ops = []
        self.seen = {}
        self.pending = []
        self.dma = []
        self.dma_i = 0


class Prog:
    def __init__(self, nc, sems):
        self.nc = nc
        it = iter(sems)
        self.eng = {}
        for n in ("tensor", "vector", "scalar", "gpsimd", "sync"):
            self.eng[n] = Eng(n, Sem(next(it)))
        self.eng["sync"].dma = [Sem(next(it)) for _ in range(40)]
        self.eng["gpsimd"].dma = [Sem(next(it)) for _ in range(16)]
        self.n_ins = 0

    def emit(self, eng, fn, reads=(), writes=(), sig=True, dma=False):
        E = self.eng[eng]
        need = {}

        def add(tok):
            if tok.sem is E.sem:
                if eng in NO_SELF_WAIT or not SAME_WAIT:
                    return
            assert tok.val is not None, f"unresolved token needed by {eng}"
            cur = need.get(tok.sem)
            if cur is None or tok.val > cur:
                need[tok.sem] = tok.val

        for b in reads:
            for t in b.w.values():
                add(t)
        for b in writes:
            for t in b.w.values():
                add(t)
            for t in b.r.values():
                add(t)
        if dma:
            s = E.dma[E.dma_i % len(E.dma)]
            E.dma_i += 1
            if s.last is not None:
                add(s.last)
        for sem, val in need.items():
            if E.seen.get(sem, 0) < val:
                E.ops.append(("wait", sem.h, val))
                E.seen[sem] = val
        if dma:
            s.count += 16
            tok = Tok(s, s.count)
            s.last = tok
            E.ops.append(("ins", fn, s.h, 16))
        elif sig:
            E.sem.count += 1
            tok = Tok(E.sem, E.sem.count)
            for p in E.pending:
                p.val = tok.val
            E.pending = []
            E.ops.append(("ins", fn, E.sem.h, 1))
        else:
            tok = Tok(E.sem, None)
            E.pending.append(tok)
            E.ops.append(("ins", fn, None, 0))
        for b in writes:
            b.w = {tok.sem: tok}
            b.r = {}
        for b in reads:
            if b not in writes:
                b.r[tok.sem] = tok
        self.n_ins += 1
        return tok

    def barrier(self):
        toks = []
        for E in self.eng.values():
            assert not E.pending
            if E.sem.count:
                toks.append((E.sem, E.sem.count))
            for s in E.dma:
                if s.last is not None:
                    toks.append((s, s.last.val))
        for E in self.eng.values():
            for sem, val in toks:
                if sem is E.sem:
                    continue
                if E.seen.get(sem, 0) < val:
                    E.ops.append(("wait", sem.h, val))
                    E.seen[sem] = val

    def finish(self):
        E = self.eng["sync"]
        for En in self.eng.values():
            for s in En.dma:
                if s.last is not None and E.seen.get(s, 0) < s.last.val:
                    E.ops.append(("wait", s.h, s.last.val))
                    E.seen[s] = s.last.val

    def replay(self, block):
        def mk(name):
            E = self.eng[name]

            def body(e):
                for op in E.ops:
                    if op[0] == "wait":
                        e.wait_ge(op[1], op[2])
                    else:
                        ins = op[1](e)
                        if op[2] is not None:
                            ins.then_inc(op[2], op[3])
            return body
        block.sync(mk("sync"))
        block.tensor(mk("tensor"))
        block.vector(mk("vector"))
        block.scalar(mk("scalar"))
        block.gpsimd(mk("gpsimd"))

    def mm(self, out, lhsT, rhs, start, stop, reads, writes, sig=True):
        return self.emit("tensor", lambda e: e.matmul(out, lhsT=lhsT, rhs=rhs, start=start, stop=stop),
                         reads, writes, sig=sig)

    def tr(self, out, in_, ident, reads, writes, sig=True):
        return self.emit("tensor", lambda e: e.transpose(out=out, in_=in_, identity=ident), reads, writes, sig=sig)

    def act(self, out, in_, func, reads, writes, bias=None, scale=None):
        kw = {}
        if bias is not None:
            kw["bias"] = bias
        if scale is not None:
            kw["scale"] = scale
        return self.emit("scalar", lambda e: e.activation(out=out, in_=in_, func=func, **kw), reads, writes)

    def ts(self, eng, out, in0, s1, s2, op0, op1, reads, writes, accum=None):
        kw = {}
        if op1 is not None:
            kw["op1"] = op1
        if accum is not None:
            kw["accum_out"] = accum
        return self.emit(eng, lambda e: e.tensor_scalar(out=out, in0=in0, scalar1=s1, scalar2=s2, op0=op0, **kw),
                         reads, writes)

    def tt(self, eng, out, in0, in1, op, reads, writes):
        return self.emit(eng, lambda e: e.tensor_tensor(out=out, in0=in0, in1=in1, op=op), reads, writes)

    def stt(self, eng, out, in0, scalar, in1, op0, op1, reads, writes):
        return self.emit(eng, lambda e: e.scalar_tensor_tensor(out=out, in0=in0, scalar=scalar, in1=in1,
                                                               op0=op0, op1=op1), reads, writes)

    def cp(self, eng, out, in_, reads, writes):
        if eng == "scalar":
            return self.emit(eng, lambda e: e.copy(out=out, in_=in_), reads, writes)
        return self.emit(eng, lambda e: e.tensor_copy(out=out, in_=in_), reads, writes)

    def memset(self, eng, ap, val, writes):
        return self.emit(eng, lambda e: e.memset(ap, val), (), writes)

    def dma(self, eng, out, in_, reads, writes, slow=False):
        if slow:
            return self.emit(eng, lambda e: e.dma_start(out=out, in_=in_, allow_slow_non_contiguous=True),
                             reads, writes, dma=True)
        return self.emit(eng, lambda e: e.dma_start(out=out, in_=in_), reads, writes, dma=True)


class SB:
    def __init__(self, nc, base, size):
        self.nc = nc
        self.base = base
        self.cur = base
        self.end = base + size
        self.k = 0
        self.peak = base

    def alloc(self, shape, dtype, name="t"):
        isz = 4 if dtype == F32 else 2
        n = 1
        for s in shape[1:]:
            n *= s
        off = (self.cur + 63) // 64 * 64
        self.k += 1
        h = self.nc.alloc_sbuf_tensor_at(f"{name}{self.k}", list(shape), dtype, offset=off)
        self.cur = off + n * isz
        assert self.cur <= self.end, f"SBUF overflow allocating {name}: {self.cur - self.base} > {self.end - self.base}"
        self.peak = max(self.peak, self.cur)
        return h

    def mark(self):
        return self.cur

    def reset(self, m):
        self.cur = m


class Ring:
    def __init__(self, items):
        self.items = items
        self.i = 0

    def next(self):
        it = self.items[self.i % len(self.items)]
        self.i += 1
        return it


def build_program(NB, S, KSEL, NITER):
    NT = S // 128
    nc = bass.Bass("TRN2", target_bir_lowering=False)

    def din(name, shape):
        return nc.dram_tensor(name, list(shape), F32, kind="ExternalInput").ap()

    x_d = din("x", [NB, S, D])
    c_d = din("c", [NB, D])
    wmod_d = din("w_mod", [D, NMOD * D])
    bmod_d = din("b_mod", [1, NMOD * D])
    win_d = din("w_in", [D, WIN])
    wpool_d = din("w_pool", [4, 128, 128])
    pscale_d = din("pool_scale", [512])
    wo_d = din("w_o", [D, D])
    ln1g_d = din("ln1_g", [1, D])
    ln1b_d = din("ln1_b", [1, D])
    wg_d = din("w_gate", [D, DFF])
    wu_d = din("w_up", [D, DFF])
    wd_d = din("w_down", [DFF, D])
    ln2g_d = din("ln2_g", [1, D])
    ln2b_d = din("ln2_b", [1, D])
    cos_d = din("k_cos", [S, 32])
    sin_d = din("k_sin", [S, 32])
    nsin_d = din("k_nsin", [S, 32])
    ident_d = din("k_ident", [128, 128])
    cb_d = din("k_cb", [128, 128])
    bands_d = din("k_bands", [12, 128, 128])
    pw_d = din("k_pw", [128, 2 * NITER])
    y_d = nc.dram_tensor("y", [NB, S, D], F32, kind="ExternalOutput").ap()
    if DEBUG:
        dbg_score = nc.dram_tensor("dbg_score", [128, S], F32, kind="ExternalOutput").ap()
        dbg_mb = nc.dram_tensor("dbg_mb", [128, S], F32, kind="ExternalOutput").ap()
        dbg_bis = nc.dram_tensor("dbg_bis", [128, 2 * NITER + 8], F32, kind="ExternalOutput").ap()
        dbg_w = nc.dram_tensor("dbg_w", [128, 8], F32, kind="ExternalOutput").ap()
    modscr_d = nc.dram_tensor("mod_scr", [NB, NMOD * D], F32).ap()
    x1scr_d = nc.dram_tensor("x1_scr", [NB * S, D], F32).ap()

    ARENA = 207 * 1024
    arena = nc.alloc_sbuf_tensor("arena", [128, ARENA], U8)
    base = nc.lookup_mloc(arena).addr
    sb = SB(nc, base, ARENA)
    banks = [nc.alloc_psum_tensor(f"bank{i}", [128, 512], F32) for i in range(8)]
    bankb = [Buf(f"bank{i}") for i in range(8)]

    sem_ctx = [nc.semaphore(f"s{i}") for i in range(5 + 40 + 16)]
    sems = [c.__enter__() for c in sem_ctx]
    P = Prog(nc, sems)

    try:
        ident_f = sb.alloc([128, 128], F32, "identf")
        ident_b = sb.alloc([128, 128], BF16, "identb")
        identx4 = sb.alloc([128, 512], BF16, "identx4")
        cb_f = sb.alloc([128, 128], F32, "cbf")
        cb_b = sb.alloc([128, 128], BF16, "cbb")
        mhalf = sb.alloc([128, 1], F32, "mhalf")
        consts = Buf("consts")
        P.dma("sync", ident_f[:], ident_d, (), [consts])
        P.dma("sync", cb_f[:], cb_d, (), [consts])
        P.cp("vector", ident_b[:], ident_f[:], [consts], [consts])
        for r in range(4):
            P.cp("vector", identx4[:, r * 128:(r + 1) * 128], ident_f[:], [consts], [consts])
        P.cp("vector", cb_b[:], cb_f[:], [consts], [consts])
        P.memset("vector", mhalf[:], -0.5, [consts])

        stop_at(0)
        rr = {"i": 0}

        def bank(lo=0, hi=5):
            k = lo + rr["i"] % (hi - lo)
            rr["i"] += 1
            return banks[k], bankb[k]

        m0 = sb.mark()
        cT = sb.alloc([128, KC, NB], F32, "cT")
        siluT = sb.alloc([128, KC, NB], BF16, "siluT")
        bmod_b = sb.alloc([128, NMOD * D], F32, "bmodb")
        modrow = sb.alloc([128, NMOD * D], F32, "modrow")
        wm = [sb.alloc([128, KC, 512], BF16, "wm") for _ in range(2)]
        cTb, bmb, mrb = Buf(), Buf(), Buf()
        wmb = [[Buf() for _ in range(KC)] for _ in range(2)]
        for b in range(NB):
            P.dma("sync", cT[:, :, b], c_d[b].rearrange("(k p) -> p k", p=128), (), [cTb], slow=True)
        P.dma("sync", bmod_b[0:NB, :], bmod_d[0:1, :].partition_broadcast(NB), (), [bmb])
        P.act(siluT[:], cT[:], AF.Silu, [cTb], [cTb])
        wmod_v = wmod_d.rearrange("(k p) c -> p k c", p=128)
        for blk in range(NMOD * 2):
            w = wm[blk % 2]
            wb = wmb[blk % 2]
            for kc in range(KC):
                P.dma("gpsimd", w[:, kc, :], wmod_v[:, kc, blk * 512:(blk + 1) * 512], (), [wb[kc]])
            ps, psb = bank()
            for kc in range(KC):
                P.mm(ps[0:NB, :], siluT[:, kc, :], w[:, kc, :], kc == 0, kc == KC - 1, [cTb, wb[kc]], [psb], sig=(kc == KC - 1))
            P.tt("vector", modrow[0:NB, blk * 512:(blk + 1) * 512], ps[0:NB, :], bmod_b[0:NB, blk * 512:(blk + 1) * 512],
                 ALU.add, [psb, bmb], [mrb])
        stop_at(1)
        modscr = Buf("modscr")
        P.dma("sync", modscr_d, modrow[0:NB, :], [mrb], [modscr])
        stop_at(2)
        P.barrier()
        sb.reset(m0)

        mA = sb.mark()
        Win = sb.alloc([128, KC, WIN], BF16, "Win")
        Wo = sb.alloc([128, KC, D], BF16, "Wo")
        Wpool = sb.alloc([128, 4, 128], BF16, "Wpool")
        bands = sb.alloc([128, 12, 128], BF16, "bands")
        cos_t = sb.alloc([128, NT, 32], F32, "cos")
        sin_t = sb.alloc([128, NT, 32], F32, "sin")
        nsin_t = sb.alloc([128, NT, 32], F32, "nsin")
        pw_t = sb.alloc([128, 2 * NITER], F32, "pw")
        ln1g = sb.alloc([128, D], F32, "ln1g")
        ln1b = sb.alloc([128, D], F32, "ln1b")
        gate1 = [sb.alloc([128, D], F32, "gate1") for _ in range(2)]
        sc1 = [sb.alloc([128, KC], F32, "sc1") for _ in range(2)]
        sh1 = [sb.alloc([128, KC], F32, "sh1") for _ in range(2)]
        pe = [sb.alloc([128, 512], F32, "pe") for _ in range(3)]
        kT = sb.alloc([128, 4, S], BF16, "kT")
        ikzA = sb.alloc([128, S], BF16, "ikzA")
        ikzB = sb.alloc([128, S], BF16, "ikzB")
        V = sb.alloc([128, NT, 8, 65], BF16, "V")
        xs = [sb.alloc([128, D], F32, "xs") for _ in range(3)]
        hT = [sb.alloc([128, KC, 128], BF16, "hT") for _ in range(2)]
        qtok = sb.alloc([128, 4, 192], F32, "qtok")
        ktok = sb.alloc([128, 512], F32, "ktok")
        iqtok = sb.alloc([128, 512], F32, "iqtok")
        iktok = sb.alloc([128, 192], F32, "iktok")
        rt1 = [sb.alloc([128, 512], F32, "rt1") for _ in range(1)]
        rt2 = [sb.alloc([128, 512], F32, "rt2") for _ in range(1)]
        qz = [sb.alloc([128, 8, 128], BF16, "qz") for _ in range(2)]
        iqT = [sb.alloc([128, 4, 128], BF16, "iqT") for _ in range(2)]
        w_t = [sb.alloc([128, 8], F32, "wt") for _ in range(2)]
        ubuf = [sb.alloc([128, 512], BF16, "u") for _ in range(2)]
        pooledT = sb.alloc([128, 512], BF16, "pooledT")
        catT = [sb.alloc([128, KC, 128], BF16, "catT") for _ in range(2)]
        obf = sb.alloc([128, 512], F32, "obf")
        rs = sb.alloc([128, 8], F32, "rs")
        score = sb.alloc([128, S], F32, "score")
        MB = sb.alloc([128, S], BF16, "MB")
        Rr = [sb.alloc([128, 512], BF16, "R") for _ in range(4)]
        dg = sb.alloc([128, 8, 128], BF16, "dg")
        PT = [sb.alloc([128, 512], BF16, "PT") for _ in range(4)]
        bis = sb.alloc([128, 2 * NITER + 8], F32, "bis")
        ybuf = [sb.alloc([128, D], F32, "y") for _ in range(2)]
        lnst = sb.alloc([128, 24], F32, "lnst")

        WinB = [Buf() for _ in range(KC)]
        WoB = [Buf() for _ in range(KC)]
        WpoolB, cA = Buf("Wpool"), Buf("constsA")
        win_v = win_d.rearrange("(k p) c -> p k c", p=128)
        wo_v = wo_d.rearrange("(k p) c -> p k c", p=128)
        for kc in range(KC):
            P.dma("gpsimd", Win[:, kc, :], win_v[:, kc, :], (), [WinB[kc]])
        for kc in range(KC):
            P.dma("gpsimd", Wo[:, kc, :], wo_v[:, kc, :], (), [WoB[kc]])
        P.dma("gpsimd", Wpool[:], wpool_d.rearrange("g c d -> c g d"), (), [WpoolB])
        P.dma("gpsimd", bands[:], bands_d.rearrange("n a b -> a n b"), (), [cA])
        P.dma("sync", cos_t[:], cos_d.rearrange("(i p) d -> p i d", p=128), (), [cA])
        P.dma("sync", sin_t[:], sin_d.rearrange("(i p) d -> p i d", p=128), (), [cA])
        P.dma("sync", nsin_t[:], nsin_d.rearrange("(i p) d -> p i d", p=128), (), [cA])
        P.dma("sync", pw_t[:], pw_d, (), [cA])
        P.dma("sync", ln1g[:], ln1g_d[0:1, :].partition_broadcast(128), (), [cA])
        P.dma("sync", ln1b[:], ln1b_d[0:1, :].partition_broadcast(128), (), [cA])
        qtokB, ktokB, iqtokB, iktokB = Buf("qtok"), Buf("ktok"), Buf("iqtok"), Buf("iktok")
        P.memset("vector", qtok[:], 0.0, [qtokB])
        P.memset("vector", iktok[:], 0.0, [iktokB])
        VB = [Buf(f"V{i}") for i in range(NT)]
        kTB = [Buf(f"kT{i}") for i in range(NT)]
        ikzBf = [Buf(f"ikz{i}") for i in range(NT)]
        for i in range(NT):
            P.memset("gpsimd", V[:, i, :, 64:65], 1.0, [VB[i]])

        stop_at(3)
        pse, pseB = rt1[0], Buf()
        P.dma("sync", pse[:], pscale_d.rearrange("(o n) -> o n", o=1).partition_broadcast(128), (), [pseB])
        P.tt("vector", Wpool[:].rearrange("c g d -> c (g d)"), Wpool[:].rearrange("c g d -> c (g d)"), pse[:], ALU.mult,
             [pseB, WpoolB], [WpoolB])
        stop_at(31)
        xsB = [Buf("xs0"), Buf("xs1"), Buf("xs2")]
        hTB = [[Buf() for _ in range(KC)] for _ in range(2)]
        rt1B = [Buf()]
        rt2B = [Buf()]
        rt1B[0] = pseB
        peB = [Buf() for _ in range(3)]
        rt_i = {"i": 0}
        pe_i = {"i": 0}
        qzB, iqTB, wtB = [Buf(), Buf()], [Buf(), Buf()], [Buf(), Buf()]
        uB = [Buf(), Buf()]
        pooledB = Buf()
        catB = [Buf(), Buf()]
        obfB, rsB, scoreB, MBB, dgB, bisB, lnB = Buf(), Buf(), Buf(), Buf(), Buf(), Buf(), Buf()
        RB = [Buf() for _ in range(4)]
        PTB = [Buf() for _ in range(4)]
        R_i = {"i": 0}
        PT_i = {"i": 0}
        yB = [Buf(), Buf()]
        modB = [Buf("modA0"), Buf("modA1")]
        x1B = [[Buf() for _ in range(NT)] for _ in range(NB)]

        steps = bis[:, 0:NITER]
        steps2 = bis[:, NITER:2 * NITER]
        A_ap = bis[:, 2 * NITER:2 * NITER + 1]
        tau = bis[:, 2 * NITER + 1:2 * NITER + 2]
        cnt = bis[:, 2 * NITER + 2:2 * NITER + 3]
        dcol = bis[:, 2 * NITER + 3:2 * NITER + 4]

        def rope(ps, psb, H, i, dests):
            pk = pe_i["i"] % 3
            pe_i["i"] += 1
            P.cp("vector", pe[pk][:, 0:H * 64], ps, [psb], [peB[pk]])
            k = 0
            t1, t2 = rt1[k], rt2[k]
            X = pe[pk][:, 0:H * 64].rearrange("p (h two d) -> p h two d", two=2, d=32)
            t1v = t1[:, 0:H * 64].rearrange("p (h two d) -> p h two d", two=2, d=32)
            t2v = t2[:, 0:H * 64].rearrange("p (h two d) -> p h two d", two=2, d=32)
            cosb = cos_t[:, i, :].unsqueeze(1).unsqueeze(1).broadcast_to([128, H, 2, 32])
            sinb = sin_t[:, i, :].unsqueeze(1).broadcast_to([128, H, 32])
            nsinb = nsin_t[:, i, :].unsqueeze(1).broadcast_to([128, H, 32])
            P.tt("gpsimd", t1v, X, cosb, ALU.mult, [peB[pk], cA], [rt1B[k]])
            P.tt("gpsimd", t2v[:, :, 0, :], X[:, :, 1, :], nsinb, ALU.mult, [peB[pk], cA], [rt2B[k]])
            P.tt("gpsimd", t2v[:, :, 1, :], X[:, :, 0, :], sinb, ALU.mult, [peB[pk], cA], [rt2B[k]])
            for sl, dest, destB in dests:
                P.tt("gpsimd", dest, sl(t1), sl(t2), ALU.add, [rt1B[k], rt2B[k]], [destB])

        def layer_norm(src_y, yb, gam, bet, gB):
            st = lnst[:, 0:12]
            mv = lnst[:, 12:14]
            veps = lnst[:, 14:15]
            rstd = lnst[:, 15:16]
            nmr = lnst[:, 16:17]
            P.emit("vector", lambda e: e.bn_stats(out=lnst[:, 0:6], in_=src_y[:, 0:512]), [yb], [lnB])
            P.emit("vector", lambda e: e.bn_stats(out=lnst[:, 6:12], in_=src_y[:, 512:1024]), [yb], [lnB])
            P.emit("vector", lambda e: e.bn_aggr(out=mv, in_=st), [lnB], [lnB])
            P.ts("vector", veps, mv[:, 1:2], EPS, None, ALU.add, None, [lnB], [lnB])
            P.tt("gpsimd", rstd, veps, mhalf[:], ALU.pow, [lnB, consts], [lnB])
            P.stt("vector", nmr, mv[:, 0:1], -1.0, rstd, ALU.mult, ALU.mult, [lnB], [lnB])
            P.act(src_y[:], src_y[:], AF.Identity, [yb, lnB], [yb], bias=nmr, scale=rstd)
            P.tt("gpsimd", src_y[:], src_y[:], gam, ALU.mult, [yb, gB], [yb])
            P.tt("gpsimd", src_y[:], src_y[:], bet, ALU.add, [yb, gB], [yb])

        full = lambda H: (lambda t: t[:, 0:H * 64].rearrange("p (h two d) -> p h two d", two=2, d=32))

        def S1a(b, i):
            k2 = i % 2
            k3 = (b * NT + i) % 3
            mb = b % 2
            if i == 0:
                mrow = modscr_d[b]
                P.dma("sync", sc1[mb][:], mrow[1 * D:2 * D].rearrange("(k p) -> p k", p=128), [modscr], [modB[mb]],
                      slow=True)
                P.dma("sync", sh1[mb][:], mrow[0 * D:1 * D].rearrange("(k p) -> p k", p=128), [modscr], [modB[mb]],
                      slow=True)
                P.dma("sync", gate1[mb][:], modscr_d[b:b + 1, 2 * D:3 * D].partition_broadcast(128), [modscr],
                      [modB[mb]])
                P.ts("vector", sc1[mb][:], sc1[mb][:], 1.0, None, ALU.add, None, [modB[mb]], [modB[mb]])
            P.dma("sync", xs[k3][:], x_d[b, i * 128:(i + 1) * 128, :], (), [xsB[k3]])
            for half in range(2):
                ps, psb = bank()
                for cc in range(4):
                    kc = half * 4 + cc
                    P.tr(ps[:, cc * 128:(cc + 1) * 128], xs[k3][:, kc * 128:(kc + 1) * 128], ident_f[:],
                         [xsB[k3], consts], [psb], sig=(cc == 3))
                for cc in range(4):
                    kc = half * 4 + cc
                    P.act(hT[k2][:, kc, :], ps[:, cc * 128:(cc + 1) * 128], AF.Identity, [psb, modB[mb]], [hTB[k2][kc]],
                          bias=sh1[mb][:, kc:kc + 1], scale=sc1[mb][:, kc:kc + 1])

            def proj(c0, c1):
                ps, psb = bank()
                for kc in range(KC):
                    P.mm(ps[:, 0:c1 - c0], hT[k2][:, kc, :], Win[:, kc, c0:c1], kc == 0, kc == KC - 1,
                         [hTB[k2][kc], WinB[kc]], [psb], sig=(kc == KC - 1))
                return ps, psb

            ps, psb = proj(512, 1024)
            rope(ps[:, 0:512], psb, 8, i, [(full(8), ktok[:].rearrange("p (h two d) -> p h two d", two=2, d=32), ktokB)])
            ps, psb = proj(0, 512)
            qd = []
            for par in range(2):
                qd.append(((lambda par: (lambda t: t[:].rearrange("p (c b d) -> p c b d", c=4, b=2)[:, :, par, :]))(par),
                           qtok[:, :, par * 128:par * 128 + 64], qtokB))
            rope(ps[:, 0:512], psb, 8, i, qd)
            ps, psb = proj(2048, 2560)
            rope(ps[:, 0:512], psb, 8, i, [(full(8), iqtok[:].rearrange("p (h two d) -> p h two d", two=2, d=32), iqtokB)])
            ps, psb = proj(2560, 2632)
            P.cp("vector", w_t[k2][:], ps[:, 64:72], [psb], [wtB[k2]])
            if 128 * (i + 1) > KSEL:
                for h in range(8):
                    P.ts("vector", dg[:, h, :], ident_b[:], w_t[k2][:, h:h + 1], None, ALU.mult, None,
                         [consts, wtB[k2]], [dgB])
            rope(ps[:, 0:64], psb, 1, i,
                 [(full(1), iktok[:, 64:128].rearrange("p (h two d) -> p h two d", two=2, d=32), iktokB)])
            ps, psb = proj(1024, 1536)
            P.cp("vector", V[:, i, :, 0:64], ps[:, 0:512].rearrange("p (h d) -> p h d", d=64), [psb], [VB[i]])
            ps, psb = proj(1536, 2048)
            P.cp("vector", ubuf[k2][:], ps[:, 0:512], [psb], [uB[k2]])

        def S1b(b, i):
            k2 = i % 2
            n = 128 * (i + 1)
            ps, psb = bank()
            for c in range(4):
                P.tr(ps[:, c * 128:(c + 1) * 128], iqtok[:, c * 128:(c + 1) * 128], ident_f[:], [iqtokB, consts], [psb],
                     sig=(c == 3))
            P.cp("vector", iqT[k2][:], ps[:, 0:512].rearrange("p (c t) -> p c t", c=4), [psb], [iqTB[k2]])
            ps, psb = bank()
            P.tr(ps[:, 0:128], iktok[:, 64:192], ident_f[:], [iktokB, consts], [psb], sig=False)
            P.tr(ps[:, 128:256], iktok[:, 0:128], ident_f[:], [iktokB, consts], [psb], sig=True)
            P.cp("vector", ikzA[:, i * 128:(i + 1) * 128], ps[:, 0:128], [psb], [ikzBf[i]])
            P.cp("vector", ikzB[:, i * 128:(i + 1) * 128], ps[:, 128:256], [psb], [ikzBf[i]])
            if n > KSEL:
                for blk in range((n + 511) // 512):
                    c0 = blk * 512
                    c1 = min(n, c0 + 512)
                    wd = c1 - c0
                    jt = list(range(c0 // 128, c1 // 128))
                    psS, psSb = banks[5], bankb[5]
                    def logit(h):
                        ikz = ikzA if h % 2 == 0 else ikzB
                        psL, psLb = bank()
                        P.mm(psL[:, 0:wd], iqT[k2][:, h // 2, :], ikz[:, c0:c1], True, True,
                             [iqTB[k2]] + [ikzBf[j] for j in jt], [psLb])
                        r = R_i["i"] % 4
                        R_i["i"] += 1
                        P.act(Rr[r][:, 0:wd], psL[:, 0:wd], AF.Relu, [psLb], [RB[r]])
                        return r

                    rl = [logit(0), logit(1)]
                    for h in range(8):
                        if h + 2 < 8:
                            rl.append(logit(h + 2))
                        r = rl[h]
                        P.mm(psS[:, 0:wd], dg[:, h, :], Rr[r][:, 0:wd], h == 0, h == 7, [dgB, RB[r]], [psSb],
                             sig=(h == 7))
                    P.cp("scalar", score[:, c0:c1], psS[:, 0:wd], [psSb], [scoreB])


        def S1c(b, i):
            k2 = i % 2
            n = 128 * (i + 1)
            ps, psb = bank()
            for c in range(4):
                P.tr(ps[:, c * 128:(c + 1) * 128], ktok[:, c * 128:(c + 1) * 128], ident_f[:], [ktokB, consts], [psb],
                     sig=(c == 3))
            P.cp("scalar", kT[:, :, i * 128:(i + 1) * 128], ps[:, 0:512].rearrange("p (c t) -> p c t", c=4),
                 [psb], [kTB[i]])
            for hp in range(2):
                ps, psb = bank()
                for cc in range(2):
                    c = hp * 2 + cc
                    P.tr(ps[:, (2 * cc) * 128:(2 * cc + 1) * 128], qtok[:, c, 0:128], ident_f[:], [qtokB, consts], [psb],
                         sig=False)
                    P.tr(ps[:, (2 * cc + 1) * 128:(2 * cc + 2) * 128], qtok[:, c, 64:192], ident_f[:], [qtokB, consts],
                         [psb], sig=(cc == 1))
                P.cp("scalar", qz[k2][:, hp * 4:(hp + 1) * 4, :], ps[:, 0:512].rearrange("p (h t) -> p h t", h=4),
                     [psb], [qzB[k2]])
            ucur, ucurB = ubuf[k2], uB[k2]
            ps, psb = bank()
            for g in range(4):
                if i == 0:
                    P.mm(ps[:, g * 128:(g + 1) * 128], ucur[:, g * 128:(g + 1) * 128], bands[:, g * 3 + 2, :], True, True,
                         [ucurB, cA], [psb], sig=(g == 3))
                else:
                    uprev, uprevB = ubuf[1 - k2], uB[1 - k2]
                    P.mm(ps[:, g * 128:(g + 1) * 128], ucur[:, g * 128:(g + 1) * 128], bands[:, g * 3 + 0, :], True, False,
                         [ucurB, cA], [psb], sig=False)
                    P.mm(ps[:, g * 128:(g + 1) * 128], uprev[:, g * 128:(g + 1) * 128], bands[:, g * 3 + 1, :], False, True,
                         [uprevB, cA], [psb], sig=(g == 3))
            P.cp("scalar", pooledT[:], ps[:, 0:512], [psb], [pooledB])
            ps, psb = bank()
            for g in range(4):
                P.mm(ps[:, g * 128:(g + 1) * 128], Wpool[:, g, :], pooledT[:, g * 128:(g + 1) * 128], True, True,
                     [WpoolB, pooledB], [psb], sig=(g == 3))
            P.cp("scalar", catT[k2][:, 4:8, :], ps[:, 0:512].rearrange("p (g t) -> p g t", g=4), [psb], [catB[k2]])
        def S2(b, i):
            n = 128 * (i + 1)
            if n > KSEL:
                P.emit("vector", (lambda n=n: (lambda e: e.reduce_max(out=A_ap, in_=score[:, 0:n],
                                                                      axis=mybir.AxisListType.X,
                                                                      apply_absolute_value=True)))(),
                       [scoreB], [bisB])
                P.tt("vector", score[:, i * 128:n], score[:, i * 128:n], cb_f[:], ALU.add, [scoreB, consts], [scoreB])
                P.ts("vector", bis[:, 0:2 * NITER], pw_t[:], A_ap, None, ALU.mult, None, [bisB, cA], [bisB])
                P.memset("vector", tau, 0.0, [bisB])
                for k in range(NITER):
                    P.ts("vector", MB[:, 0:n], score[:, 0:n], tau, None, ALU.is_ge, ALU.add, [scoreB, bisB], [MBB, bisB],
                         accum=cnt)
                    P.stt("vector", dcol, cnt, float(KSEL) - 0.5, steps2[:, k:k + 1], ALU.is_ge, ALU.mult, [bisB], [bisB])
                    P.stt("vector", tau, dcol, steps[:, k:k + 1], tau, ALU.subtract, ALU.add, [bisB], [bisB])
                P.ts("vector", MB[:, 0:n], score[:, 0:n], tau, -BIG, ALU.is_lt, ALU.mult, [scoreB, bisB], [MBB])
            else:
                if i > 0:
                    P.memset("vector", MB[:, 0:i * 128], 0.0, [MBB])
                P.cp("vector", MB[:, i * 128:n], cb_b[:], [consts], [MBB])

        def S3a(b, i):
            k2 = i % 2
            psO = [banks[6], banks[7]]
            psOb = [bankb[6], bankb[7]]
            units = [(j, half) for j in range(i + 1) for half in range(2)]

            def scores(j, half):
                psS, psSb = bank()
                P.mm(psS[:, 0:512], MB[:, j * 128:(j + 1) * 128], identx4[:], True, False, [MBB, consts], [psSb],
                     sig=False)
                for hh in range(4):
                    h = half * 4 + hh
                    P.mm(psS[:, hh * 128:(hh + 1) * 128], kT[:, h // 2, j * 128:(j + 1) * 128], qz[k2][:, h, :],
                         False, True, [kTB[j], qzB[k2]], [psSb], sig=(hh == 3))
                r = PT_i["i"] % 4
                PT_i["i"] += 1
                P.act(PT[r][:], psS[:, 0:512], AF.Exp, [psSb], [PTB[r]], scale=0.125)
                return r

            def pv(j, half, r):
                for hh in range(4):
                    h = half * 4 + hh
                    P.mm(psO[half][:, hh * 128:hh * 128 + 65], PT[r][:, hh * 128:(hh + 1) * 128], V[:, j, h, :],
                         (j == 0 and hh == 0), j == i, [PTB[r], VB[j]], [psOb[half]], sig=(hh == 3))

            LOOK = 2
            rq = []
            for u in range(min(LOOK, len(units))):
                rq.append(scores(*units[u]))
            for u in range(len(units)):
                if u + LOOK < len(units):
                    rq.append(scores(*units[u + LOOK]))
                pv(units[u][0], units[u][1], rq[u])

        def S3b1(b, i):
            k2 = i % 2
            psO = [banks[6], banks[7]]
            psOb = [bankb[6], bankb[7]]
            for half in range(2):
                pv = psO[half][:, 0:512].rearrange("p (h t) -> p h t", h=4)
                P.emit("vector", (lambda pv=pv, half=half: (lambda e: e.reciprocal(out=rs[:, half * 4:(half + 1) * 4],
                                                                                   in_=pv[:, :, 64])))(),
                       [psOb[half]], [rsB])
                P.tt("vector", obf[:].rearrange("p (h d) -> p h d", d=64)[:, half * 4:(half + 1) * 4, :],
                     pv[:, :, 0:64], rs[:, half * 4:(half + 1) * 4].unsqueeze(2).broadcast_to([128, 4, 64]),
                     ALU.mult, [psOb[half], rsB], [obfB])

        def S3b1b(b, i):
            k2 = i % 2
            ps, psb = bank()
            for c in range(4):
                P.tr(ps[:, c * 128:(c + 1) * 128], obf[:, c * 128:(c + 1) * 128], ident_f[:], [obfB, consts], [psb],
                     sig=(c == 3))
            P.cp("scalar", catT[k2][:, 0:4, :], ps[:, 0:512].rearrange("p (c t) -> p c t", c=4), [psb], [catB[k2]])
            yk, ykB = ybuf[k2], yB[k2]
            for half in range(2):
                ps, psb = bank()
                for kc in range(KC):
                    P.mm(ps[:, 0:512], catT[k2][:, kc, :], Wo[:, kc, half * 512:(half + 1) * 512], kc == 0, kc == KC - 1,
                         [catB[k2], WoB[kc]], [psb], sig=(kc == KC - 1))
                P.cp("scalar", yk[:, half * 512:(half + 1) * 512], ps[:, 0:512], [psb], [ykB])

        def S3b2(b, i):
            k2 = i % 2
            k3 = (b * NT + i) % 3
            mb = b % 2
            yk, ykB = ybuf[k2], yB[k2]
            P.tt("gpsimd", yk[:], yk[:], gate1[mb][:], ALU.mult, [ykB, modB[mb]], [ykB])
            P.stt("vector", yk[:], xs[k3][:], ALPHA, yk[:], ALU.mult, ALU.add, [xsB[k3], ykB], [ykB])
            layer_norm(yk, ykB, ln1g[:], ln1b[:], cA)
            P.dma("sync", x1scr_d[(b * NT + i) * 128:(b * NT + i + 1) * 128, :], yk[:], [ykB], [x1B[b][i]])

        tiles = [(b, i) for b in range(NB) for i in range(NT)]
        S1a(*tiles[0])
        stop_at(32)
        S1b(*tiles[0])
        stop_at(33)
        S2(*tiles[0])
        S1c(*tiles[0])
        stop_at(34)
        for g in range(len(tiles)):
            nxt = tiles[g + 1] if g + 1 < len(tiles) else None
            if nxt and nxt[1] == 0:
                if g > 0:
                    S3b2(*tiles[g - 1])
                S3a(*tiles[g])
                S3b1(*tiles[g])
                S1a(*nxt)
                S1b(*nxt)
                S2(*nxt)
                S1c(*nxt)
                S3b1b(*tiles[g])
                continue
            if nxt:
                S1a(*nxt)
            if g > 0:
                S3b2(*tiles[g - 1])
            S3a(*tiles[g])
            S3b1(*tiles[g])
            if nxt:
                S1b(*nxt)
                S2(*nxt)
                S1c(*nxt)
            S3b1b(*tiles[g])
        S3b2(*tiles[-1])

        stop_at(20)
        P.barrier()
        sb.reset(mA)

        Wg = sb.alloc([128, KC, DFF], BF16, "Wg")
        Wu = sb.alloc([128, KC, DFF], BF16, "Wu")
        Wd = sb.alloc([128, FC, D], BF16, "Wd")
        ln2g = sb.alloc([128, D], F32, "ln2g")
        ln2b = sb.alloc([128, D], F32, "ln2b")
        gate2 = [sb.alloc([128, D], F32, "gate2") for _ in range(2)]
        sc2 = [sb.alloc([128, KC], F32, "sc2") for _ in range(2)]
        sh2 = [sb.alloc([128, KC], F32, "sh2") for _ in range(2)]
        x1s = [sb.alloc([128, D], F32, "x1s") for _ in range(4)]
        h2T = [sb.alloc([128, KC, 256], BF16, "h2T") for _ in range(2)]
        aT = sb.alloc([128, FC, 256], BF16, "aT")
        sgt = [sb.alloc([128, 256], F32, "sgt") for _ in range(3)]
        y2 = [sb.alloc([128, D], F32, "y2") for _ in range(2)]
        lnst2 = sb.alloc([128, 24], F32, "lnst2")
        WgB = [Buf() for _ in range(KC)]
        WuB = [Buf() for _ in range(KC)]
        WdB = [Buf() for _ in range(FC)]
        cB, modB2 = Buf(), [Buf(), Buf()]
        x1sB = [Buf() for _ in range(4)]
        h2TB = [Buf(), Buf()]
        aTB = Buf()
        sgtB = [Buf() for _ in range(3)]
        y2B = [Buf(), Buf()]
        lnB2 = Buf()
        wg_v = wg_d.rearrange("(k p) c -> p k c", p=128)
        wu_v = wu_d.rearrange("(k p) c -> p k c", p=128)
        wd_v = wd_d.rearrange("(f p) c -> p f c", p=128)
        for kc in range(KC):
            P.dma("gpsimd", Wg[:, kc, :], wg_v[:, kc, :], (), [WgB[kc]])
            P.dma("gpsimd", Wu[:, kc, :], wu_v[:, kc, :], (), [WuB[kc]])
        for fc in range(FC):
            P.dma("gpsimd", Wd[:, fc, :], wd_v[:, fc, :], (), [WdB[fc]])
        P.dma("sync", ln2g[:], ln2g_d[0:1, :].partition_broadcast(128), (), [cB])
        P.dma("sync", ln2b[:], ln2b_d[0:1, :].partition_broadcast(128), (), [cB])

        def layer_norm2(src_y, yb, gam, bet, gB):
            st = lnst2[:, 0:12]
            mv = lnst2[:, 12:14]
            veps = lnst2[:, 14:15]
            rstd = lnst2[:, 15:16]
            nmr = lnst2[:, 16:17]
            P.emit("vector", lambda e: e.bn_stats(out=lnst2[:, 0:6], in_=src_y[:, 0:512]), [yb], [lnB2])
            P.emit("vector", lambda e: e.bn_stats(out=lnst2[:, 6:12], in_=src_y[:, 512:1024]), [yb], [lnB2])
            P.emit("vector", lambda e: e.bn_aggr(out=mv, in_=st), [lnB2], [lnB2])
            P.ts("vector", veps, mv[:, 1:2], EPS, None, ALU.add, None, [lnB2], [lnB2])
            P.tt("gpsimd", rstd, veps, mhalf[:], ALU.pow, [lnB2, consts], [lnB2])
            P.stt("vector", nmr, mv[:, 0:1], -1.0, rstd, ALU.mult, ALU.mult, [lnB2], [lnB2])
            P.act(src_y[:], src_y[:], AF.Identity, [yb, lnB2], [yb], bias=nmr, scale=rstd)
            P.tt("gpsimd", src_y[:], src_y[:], gam, ALU.mult, [yb, gB], [yb])
            P.tt("gpsimd", src_y[:], src_y[:], bet, ALU.add, [yb, gB], [yb])

        x1_i = {"i": 0}
        y2_i = {"i": 0}
        sg_i = {"i": 0}
        NG = NT // 2
        groups = [(b, g) for b in range(NB) for g in range(NG)]
        xts = {}

        def prep(gi):
            b, g = groups[gi]
            hk = gi % 2
            mb = b % 2
            if g == 0:
                mrow = modscr_d[b]
                P.dma("sync", sc2[mb][:], mrow[4 * D:5 * D].rearrange("(k p) -> p k", p=128), [modscr], [modB2[mb]],
                      slow=True)
                P.dma("sync", sh2[mb][:], mrow[3 * D:4 * D].rearrange("(k p) -> p k", p=128), [modscr], [modB2[mb]],
                      slow=True)
                P.dma("sync", gate2[mb][:], modscr_d[b:b + 1, 5 * D:6 * D].partition_broadcast(128), [modscr],
                      [modB2[mb]])
                P.ts("vector", sc2[mb][:], sc2[mb][:], 1.0, None, ALU.add, None, [modB2[mb]], [modB2[mb]])
            xt = []
            for tl in range(2):
                i = 2 * g + tl
                xk = x1_i["i"] % 4
                x1_i["i"] += 1
                xt.append(xk)
                P.dma("sync", x1s[xk][:], x1scr_d[(b * NT + i) * 128:(b * NT + i + 1) * 128, :], [x1B[b][i]], [x1sB[xk]])
                for half in range(2):
                    ps, psb = bank(0, 8)
                    for cc in range(4):
                        kc = half * 4 + cc
                        P.tr(ps[:, cc * 128:(cc + 1) * 128], x1s[xk][:, kc * 128:(kc + 1) * 128], ident_f[:],
                             [x1sB[xk], consts], [psb], sig=(cc == 3))
                    for cc in range(4):
                        kc = half * 4 + cc
                        P.act(h2T[hk][:, kc, tl * 128:(tl + 1) * 128], ps[:, cc * 128:(cc + 1) * 128], AF.Identity,
                              [psb, modB2[mb]], [h2TB[hk]], bias=sh2[mb][:, kc:kc + 1], scale=sc2[mb][:, kc:kc + 1])
            xts[gi] = xt

        def gateup(gi):
            hk = gi % 2
            for fc in range(FC):
                ps, psb = bank(0, 8)
                for kc in range(KC):
                    P.mm(ps[:, 0:256], Wg[:, kc, fc * 128:(fc + 1) * 128], h2T[hk][:, kc, :], kc == 0, kc == KC - 1,
                         [WgB[kc], h2TB[hk]], [psb], sig=False)
                for kc in range(KC):
                    P.mm(ps[:, 256:512], Wu[:, kc, fc * 128:(fc + 1) * 128], h2T[hk][:, kc, :], kc == 0, kc == KC - 1,
                         [WuB[kc], h2TB[hk]], [psb], sig=(kc == KC - 1))
                sk = sg_i["i"] % 3
                sg_i["i"] += 1
                P.act(sgt[sk][:], ps[:, 0:256], AF.Silu, [psb], [sgtB[sk]])
                P.tt("vector", aT[:, fc, :], sgt[sk][:], ps[:, 256:512], ALU.mult, [sgtB[sk], psb], [aTB])

        def down(gi):
            b, g = groups[gi]
            mb = b % 2
            xt = xts[gi]
            for tl in range(2):
                i = 2 * g + tl
                yk = y2_i["i"] % 2
                y2_i["i"] += 1
                for half in range(2):
                    ps, psb = bank(0, 8)
                    for fc in range(FC):
                        P.mm(ps[:, 0:512], aT[:, fc, tl * 128:(tl + 1) * 128], Wd[:, fc, half * 512:(half + 1) * 512],
                             fc == 0, fc == FC - 1, [aTB, WdB[fc]], [psb], sig=(fc == FC - 1))
                    P.tt("vector", y2[yk][:, half * 512:(half + 1) * 512], ps[:, 0:512],
                         gate2[mb][:, half * 512:(half + 1) * 512], ALU.mult, [psb, modB2[mb]], [y2B[yk]])
                P.stt("vector", y2[yk][:], x1s[xt[tl]][:], ALPHA, y2[yk][:], ALU.mult, ALU.add,
                      [x1sB[xt[tl]], y2B[yk]], [y2B[yk]])
                layer_norm2(y2[yk], y2B[yk], ln2g[:], ln2b[:], cB)
                P.dma("sync", y_d[b, i * 128:(i + 1) * 128, :], y2[yk][:], [y2B[yk]], [Buf()])

        prep(0)
        for gi in range(len(groups)):
            gateup(gi)
            if gi + 1 < len(groups):
                prep(gi + 1)
            down(gi)

    except StopBuild:
        pass
    P.finish()
    with nc.Block() as block:
        P.replay(block)
    for c in reversed(sem_ctx):
        c.__exit__(None, None, None)
    return nc, P, sb


def host_consts(S, NITER):
    inv_freq = (10000.0 ** (-np.arange(0, 64, 2, dtype=np.float32) / np.float32(64))).astype(np.float32)
    ang = (np.arange(S, dtype=np.float32)[:, None] * inv_freq[None, :]).astype(np.float32)
    cos = np.cos(ang).astype(np.float32)
    sin = np.sin(ang).astype(np.float32)
    ident = np.eye(128, dtype=np.float32)
    t = np.arange(128)
    cb = np.where(t[None, :] <= t[:, None], 0.0, -BIG).astype(np.float32)
    bands = np.zeros((12, 128, 128), np.float32)
    tp = t[:, None]
    tq = t[None, :]
    for g, w in enumerate(POOL_WINDOWS):
        bd = ((tp >= tq - w + 1) & (tp <= tq)).astype(np.float32) / w - (tp == tq)
        bp = (tp >= tq + 129 - w).astype(np.float32) / w
        cntf = np.minimum(tq + 1, w).astype(np.float32)
        bdf = ((tp >= np.maximum(tq - w + 1, 0)) & (tp <= tq)).astype(np.float32) / cntf - (tp == tq)
        bands[g * 3 + 0] = bd
        bands[g * 3 + 1] = bp
        bands[g * 3 + 2] = bdf
    pw = np.zeros((128, 2 * NITER), np.float32)
    for k in range(NITER):
        pw[:, k] = 2.0 ** -(k + 1)
        pw[:, NITER + k] = 2.0 ** -k
    return {"k_cos": cos, "k_sin": sin, "k_nsin": -sin, "k_ident": ident, "k_cb": cb, "k_bands": bands, "k_pw": pw}


_CACHE = {}


def run(inputs, n_cores, NB, S, KSEL, NITER):
    key = (NB, S, KSEL, NITER)
    if key not in _CACHE:
        _CACHE[key] = build_program(NB, S, KSEL, NITER)[0]
    nc = _CACHE[key]
    f = lambda a: np.ascontiguousarray(np.asarray(a, dtype=np.float32))
    shared = {
        "w_mod": f(inputs["w_mod"][0]), "b_mod": f(inputs["b_mod"][0]).reshape(1, -1), "w_in": f(inputs["w_in"][0]),
        "w_pool": f(inputs["w_pool"][0]), "pool_scale": f(inputs["pool_scale"][0]), "w_o": f(inputs["w_o"][0]),
        "ln1_g": f(inputs["ln1_g"][0]).reshape(1, -1), "ln1_b": f(inputs["ln1_b"][0]).reshape(1, -1),
        "w_gate": f(inputs["w_gate"][0]), "w_up": f(inputs["w_up"][0]), "w_down": f(inputs["w_down"][0]),
        "ln2_g": f(inputs["ln2_g"][0]).reshape(1, -1), "ln2_b": f(inputs["ln2_b"][0]).reshape(1, -1),
    }
    shared.update(host_consts(S, NITER))
    x = f(inputs["x"])
    c = f(inputs["c"])
    in_maps = []
    for k in range(n_cores):
        m = dict(shared)
        m["x"] = np.ascontiguousarray(x[k * NB:(k + 1) * NB])
        m["c"] = np.ascontiguousarray(c[k * NB:(k + 1) * NB])
        in_maps.append(m)
    res = run_bass_kernel_spmd(nc, in_maps, core_ids=list(range(n_cores)))
    return np.concatenate([np.asarray(r["y"]) for r in res.results], axis=0).astype(np.float32)


def kernel(x, c, w_mod, b_mod, w_in, w_pool, pool_scale, w_o, ln1_g, ln1_b, w_gate, w_up, w_down, ln2_g, ln2_b):
    inputs = dict(x=x, c=c, w_mod=w_mod, b_mod=b_mod, w_in=w_in, w_pool=w_pool, pool_scale=pool_scale, w_o=w_o,
                  ln1_g=ln1_g, ln1_b=ln1_b, w_gate=w_gate, w_up=w_up, w_down=w_down, ln2_g=ln2_g, ln2_b=ln2_b)
    return run(inputs, N_CORES, 4, 2048, 256, 10)
```

```python
import numpy as np
import concourse.bass as bass
import concourse.mybir as mybir
from concourse.bass_utils import run_bass_kernel_spmd

F32 = mybir.dt.float32
BF16 = mybir.dt.bfloat16
U8 = mybir.dt.uint8
AF = mybir.ActivationFunctionType
ALU = mybir.AluOpType

D = 1024
KC = 8
DFF = 2816
FC = 22
WIN = 2632
NMOD = 6
ALPHA = float(2.0 ** 0.25)
EPS = 1e-5
BIG = 32768.0
N_CORES = 8
POOL_WINDOWS = (2, 4, 8, 16)
SAME_WAIT = True
NO_SELF_WAIT = ("tensor",)
STOP = 99
DEBUG = 0


class StopBuild(Exception):
    pass


def stop_at(k):
    if STOP == k:
        raise StopBuild()


class Tok:
    __slots__ = ("sem", "val")

    def __init__(self, sem, val):
        self.sem = sem
        self.val = val


class Sem:
    def __init__(self, h):
        self.h = h
        self.count = 0
        self.last = None


class Buf:
    __slots__ = ("name", "w", "r")

    def __init__(self, name=""):
        self.name = name
        self.w = {}
        self.r = {}


class Eng:
    def __init__(self, name, sem):
        self.name = name
        self.sem = sem
        self.ops = []
        self.seen = {}
        self.pending = []
        self.dma = []
        self.dma_i = 0


class Prog:
    def __init__(self, nc, sems):
        self.nc = nc
        it = iter(sems)
        self.eng = {}
        for n in ("tensor", "vector", "scalar", "gpsimd", "sync"):
            self.eng[n] = Eng(n, Sem(next(it)))
        self.eng["sync"].dma = [Sem(next(it)) for _ in range(40)]
        self.eng["gpsimd"].dma = [Sem(next(it)) for _ in range(16)]
        self.n_ins = 0

    def emit(self, eng, fn, reads=(), writes=(), sig=True, dma=False):
        E = self.eng[eng]
        need = {}

        def add(tok):
            if tok.sem is E.sem:
                if eng in NO_SELF_WAIT or not SAME_WAIT:
                    return
            assert tok.val is not None, f"unresolved token needed by {eng}"
            cur = need.get(tok.sem)
            if cur is None or tok.val > cur:
                need[tok.sem] = tok.val

        for b in reads:
            for t in b.w.values():
                add(t)
        for b in writes:
            for t in b.w.values():
                add(t)
            for t in b.r.values():
                add(t)
        if dma:
            s = E.dma[E.dma_i % len(E.dma)]
            E.dma_i += 1
            if s.last is not None:
                add(s.last)
        for sem, val in need.items():
            if E.seen.get(sem, 0) < val:
                E.ops.append(("wait", sem.h, val))
                E.seen[sem] = val
        if dma:
            s.count += 16
            tok = Tok(s, s.count)
            s.last = tok
            E.ops.append(("ins", fn, s.h, 16))
        elif sig:
            E.sem.count += 1
            tok = Tok(E.sem, E.sem.count)
            for p in E.pending:
                p.val = tok.val
            E.pending = []
            E.ops.append(("ins", fn, E.sem.h, 1))
        else:
            tok = Tok(E.sem, None)
            E.pending.append(tok)
            E.ops.append(("ins", fn, None, 0))
        for b in writes:
            b.w = {tok.sem: tok}
            b.r = {}
        for b in reads:
            if b not in writes:
                b.r[tok.sem] = tok
        self.n_ins += 1
        return tok

    def barrier(self):
        toks = []
        for E in self.eng.values():
            assert not E.pending
            if E.sem.count:
                toks.append((E.sem, E.sem.count))
            for s in E.dma:
                if s.last is not None:
                    toks.append((s, s.last.val))
        for E in self.eng.values():
            for sem, val in toks:
                if sem is E.sem:
                    continue
                if E.seen.get(sem, 0) < val:
                    E.ops.append(("wait", sem.h, val))
                    E.seen[sem] = val

    def finish(self):
        E = self.eng["sync"]
        for En in self.eng.values():
            for s in En.dma:
                if s.last is not None and E.seen.get(s, 0) < s.last.val:
                    E.ops.append(("wait", s.h, s.last.val))
                    E.seen[s] = s.last.val

    def replay(self, block):
        def mk(name):
            E = self.eng[name]

            def body(e):
                for op in E.ops:
                    if op[0] == "wait":
                        e.wait_ge(op[1], op[2])
                    else:
                        ins = op[1](e)
                        if op[2] is not None:
                            ins.then_inc(op[2], op[3])
            return body
        block.sync(mk("sync"))
        block.tensor(mk("tensor"))
        block.vector(mk("vector"))
        block.scalar(mk("scalar"))
        block.gpsimd(mk("gpsimd"))

    def mm(self, out, lhsT, rhs, start, stop, reads, writes, sig=True):
        return self.emit("tensor", lambda e: e.matmul(out, lhsT=lhsT, rhs=rhs, start=start, stop=stop),
                         reads, writes, sig=sig)

    def tr(self, out, in_, ident, reads, writes, sig=True):
        return self.emit("tensor", lambda e: e.transpose(out=out, in_=in_, identity=ident), reads, writes, sig=sig)

    def act(self, out, in_, func, reads, writes, bias=None, scale=None):
        kw = {}
        if bias is not None:
            kw["bias"] = bias
        if scale is not None:
            kw["scale"] = scale
        return self.emit("scalar", lambda e: e.activation(out=out, in_=in_, func=func, **kw), reads, writes)

    def ts(self, eng, out, in0, s1, s2, op0, op1, reads, writes, accum=None):
        kw = {}
        if op1 is not None:
            kw["op1"] = op1
        if accum is not None:
            kw["accum_out"] = accum
        return self.emit(eng, lambda e: e.tensor_scalar(out=out, in0=in0, scalar1=s1, scalar2=s2, op0=op0, **kw),
                         reads, writes)

    def tt(self, eng, out, in0, in1, op, reads, writes):
        return self.emit(eng, lambda e: e.tensor_tensor(out=out, in0=in0, in1=in1, op=op), reads, writes)

    def stt(self, eng, out, in0, scalar, in1, op0, op1, reads, writes):
        return self.emit(eng, lambda e: e.scalar_tensor_tensor(out=out, in0=in0, scalar=scalar, in1=in1,
                                                               op0=op0, op1=op1), reads, writes)

    def cp(self, eng, out, in_, reads, writes):
        if eng == "scalar":
            return self.emit(eng, lambda e: e.copy(out=out, in_=in_), reads, writes)
        return self.emit(eng, lambda e: e.tensor_copy(out=out, in_=in_), reads, writes)

    def memset(self, eng, ap, val, writes):
        return self.emit(eng, lambda e: e.memset(ap, val), (), writes)

    def dma(self, eng, out, in_, reads, writes, slow=False):
        if slow:
            return self.emit(eng, lambda e: e.dma_start(out=out, in_=in_, allow_slow_non_contiguous=True),
                             reads, writes, dma=True)
        return self.emit(eng, lambda e: e.dma_start(out=out, in_=in_), reads, writes, dma=True)


class SB:
    def __init__(self, nc, base, size):
        self.nc = nc
        self.base = base
        self.cur = base
        self.end = base + size
        self.k = 0
        self.peak = base

    def alloc(self, shape, dtype, name="t"):
        isz = 4 if dtype == F32 else 2
        n = 1
        for s in shape[1:]:
            n *= s
        off = (self.cur + 63) // 64 * 64
        self.k += 1
        h = self.nc.alloc_sbuf_tensor_at(f"{name}{self.k}", list(shape), dtype, offset=off)
        self.cur = off + n * isz
        assert self.cur <= self.end, f"SBUF overflow allocating {name}: {self.cur - self.base} > {self.end - self.base}"
        self.peak = max(self.peak, self.cur)
        return h

    def mark(self):
        return self.cur

    def reset(self, m):
        self.cur = m


class Ring:
    def __init__(self, items):
        self.items = items
        self.i = 0

    def next(self):
        it = self.items[self.i % len(self.items)]
        self.i += 1
        return it


def build_program(NB, S, KSEL, NITER):
    NT = S // 128
    nc = bass.Bass("TRN2", target_bir_lowering=False)

    def din(name, shape):
        return nc.dram_tensor(name, list(shape), F32, kind="ExternalInput").ap()

    x_d = din("x", [NB, S, D])
    c_d = din("c", [NB, D])
    wmod_d = din("w_mod", [D, NMOD * D])
    bmod_d = din("b_mod", [1, NMOD * D])
    win_d = din("w_in", [D, WIN])
    wpool_d = din("w_pool", [4, 128, 128])
    pscale_d = din("pool_scale", [512])
    wo_d = din("w_o", [D, D])
    ln1g_d = din("ln1_g", [1, D])
    ln1b_d = din("ln1_b", [1, D])
    wg_d = din("w_gate", [D, DFF])
    wu_d = din("w_up", [D, DFF])
    wd_d = din("w_down", [DFF, D])
    ln2g_d = din("ln2_g", [1, D])
    ln2b_d = din("ln2_b", [1, D])
    cos_d = din("k_cos", [S, 32])
    sin_d = din("k_sin", [S, 32])
    nsin_d = din("k_nsin", [S, 32])
    ident_d = din("k_ident", [128, 128])
    cb_d = din("k_cb", [128, 128])
    bands_d = din("k_bands", [12, 128, 128])
    pw_d = din("k_pw", [128, 2 * NITER])
    y_d = nc.dram_tensor("y", [NB, S, D], F32, kind="ExternalOutput").ap()
    if DEBUG:
        dbg_score = nc.dram_tensor("dbg_score", [128, S], F32, kind="ExternalOutput").ap()
        dbg_mb = nc.dram_tensor("dbg_mb", [128, S], F32, kind="ExternalOutput").ap()
        dbg_bis = nc.dram_tensor("dbg_bis", [128, 2 * NITER + 8], F32, kind="ExternalOutput").ap()
        dbg_w = nc.dram_tensor("dbg_w", [128, 8], F32, kind="ExternalOutput").ap()
    modscr_d = nc.dram_tensor("mod_scr", [NB, NMOD * D], F32).ap()
    x1scr_d = nc.dram_tensor("x1_scr", [NB * S, D], F32).ap()

    ARENA = 207 * 1024
    arena = nc.alloc_sbuf_tensor("arena", [128, ARENA], U8)
    base = nc.lookup_mloc(arena).addr
    sb = SB(nc, base, ARENA)
    banks = [nc.alloc_psum_tensor(f"bank{i}", [128, 512], F32) for i in range(8)]
    bankb = [Buf(f"bank{i}") for i in range(8)]

    sem_ctx = [nc.semaphore(f"s{i}") for i in range(5 + 40 + 16)]
    sems = [c.__enter__() for c in sem_ctx]
    P = Prog(nc, sems)

    try:
        ident_f = sb.alloc([128, 128], F32, "identf")
        ident_b = sb.alloc([128, 128], BF16, "identb")
        identx4 = sb.alloc([128, 512], BF16, "identx4")
        cb_f = sb.alloc([128, 128], F32, "cbf")
        cb_b = sb.alloc([128, 128], BF16, "cbb")
        mhalf = sb.alloc([128, 1], F32, "mhalf")
        consts = Buf("consts")
        P.dma("sync", ident_f[:], ident_d, (), [consts])
        P.dma("sync", cb_f[:], cb_d, (), [consts])
        P.cp("vector", ident_b[:], ident_f[:], [consts], [consts])
        for r in range(4):
            P.cp("vector", identx4[:, r * 128:(r + 1) * 128], ident_f[:], [consts], [consts])
        P.cp("vector", cb_b[:], cb_f[:], [consts], [consts])
        P.memset("vector", mhalf[:], -0.5, [consts])

        stop_at(0)
        rr = {"i": 0}

        def bank(lo=0, hi=5):
            k = lo + rr["i"] % (hi - lo)
            rr["i"] += 1
            return banks[k], bankb[k]

        m0 = sb.mark()
        cT = sb.alloc([128, KC, NB], F32, "cT")
        siluT = sb.alloc([128, KC, NB], BF16, "siluT")
        bmod_b = sb.alloc([128, NMOD * D], F32, "bmodb")
        modrow = sb.alloc([128, NMOD * D], F32, "modrow")
        wm = [sb.alloc([128, KC, 512], BF16, "wm") for _ in range(2)]
        cTb, bmb, mrb = Buf(), Buf(), Buf()
        wmb = [[Buf() for _ in range(KC)] for _ in range(2)]
        for b in range(NB):
            P.dma("sync", cT[:, :, b], c_d[b].rearrange("(k p) -> p k", p=128), (), [cTb], slow=True)
        P.dma("sync", bmod_b[0:NB, :], bmod_d[0:1, :].partition_broadcast(NB), (), [bmb])
        P.act(siluT[:], cT[:], AF.Silu, [cTb], [cTb])
        wmod_v = wmod_d.rearrange("(k p) c -> p k c", p=128)
        for blk in range(NMOD * 2):
            w = wm[blk % 2]
            wb = wmb[blk % 2]
            for kc in range(KC):
                P.dma("gpsimd", w[:, kc, :], wmod_v[:, kc, blk * 512:(blk + 1) * 512], (), [wb[kc]])
            ps, psb = bank()
            for kc in range(KC):
                P.mm(ps[0:NB, :], siluT[:, kc, :], w[:, kc, :], kc == 0, kc == KC - 1, [cTb, wb[kc]], [psb], sig=(kc == KC - 1))
            P.tt("vector", modrow[0:NB, blk * 512:(blk + 1) * 512], ps[0:NB, :], bmod_b[0:NB, blk * 512:(blk + 1) * 512],
                 ALU.add, [psb, bmb], [mrb])
        stop_at(1)
        modscr = Buf("modscr")
        P.dma("sync", modscr_d, modrow[0:NB, :], [mrb], [modscr])
        stop_at(2)
        P.barrier()
        sb.reset(m0)

        mA = sb.mark()
        Win = sb.alloc([128, KC, WIN], BF16, "Win")
        Wo = sb.alloc([128, KC, D], BF16, "Wo")
        Wpool = sb.alloc([128, 4, 128], BF16, "Wpool")
        bands = sb.alloc([128, 12, 128], BF16, "bands")
        cos_t = sb.alloc([128, NT, 32], F32, "cos")
        sin_t = sb.alloc([128, NT, 32], F32, "sin")
        nsin_t = sb.alloc([128, NT, 32], F32, "nsin")
        pw_t = sb.alloc([128, 2 * NITER], F32, "pw")
        ln1g = sb.alloc([128, D], F32, "ln1g")
        ln1b = sb.alloc([128, D], F32, "ln1b")
        gate1 = [sb.alloc([128, D], F32, "gate1") for _ in range(2)]
        sc1 = [sb.alloc([128, KC], F32, "sc1") for _ in range(2)]
        sh1 = [sb.alloc([128, KC], F32, "sh1") for _ in range(2)]
        pe = [sb.alloc([128, 512], F32, "pe") for _ in range(3)]
        kT = sb.alloc([128, 4, S], BF16, "kT")
        ikzA = sb.alloc([128, S], BF16, "ikzA")
        ikzB = sb.alloc([128, S], BF16, "ikzB")
        V = sb.alloc([128, NT, 8, 65], BF16, "V")
        xs = [sb.alloc([128, D], F32, "xs") for _ in range(3)]
        hT = [sb.alloc([128, KC, 128], BF16, "hT") for _ in range(2)]
        qtok = sb.alloc([128, 4, 192], F32, "qtok")
        ktok = sb.alloc([128, 512], F32, "ktok")
        iqtok = sb.alloc([128, 512], F32, "iqtok")
        iktok = sb.alloc([128, 192], F32, "iktok")
        rt1 = [sb.alloc([128, 512], F32, "rt1") for _ in range(1)]
        rt2 = [sb.alloc([128, 512], F32, "rt2") for _ in range(1)]
        qz = [sb.alloc([128, 8, 128], BF16, "qz") for _ in range(2)]
        iqT = [sb.alloc([128, 4, 128], BF16, "iqT") for _ in range(2)]
        w_t = [sb.alloc([128, 8], F32, "wt") for _ in range(2)]
        ubuf = [sb.alloc([128, 512], BF16, "u") for _ in range(2)]
        pooledT = sb.alloc([128, 512], BF16, "pooledT")
        catT = [sb.alloc([128, KC, 128], BF16, "catT") for _ in range(2)]
        obf = sb.alloc([128, 512], F32, "obf")
        rs = sb.alloc([128, 8], F32, "rs")
        score = sb.alloc([128, S], F32, "score")
        MB = sb.alloc([128, S], BF16, "MB")
        Rr = [sb.alloc([128, 512], BF16, "R") for _ in range(4)]
        dg = sb.alloc([128, 8, 128], BF16, "dg")
        PT = [sb.alloc([128, 512], BF16, "PT") for _ in range(4)]
        bis = sb.alloc([128, 2 * NITER + 8], F32, "bis")
        ybuf = [sb.alloc([128, D], F32, "y") for _ in range(2)]
        lnst = sb.alloc([128, 24], F32, "lnst")

        WinB = [Buf() for _ in range(KC)]
        WoB = [Buf() for _ in range(KC)]
        WpoolB, cA = Buf("Wpool"), Buf("constsA")
        win_v = win_d.rearrange("(k p) c -> p k c", p=128)
        wo_v = wo_d.rearrange("(k p) c -> p k c", p=128)
        for kc in range(KC):
            P.dma("gpsimd", Win[:, kc, :], win_v[:, kc, :], (), [WinB[kc]])
        for kc in range(KC):
            P.dma("gpsimd", Wo[:, kc, :], wo_v[:, kc, :], (), [WoB[kc]])
        P.dma("gpsimd", Wpool[:], wpool_d.rearrange("g c d -> c g d"), (), [WpoolB])
        P.dma("gpsimd", bands[:], bands_d.rearrange("n a b -> a n b"), (), [cA])
        P.dma("sync", cos_t[:], cos_d.rearrange("(i p) d -> p i d", p=128), (), [cA])
        P.dma("sync", sin_t[:], sin_d.rearrange("(i p) d -> p i d", p=128), (), [cA])
        P.dma("sync", nsin_t[:], nsin_d.rearrange("(i p) d -> p i d", p=128), (), [cA])
        P.dma("sync", pw_t[:], pw_d, (), [cA])
        P.dma("sync", ln1g[:], ln1g_d[0:1, :].partition_broadcast(128), (), [cA])
        P.dma("sync", ln1b[:], ln1b_d[0:1, :].partition_broadcast(128), (), [cA])
        qtokB, ktokB, iqtokB, iktokB = Buf("qtok"), Buf("ktok"), Buf("iqtok"), Buf("iktok")
        P.memset("vector", qtok[:], 0.0, [qtokB])
        P.memset("vector", iktok[:], 0.0, [iktokB])
        VB = [Buf(f"V{i}") for i in range(NT)]
        kTB = [Buf(f"kT{i}") for i in range(NT)]
        ikzBf = [Buf(f"ikz{i}") for i in range(NT)]
        for i in range(NT):
            P.memset("gpsimd", V[:, i, :, 64:65], 1.0, [VB[i]])

        stop_at(3)
        pse, pseB = rt1[0], Buf()
        P.dma("sync", pse[:], pscale_d.rearrange("(o n) -> o n", o=1).partition_broadcast(128), (), [pseB])
        P.tt("vector", Wpool[:].rearrange("c g d -> c (g d)"), Wpool[:].rearrange("c g d -> c (g d)"), pse[:], ALU.mult,
             [pseB, WpoolB], [WpoolB])
        stop_at(31)
        xsB = [Buf("xs0"), Buf("xs1"), Buf("xs2")]
        hTB = [[Buf() for _ in range(KC)] for _ in range(2)]
        rt1B = [Buf()]
        rt2B = [Buf()]
        rt1B[0] = pseB
        peB = [Buf() for _ in range(3)]
        rt_i = {"i": 0}
        pe_i = {"i": 0}
        qzB, iqTB, wtB = [Buf(), Buf()], [Buf(), Buf()], [Buf(), Buf()]
        uB = [Buf(), Buf()]
        pooledB = Buf()
        catB = [Buf(), Buf()]
        obfB, rsB, scoreB, MBB, dgB, bisB, lnB = Buf(), Buf(), Buf(), Buf(), Buf(), Buf(), Buf()
        RB = [Buf() for _ in range(4)]
        PTB = [Buf() for _ in range(4)]
        R_i = {"i": 0}
        PT_i = {"i": 0}
        yB = [Buf(), Buf()]
        modB = [Buf("modA0"), Buf("modA1")]
        x1B = [[Buf() for _ in range(NT)] for _ in range(NB)]

        steps = bis[:, 0:NITER]
        steps2 = bis[:, NITER:2 * NITER]
        A_ap = bis[:, 2 * NITER:2 * NITER + 1]
        tau = bis[:, 2 * NITER + 1:2 * NITER + 2]
        cnt = bis[:, 2 * NITER + 2:2 * NITER + 3]
        dcol = bis[:, 2 * NITER + 3:2 * NITER + 4]

        def rope(ps, psb, H, i, dests):
            pk = pe_i["i"] % 3
            pe_i["i"] += 1
            P.cp("vector", pe[pk][:, 0:H * 64], ps, [psb], [peB[pk]])
            k = 0
            t1, t2 = rt1[k], rt2[k]
            X = pe[pk][:, 0:H * 64].rearrange("p (h two d) -> p h two d", two=2, d=32)
            t1v = t1[:, 0:H * 64].rearrange("p (h two d) -> p h two d", two=2, d=32)
            t2v = t2[:, 0:H * 64].rearrange("p (h two d) -> p h two d", two=2, d=32)
            cosb = cos_t[:, i, :].unsqueeze(1).unsqueeze(1).broadcast_to([128, H, 2, 32])
            sinb = sin_t[:, i, :].unsqueeze(1).broadcast_to([128, H, 32])
            nsinb = nsin_t[:, i, :].unsqueeze(1).broadcast_to([128, H, 32])
            P.tt("gpsimd", t1v, X, cosb, ALU.mult, [peB[pk], cA], [rt1B[k]])
            P.tt("gpsimd", t2v[:, :, 0, :], X[:, :, 1, :], nsinb, ALU.mult, [peB[pk], cA], [rt2B[k]])
            P.tt("gpsimd", t2v[:, :, 1, :], X[:, :, 0, :], sinb, ALU.mult, [peB[pk], cA], [rt2B[k]])
            for sl, dest, destB in dests:
                P.tt("gpsimd", dest, sl(t1), sl(t2), ALU.add, [rt1B[k], rt2B[k]], [destB])

        def layer_norm(src_y, yb, gam, bet, gB):
            st = lnst[:, 0:12]
            mv = lnst[:, 12:14]
            veps = lnst[:, 14:15]
            rstd = lnst[:, 15:16]
            nmr = lnst[:, 16:17]
            P.emit("vector", lambda e: e.bn_stats(out=lnst[:, 0:6], in_=src_y[:, 0:512]), [yb], [lnB])
            P.emit("vector", lambda e: e.bn_stats(out=lnst[:, 6:12], in_=src_y[:, 512:1024]), [yb], [lnB])
            P.emit("vector", lambda e: e.bn_aggr(out=mv, in_=st), [lnB], [lnB])
            P.ts("vector", veps, mv[:, 1:2], EPS, None, ALU.add, None, [lnB], [lnB])
            P.tt("gpsimd", rstd, veps, mhalf[:], ALU.pow, [lnB, consts], [lnB])
            P.stt("vector", nmr, mv[:, 0:1], -1.0, rstd, ALU.mult, ALU.mult, [lnB], [lnB])
            P.act(src_y[:], src_y[:], AF.Identity, [yb, lnB], [yb], bias=nmr, scale=rstd)
            P.tt("gpsimd", src_y[:], src_y[:], gam, ALU.mult, [yb, gB], [yb])
            P.tt("gpsimd", src_y[:], src_y[:], bet, ALU.add, [yb, gB], [yb])

        full = lambda H: (lambda t: t[:, 0:H * 64].rearrange("p (h two d) -> p h two d", two=2, d=32))

        def S1a(b, i):
            k2 = i % 2
            k3 = (b * NT + i) % 3
            mb = b % 2
            if i == 0:
                mrow = modscr_d[b]
                P.dma("sync", sc1[mb][:], mrow[1 * D:2 * D].rearrange("(k p) -> p k", p=128), [modscr], [modB[mb]],
                      slow=True)
                P.dma("sync", sh1[mb][:], mrow[0 * D:1 * D].rearrange("(k p) -> p k", p=128), [modscr], [modB[mb]],
                      slow=True)
                P.dma("sync", gate1[mb][:], modscr_d[b:b + 1, 2 * D:3 * D].partition_broadcast(128), [modscr],
                      [modB[mb]])
                P.ts("vector", sc1[mb][:], sc1[mb][:], 1.0, None, ALU.add, None, [modB[mb]], [modB[mb]])
            P.dma("sync", xs[k3][:], x_d[b, i * 128:(i + 1) * 128, :], (), [xsB[k3]])
            for half in range(2):
                ps, psb = bank()
                for cc in range(4):
                    kc = half * 4 + cc
                    P.tr(ps[:, cc * 128:(cc + 1) * 128], xs[k3][:, kc * 128:(kc + 1) * 128], ident_f[:],
                         [xsB[k3], consts], [psb], sig=(cc == 3))
                for cc in range(4):
                    kc = half * 4 + cc
                    P.act(hT[k2][:, kc, :], ps[:, cc * 128:(cc + 1) * 128], AF.Identity, [psb, modB[mb]], [hTB[k2][kc]],
                          bias=sh1[mb][:, kc:kc + 1], scale=sc1[mb][:, kc:kc + 1])

            def proj(c0, c1):
                ps, psb = bank()
                for kc in range(KC):
                    P.mm(ps[:, 0:c1 - c0], hT[k2][:, kc, :], Win[:, kc, c0:c1], kc == 0, kc == KC - 1,
                         [hTB[k2][kc], WinB[kc]], [psb], sig=(kc == KC - 1))
                return ps, psb

            ps, psb = proj(512, 1024)
            rope(ps[:, 0:512], psb, 8, i, [(full(8), ktok[:].rearrange("p (h two d) -> p h two d", two=2, d=32), ktokB)])
            ps, psb = proj(0, 512)
            qd = []
            for par in range(2):
                qd.append(((lambda par: (lambda t: t[:].rearrange("p (c b d) -> p c b d", c=4, b=2)[:, :, par, :]))(par),
                           qtok[:, :, par * 128:par * 128 + 64], qtokB))
            rope(ps[:, 0:512], psb, 8, i, qd)
            ps, psb = proj(2048, 2560)
            rope(ps[:, 0:512], psb, 8, i, [(full(8), iqtok[:].rearrange("p (h two d) -> p h two d", two=2, d=32), iqtokB)])
            ps, psb = proj(2560, 2632)
            P.cp("vector", w_t[k2][:], ps[:, 64:72], [psb], [wtB[k2]])
            if 128 * (i + 1) > KSEL:
                for h in range(8):
                    P.ts("vector", dg[:, h, :], ident_b[:], w_t[k2][:, h:h + 1], None, ALU.mult, None,
                         [consts, wtB[k2]], [dgB])
            rope(ps[:, 0:64], psb, 1, i,
                 [(full(1), iktok[:, 64:128].rearrange("p (h two d) -> p h two d", two=2, d=32), iktokB)])
            ps, psb = proj(1024, 1536)
            P.cp("vector", V[:, i, :, 0:64], ps[:, 0:512].rearrange("p (h d) -> p h d", d=64), [psb], [VB[i]])
            ps, psb = proj(1536, 2048)
            P.cp("vector", ubuf[k2][:], ps[:, 0:512], [psb], [uB[k2]])

        def S1b(b, i):
            k2 = i % 2
            n = 128 * (i + 1)
            ps, psb = bank()
            for c in range(4):
                P.tr(ps[:, c * 128:(c + 1) * 128], iqtok[:, c * 128:(c + 1) * 128], ident_f[:], [iqtokB, consts], [psb],
                     sig=(c == 3))
            P.cp("vector", iqT[k2][:], ps[:, 0:512].rearrange("p (c t) -> p c t", c=4), [psb], [iqTB[k2]])
            ps, psb = bank()
            P.tr(ps[:, 0:128], iktok[:, 64:192], ident_f[:], [iktokB, consts], [psb], sig=False)
            P.tr(ps[:, 128:256], iktok[:, 0:128], ident_f[:], [iktokB, consts], [psb], sig=True)
            P.cp("vector", ikzA[:, i * 128:(i + 1) * 128], ps[:, 0:128], [psb], [ikzBf[i]])
            P.cp("vector", ikzB[:, i * 128:(i + 1) * 128], ps[:, 128:256], [psb], [ikzBf[i]])
            if n > KSEL:
                for blk in range((n + 511) // 512):
                    c0 = blk * 512
                    c1 = min(n, c0 + 512)
                    wd = c1 - c0
                    jt = list(range(c0 // 128, c1 // 128))
                    psS, psSb = banks[5], bankb[5]
                    def logit(h):
                        ikz = ikzA if h % 2 == 0 else ikzB
                        psL, psLb = bank()
                        P.mm(psL[:, 0:wd], iqT[k2][:, h // 2, :], ikz[:, c0:c1], True, True,
                             [iqTB[k2]] + [ikzBf[j] for j in jt], [psLb])
                        r = R_i["i"] % 4
                        R_i["i"] += 1
                        P.act(Rr[r][:, 0:wd], psL[:, 0:wd], AF.Relu, [psLb], [RB[r]])
                        return r

                    rl = [logit(0), logit(1)]
                    for h in range(8):
                        if h + 2 < 8:
                            rl.append(logit(h + 2))
                        r = rl[h]
                        P.mm(psS[:, 0:wd], dg[:, h, :], Rr[r][:, 0:wd], h == 0, h == 7, [dgB, RB[r]], [psSb],
                             sig=(h == 7))
                    P.cp("scalar", score[:, c0:c1], psS[:, 0:wd], [psSb], [scoreB])


        def S1c(b, i):
            k2 = i % 2
            n = 128 * (i + 1)
            ps, psb = bank()
            for c in range(4):
                P.tr(ps[:, c * 128:(c + 1) * 128], ktok[:, c * 128:(c + 1) * 128], ident_f[:], [ktokB, consts], [psb],
                     sig=(c == 3))
            P.cp("scalar", kT[:, :, i * 128:(i + 1) * 128], ps[:, 0:512].rearrange("p (c t) -> p c t", c=4),
                 [psb], [kTB[i]])
            for hp in range(2):
                ps, psb = bank()
                for cc in range(2):
                    c = hp * 2 + cc
                    P.tr(ps[:, (2 * cc) * 128:(2 * cc + 1) * 128], qtok[:, c, 0:128], ident_f[:], [qtokB, consts], [psb],
                         sig=False)
                    P.tr(ps[:, (2 * cc + 1) * 128:(2 * cc + 2) * 128], qtok[:, c, 64:192], ident_f[:], [qtokB, consts],
                         [psb], sig=(cc == 1))
                P.cp("scalar", qz[k2][:, hp * 4:(hp + 1) * 4, :], ps[:, 0:512].rearrange("p (h t) -> p h t", h=4),
                     [psb], [qzB[k2]])
            ucur, ucurB = ubuf[k2], uB[k2]
            ps, psb = bank()
            for g in range(4):
                if i == 0:
                    P.mm(ps[:, g * 128:(g + 1) * 128], ucur[:, g * 128:(g + 1) * 128], bands[:, g * 3 + 2, :], True, True,
                         [ucurB, cA], [psb], sig=(g == 3))
                else:
                    uprev, uprevB = ubuf[1 - k2], uB[1 - k2]
                    P.mm(ps[:, g * 128:(g + 1) * 128], ucur[:, g * 128:(g + 1) * 128], bands[:, g * 3 + 0, :], True, False,
                         [ucurB, cA], [psb], sig=False)
                    P.mm(ps[:, g * 128:(g + 1) * 128], uprev[:, g * 128:(g + 1) * 128], bands[:, g * 3 + 1, :], False, True,
                         [uprevB, cA], [psb], sig=(g == 3))
            P.cp("scalar", pooledT[:], ps[:, 0:512], [psb], [pooledB])
            ps, psb = bank()
            for g in range(4):
                P.mm(ps[:, g * 128:(g + 1) * 128], Wpool[:, g, :], pooledT[:, g * 128:(g + 1) * 128], True, True,
                     [WpoolB, pooledB], [psb], sig=(g == 3))
            P.cp("scalar", catT[k2][:, 4:8, :], ps[:, 0:512].rearrange("p (g t) -> p g t", g=4), [psb], [catB[k2]])
        def S2(b, i):
            n = 128 * (i + 1)
            if n > KSEL:
                P.emit("vector", (lambda n=n: (lambda e: e.reduce_max(out=A_ap, in_=score[:, 0:n],
                                                                      axis=mybir.AxisListType.X,
                                                                      apply_absolute_value=True)))(),
                       [scoreB], [bisB])
                P.tt("vector", score[:, i * 128:n], score[:, i * 128:n], cb_f[:], ALU.add, [scoreB, consts], [scoreB])
                P.ts("vector", bis[:, 0:2 * NITER], pw_t[:], A_ap, None, ALU.mult, None, [bisB, cA], [bisB])
                P.memset("vector", tau, 0.0, [bisB])
                for k in range(NITER):
                    P.ts("vector", MB[:, 0:n], score[:, 0:n], tau, None, ALU.is_ge, ALU.add, [scoreB, bisB], [MBB, bisB],
                         accum=cnt)
                    P.stt("vector", dcol, cnt, float(KSEL) - 0.5, steps2[:, k:k + 1], ALU.is_ge, ALU.mult, [bisB], [bisB])
                    P.stt("vector", tau, dcol, steps[:, k:k + 1], tau, ALU.subtract, ALU.add, [bisB], [bisB])
                P.ts("vector", MB[:, 0:n], score[:, 0:n], tau, -BIG, ALU.is_lt, ALU.mult, [scoreB, bisB], [MBB])
            else:
                if i > 0:
                    P.memset("vector", MB[:, 0:i * 128], 0.0, [MBB])
                P.cp("vector", MB[:, i * 128:n], cb_b[:], [consts], [MBB])

        def S3a(b, i):
            k2 = i % 2
            psO = [banks[6], banks[7]]
            psOb = [bankb[6], bankb[7]]
            units = [(j, half) for j in range(i + 1) for half in range(2)]

            def scores(j, half):
                psS, psSb = bank()
                P.mm(psS[:, 0:512], MB[:, j * 128:(j + 1) * 128], identx4[:], True, False, [MBB, consts], [psSb],
                     sig=False)
                for hh in range(4):
                    h = half * 4 + hh
                    P.mm(psS[:, hh * 128:(hh + 1) * 128], kT[:, h // 2, j * 128:(j + 1) * 128], qz[k2][:, h, :],
                         False, True, [kTB[j], qzB[k2]], [psSb], sig=(hh == 3))
                r = PT_i["i"] % 4
                PT_i["i"] += 1
                P.act(PT[r][:], psS[:, 0:512], AF.Exp, [psSb], [PTB[r]], scale=0.125)
                return r

            def pv(j, half, r):
                for hh in range(4):
                    h = half * 4 + hh
                    P.mm(psO[half][:, hh * 128:hh * 128 + 65], PT[r][:, hh * 128:(hh + 1) * 128], V[:, j, h, :],
                         (j == 0 and hh == 0), j == i, [PTB[r], VB[j]], [psOb[half]], sig=(hh == 3))

            LOOK = 2
            rq = []
            for u in range(min(LOOK, len(units))):
                rq.append(scores(*units[u]))
            for u in range(len(units)):
                if u + LOOK < len(units):
                    rq.append(scores(*units[u + LOOK]))
                pv(units[u][0], units[u][1], rq[u])

        def S3b1(b, i):
            k2 = i % 2
            psO = [banks[6], banks[7]]
            psOb = [bankb[6], bankb[7]]
            for half in range(2):
                pv = psO[half][:, 0:512].rearrange("p (h t) -> p h t", h=4)
                P.emit("vector", (lambda pv=pv, half=half: (lambda e: e.reciprocal(out=rs[:, half * 4:(half + 1) * 4],
                                                                                   in_=pv[:, :, 64])))(),
                       [psOb[half]], [rsB])
                P.tt("vector", obf[:].rearrange("p (h d) -> p h d", d=64)[:, half * 4:(half + 1) * 4, :],
                     pv[:, :, 0:64], rs[:, half * 4:(half + 1) * 4].unsqueeze(2).broadcast_to([128, 4, 64]),
                     ALU.mult, [psOb[half], rsB], [obfB])

        def S3b1b(b, i):
            k2 = i % 2
            ps, psb = bank()
            for c in range(4):
                P.tr(ps[:, c * 128:(c + 1) * 128], obf[:, c * 128:(c + 1) * 128], ident_f[:], [obfB, consts], [psb],
                     sig=(c == 3))
            P.cp("scalar", catT[k2][:, 0:4, :], ps[:, 0:512].rearrange("p (c t) -> p c t", c=4), [psb], [catB[k2]])
            yk, ykB = ybuf[k2], yB[k2]
            for half in range(2):
                ps, psb = bank()
                for kc in range(KC):
                    P.mm(ps[:, 0:512], catT[k2][:, kc, :], Wo[:, kc, half * 512:(half + 1) * 512], kc == 0, kc == KC - 1,
                         [catB[k2], WoB[kc]], [psb], sig=(kc == KC - 1))
                P.cp("scalar", yk[:, half * 512:(half + 1) * 512], ps[:, 0:512], [psb], [ykB])

        def S3b2(b, i):
            k2 = i % 2
            k3 = (b * NT + i) % 3
            mb = b % 2
            yk, ykB = ybuf[k2], yB[k2]
            P.tt("gpsimd", yk[:], yk[:], gate1[mb][:], ALU.mult, [ykB, modB[mb]], [ykB])
            P.stt("vector", yk[:], xs[k3][:], ALPHA, yk[:], ALU.mult, ALU.add, [xsB[k3], ykB], [ykB])
            layer_norm(yk, ykB, ln1g[:], ln1b[:], cA)
            P.dma("sync", x1scr_d[(b * NT + i) * 128:(b * NT + i + 1) * 128, :], yk[:], [ykB], [x1B[b][i]])

        tiles = [(b, i) for b in range(NB) for i in range(NT)]
        S1a(*tiles[0])
        stop_at(32)
        S1b(*tiles[0])
        stop_at(33)
        S2(*tiles[0])
        S1c(*tiles[0])
        stop_at(34)
        for g in range(len(tiles)):
            nxt = tiles[g + 1] if g + 1 < len(tiles) else None
            if nxt and nxt[1] == 0:
                if g > 0:
                    S3b2(*tiles[g - 1])
                S3a(*tiles[g])
                S3b1(*tiles[g])
                S1a(*nxt)
                S1b(*nxt)
                S2(*nxt)
                S1c(*nxt)
                S3b1b(*tiles[g])
                continue
            if nxt:
                S1a(*nxt)
            S3a(*tiles[g])
            S3b1(*tiles[g])
            if g > 0:
                S3b2(*tiles[g - 1])
            if nxt:
                S1b(*nxt)
                S2(*nxt)
                S1c(*nxt)
            S3b1b(*tiles[g])
        S3b2(*tiles[-1])

        stop_at(20)
        P.barrier()
        sb.reset(mA)

        Wg = sb.alloc([128, KC, DFF], BF16, "Wg")
        Wu = sb.alloc([128, KC, DFF], BF16, "Wu")
        Wd = sb.alloc([128, FC, D], BF16, "Wd")
        ln2g = sb.alloc([128, D], F32, "ln2g")
        ln2b = sb.alloc([128, D], F32, "ln2b")
        gate2 = [sb.alloc([128, D], F32, "gate2") for _ in range(2)]
        sc2 = [sb.alloc([128, KC], F32, "sc2") for _ in range(2)]
        sh2 = [sb.alloc([128, KC], F32, "sh2") for _ in range(2)]
        x1s = [sb.alloc([128, D], F32, "x1s") for _ in range(4)]
        h2T = [sb.alloc([128, KC, 256], BF16, "h2T") for _ in range(2)]
        aT = sb.alloc([128, FC, 256], BF16, "aT")
        sgt = [sb.alloc([128, 256], F32, "sgt") for _ in range(3)]
        y2 = [sb.alloc([128, D], F32, "y2") for _ in range(2)]
        lnst2 = sb.alloc([128, 24], F32, "lnst2")
        WgB = [Buf() for _ in range(KC)]
        WuB = [Buf() for _ in range(KC)]
        WdB = [Buf() for _ in range(FC)]
        cB, modB2 = Buf(), [Buf(), Buf()]
        x1sB = [Buf() for _ in range(4)]
        h2TB = [Buf(), Buf()]
        aTB = Buf()
        sgtB = [Buf() for _ in range(3)]
        y2B = [Buf(), Buf()]
        lnB2 = Buf()
        wg_v = wg_d.rearrange("(k p) c -> p k c", p=128)
        wu_v = wu_d.rearrange("(k p) c -> p k c", p=128)
        wd_v = wd_d.rearrange("(f p) c -> p f c", p=128)
        for kc in range(KC):
            P.dma("gpsimd", Wg[:, kc, :], wg_v[:, kc, :], (), [WgB[kc]])
            P.dma("gpsimd", Wu[:, kc, :], wu_v[:, kc, :], (), [WuB[kc]])
        for fc in range(FC):
            P.dma("gpsimd", Wd[:, fc, :], wd_v[:, fc, :], (), [WdB[fc]])
        P.dma("sync", ln2g[:], ln2g_d[0:1, :].partition_broadcast(128), (), [cB])
        P.dma("sync", ln2b[:], ln2b_d[0:1, :].partition_broadcast(128), (), [cB])

        def layer_norm2(src_y, yb, gam, bet, gB):
            st = lnst2[:, 0:12]
            mv = lnst2[:, 12:14]
            veps = lnst2[:, 14:15]
            rstd = lnst2[:, 15:16]
            nmr = lnst2[:, 16:17]
            P.emit("vector", lambda e: e.bn_stats(out=lnst2[:, 0:6], in_=src_y[:, 0:512]), [yb], [lnB2])
            P.emit("vector", lambda e: e.bn_stats(out=lnst2[:, 6:12], in_=src_y[:, 512:1024]), [yb], [lnB2])
            P.emit("vector", lambda e: e.bn_aggr(out=mv, in_=st), [lnB2], [lnB2])
            P.ts("vector", veps, mv[:, 1:2], EPS, None, ALU.add, None, [lnB2], [lnB2])
            P.tt("gpsimd", rstd, veps, mhalf[:], ALU.pow, [lnB2, consts], [lnB2])
            P.stt("vector", nmr, mv[:, 0:1], -1.0, rstd, ALU.mult, ALU.mult, [lnB2], [lnB2])
            P.act(src_y[:], src_y[:], AF.Identity, [yb, lnB2], [yb], bias=nmr, scale=rstd)
            P.tt("gpsimd", src_y[:], src_y[:], gam, ALU.mult, [yb, gB], [yb])
            P.tt("gpsimd", src_y[:], src_y[:], bet, ALU.add, [yb, gB], [yb])

        x1_i = {"i": 0}
        y2_i = {"i": 0}
        sg_i = {"i": 0}
        NG = NT // 2
        groups = [(b, g) for b in range(NB) for g in range(NG)]
        xts = {}

        def prep(gi):
            b, g = groups[gi]
            hk = gi % 2
            mb = b % 2
            if g == 0:
                mrow = modscr_d[b]
                P.dma("sync", sc2[mb][:], mrow[4 * D:5 * D].rearrange("(k p) -> p k", p=128), [modscr], [modB2[mb]],
                      slow=True)
                P.dma("sync", sh2[mb][:], mrow[3 * D:4 * D].rearrange("(k p) -> p k", p=128), [modscr], [modB2[mb]],
                      slow=True)
                P.dma("sync", gate2[mb][:], modscr_d[b:b + 1, 5 * D:6 * D].partition_broadcast(128), [modscr],
                      [modB2[mb]])
                P.ts("vector", sc2[mb][:], sc2[mb][:], 1.0, None, ALU.add, None, [modB2[mb]], [modB2[mb]])
            xt = []
            for tl in range(2):
                i = 2 * g + tl
                xk = x1_i["i"] % 4
                x1_i["i"] += 1
                xt.append(xk)
                P.dma("sync", x1s[xk][:], x1scr_d[(b * NT + i) * 128:(b * NT + i + 1) * 128, :], [x1B[b][i]], [x1sB[xk]])
                for half in range(2):
                    ps, psb = bank(0, 8)
                    for cc in range(4):
                        kc = half * 4 + cc
                        P.tr(ps[:, cc * 128:(cc + 1) * 128], x1s[xk][:, kc * 128:(kc + 1) * 128], ident_f[:],
                             [x1sB[xk], consts], [psb], sig=(cc == 3))
                    for cc in range(4):
                        kc = half * 4 + cc
                        P.act(h2T[hk][:, kc, tl * 128:(tl + 1) * 128], ps[:, cc * 128:(cc + 1) * 128], AF.Identity,
                              [psb, modB2[mb]], [h2TB[hk]], bias=sh2[mb][:, kc:kc + 1], scale=sc2[mb][:, kc:kc + 1])
            xts[gi] = xt

        def gateup(gi):
            hk = gi % 2
            for fc in range(FC):
                ps, psb = bank(0, 8)
                for kc in range(KC):
                    P.mm(ps[:, 0:256], Wg[:, kc, fc * 128:(fc + 1) * 128], h2T[hk][:, kc, :], kc == 0, kc == KC - 1,
                         [WgB[kc], h2TB[hk]], [psb], sig=False)
                for kc in range(KC):
                    P.mm(ps[:, 256:512], Wu[:, kc, fc * 128:(fc + 1) * 128], h2T[hk][:, kc, :], kc == 0, kc == KC - 1,
                         [WuB[kc], h2TB[hk]], [psb], sig=(kc == KC - 1))
                sk = sg_i["i"] % 3
                sg_i["i"] += 1
                P.act(sgt[sk][:], ps[:, 0:256], AF.Silu, [psb], [sgtB[sk]])
                P.tt("vector", aT[:, fc, :], sgt[sk][:], ps[:, 256:512], ALU.mult, [sgtB[sk], psb], [aTB])

        def down(gi):
            b, g = groups[gi]
            mb = b % 2
            xt = xts[gi]
            for tl in range(2):
                i = 2 * g + tl
                yk = y2_i["i"] % 2
                y2_i["i"] += 1
                for half in range(2):
                    ps, psb = bank(0, 8)
                    for fc in range(FC):
                        P.mm(ps[:, 0:512], aT[:, fc, tl * 128:(tl + 1) * 128], Wd[:, fc, half * 512:(half + 1) * 512],
                             fc == 0, fc == FC - 1, [aTB, WdB[fc]], [psb], sig=(fc == FC - 1))
                    P.tt("vector", y2[yk][:, half * 512:(half + 1) * 512], ps[:, 0:512],
                         gate2[mb][:, half * 512:(half + 1) * 512], ALU.mult, [psb, modB2[mb]], [y2B[yk]])
                P.stt("vector", y2[yk][:], x1s[xt[tl]][:], ALPHA, y2[yk][:], ALU.mult, ALU.add,
                      [x1sB[xt[tl]], y2B[yk]], [y2B[yk]])
                layer_norm2(y2[yk], y2B[yk], ln2g[:], ln2b[:], cB)
                P.dma("sync", y_d[b, i * 128:(i + 1) * 128, :], y2[yk][:], [y2B[yk]], [Buf()])

        prep(0)
        for gi in range(len(groups)):
            gateup(gi)
            if gi + 1 < len(groups):
                prep(gi + 1)
            down(gi)

    except StopBuild:
        pass
    P.finish()
    with nc.Block() as block:
        P.replay(block)
    for c in reversed(sem_ctx):
        c.__exit__(None, None, None)
    return nc, P, sb


def host_consts(S, NITER):
    inv_freq = (10000.0 ** (-np.arange(0, 64, 2, dtype=np.float32) / np.float32(64))).astype(np.float32)
    ang = (np.arange(S, dtype=np.float32)[:, None] * inv_freq[None, :]).astype(np.float32)
    cos = np.cos(ang).astype(np.float32)
    sin = np.sin(ang).astype(np.float32)
    ident = np.eye(128, dtype=np.float32)
    t = np.arange(128)
    cb = np.where(t[None, :] <= t[:, None], 0.0, -BIG).astype(np.float32)
    bands = np.zeros((12, 128, 128), np.float32)
    tp = t[:, None]
    tq = t[None, :]
    for g, w in enumerate(POOL_WINDOWS):
        bd = ((tp >= tq - w + 1) & (tp <= tq)).astype(np.float32) / w - (tp == tq)
        bp = (tp >= tq + 129 - w).astype(np.float32) / w
        cntf = np.minimum(tq + 1, w).astype(np.float32)
        bdf = ((tp >= np.maximum(tq - w + 1, 0)) & (tp <= tq)).astype(np.float32) / cntf - (tp == tq)
        bands[g * 3 + 0] = bd
        bands[g * 3 + 1] = bp
        bands[g * 3 + 2] = bdf
    pw = np.zeros((128, 2 * NITER), np.float32)
    for k in range(NITER):
        pw[:, k] = 2.0 ** -(k + 1)
        pw[:, NITER + k] = 2.0 ** -k
    return {"k_cos": cos, "k_sin": sin, "k_nsin": -sin, "k_ident": ident, "k_cb": cb, "k_bands": bands, "k_pw": pw}


_CACHE = {}


def run(inputs, n_cores, NB, S, KSEL, NITER):
    key = (NB, S, KSEL, NITER)
    if key not in _CACHE:
        _CACHE[key] = build_program(NB, S, KSEL, NITER)[0]
    nc = _CACHE[key]
    f = lambda a: np.ascontiguousarray(np.asarray(a, dtype=np.float32))
    shared = {
        "w_mod": f(inputs["w_mod"][0]), "b_mod": f(inputs["b_mod"][0]).reshape(1, -1), "w_in": f(inputs["w_in"][0]),
        "w_pool": f(inputs["w_pool"][0]), "pool_scale": f(inputs["pool_scale"][0]), "w_o": f(inputs["w_o"][0]),
        "ln1_g": f(inputs["ln1_g"][0]).reshape(1, -1), "ln1_b": f(inputs["ln1_b"][0]).reshape(1, -1),
        "w_gate": f(inputs["w_gate"][0]), "w_up": f(inputs["w_up"][0]), "w_down": f(inputs["w_down"][0]),
        "ln2_g": f(inputs["ln2_g"][0]).reshape(1, -1), "ln2_b": f(inputs["ln2_b"][0]).reshape(1, -1),
    }
    shared.update(host_consts(S, NITER))
    x = f(inputs["x"])
    c = f(inputs["c"])
    in_maps = []
    for k in range(n_cores):
        m = dict(shared)
        m["x"] = np.ascontiguousarray(x[k * NB:(k + 1) * NB])
        m["c"] = np.ascontiguousarray(c[k * NB:(k + 1) * NB])
        in_maps.append(m)
    res = run_bass_kernel_spmd(nc, in_maps, core_ids=list(range(n_cores)))
    return np.concatenate([np.asarray(r["y"]) for r in res.results], axis=0).astype(np.float32)


def kernel(x, c, w_mod, b_mod, w_in, w_pool, pool_scale, w_o, ln1_g, ln1_b, w_gate, w_up, w_down, ln2_g, ln2_b):
    inputs = dict(x=x, c=c, w_mod=w_mod, b_mod=b_mod, w_in=w_in, w_pool=w_pool, pool_scale=pool_scale, w_o=w_o,
                  ln1_g=ln1_g, ln1_b=ln1_b, w_gate=w_gate, w_up=w_up, w_down=w_down, ln2_g=ln2_g, ln2_b=ln2_b)
    return run(inputs, N_CORES, 4, 2048, 256, 10)
```

```python
import numpy as np
import concourse.bass as bass
import concourse.mybir as mybir
from concourse.bass_utils import run_bass_kernel_spmd

F32 = mybir.dt.float32
BF16 = mybir.dt.bfloat16
U8 = mybir.dt.uint8
AF = mybir.ActivationFunctionType
ALU = mybir.AluOpType

D = 1024
KC = 8
DFF = 2816
FC = 22
WIN = 2632
NMOD = 6
ALPHA = float(2.0 ** 0.25)
EPS = 1e-5
BIG = 32768.0
N_CORES = 8
POOL_WINDOWS = (2, 4, 8, 16)
SAME_WAIT = True
NO_SELF_WAIT = ("tensor",)
STOP = 99
DEBUG = 0


class StopBuild(Exception):
    pass


def stop_at(k):
    if STOP == k:
        raise StopBuild()


class Tok:
    __slots__ = ("sem", "val")

    def __init__(self, sem, val):
        self.sem = sem
        self.val = val


class Sem:
    def __init__(self, h):
        self.h = h
        self.count = 0
        self.last = None


class Buf:
    __slots__ = ("name", "w", "r")

    def __init__(self, name=""):
        self.name = name
        self.w = {}
        self.r = {}


class Eng:
    def __init__(self, name, sem):
        self.name = name
        self.sem = sem
        self.ops = []
        self.seen = {}
        self.pending = []
        self.dma = []
        self.dma_i = 0


class Prog:
    def __init__(self, nc, sems):
        self.nc = nc
        it = iter(sems)
        self.eng = {}
        for n in ("tensor", "vector", "scalar", "gpsimd", "sync"):
            self.eng[n] = Eng(n, Sem(next(it)))
        self.eng["sync"].dma = [Sem(next(it)) for _ in range(40)]
        self.eng["gpsimd"].dma = [Sem(next(it)) for _ in range(16)]
        self.n_ins = 0

    def emit(self, eng, fn, reads=(), writes=(), sig=True, dma=False):
        E = self.eng[eng]
        need = {}

        def add(tok):
            if tok.sem is E.sem:
                if eng in NO_SELF_WAIT or not SAME_WAIT:
                    return
            assert tok.val is not None, f"unresolved token needed by {eng}"
            cur = need.get(tok.sem)
            if cur is None or tok.val > cur:
                need[tok.sem] = tok.val

        for b in reads:
            for t in b.w.values():
                add(t)
        for b in writes:
            for t in b.w.values():
                add(t)
            for t in b.r.values():
                add(t)
        if dma:
            s = E.dma[E.dma_i % len(E.dma)]
            E.dma_i += 1
            if s.last is not None:
                add(s.last)
        for sem, val in need.items():
            if E.seen.get(sem, 0) < val:
                E.ops.append(("wait", sem.h, val))
                E.seen[sem] = val
        if dma:
            s.count += 16
            tok = Tok(s, s.count)
            s.last = tok
            E.ops.append(("ins", fn, s.h, 16))
        elif sig:
            E.sem.count += 1
            tok = Tok(E.sem, E.sem.count)
            for p in E.pending:
                p.val = tok.val
            E.pending = []
            E.ops.append(("ins", fn, E.sem.h, 1))
        else:
            tok = Tok(E.sem, None)
            E.pending.append(tok)
            E.ops.append(("ins", fn, None, 0))
        for b in writes:
            b.w = {tok.sem: tok}
            b.r = {}
        for b in reads:
            if b not in writes:
                b.r[tok.sem] = tok
        self.n_ins += 1
        return tok

    def barrier(self):
        toks = []
        for E in self.eng.values():
            assert not E.pending
            if E.sem.count:
                toks.append((E.sem, E.sem.count))
            for s in E.dma:
                if s.last is not None:
                    toks.append((s, s.last.val))
        for E in self.eng.values():
            for sem, val in toks:
                if sem is E.sem:
                    continue
                if E.seen.get(sem, 0) < val:
                    E.ops.append(("wait", sem.h, val))
                    E.seen[sem] = val

    def finish(self):
        E = self.eng["sync"]
        for En in self.eng.values():
            for s in En.dma:
                if s.last is not None and E.seen.get(s, 0) < s.last.val:
                    E.ops.append(("wait", s.h, s.last.val))
                    E.seen[s] = s.last.val

    def replay(self, block):
        def mk(name):
            E = self.eng[name]

            def body(e):
                for op in E.ops:
                    if op[0] == "wait":
                        e.wait_ge(op[1], op[2])
                    else:
                        ins = op[1](e)
                        if op[2] is not None:
                            ins.then_inc(op[2], op[3])
            return body
        block.sync(mk("sync"))
        block.tensor(mk("tensor"))
        block.vector(mk("vector"))
        block.scalar(mk("scalar"))
        block.gpsimd(mk("gpsimd"))

    def mm(self, out, lhsT, rhs, start, stop, reads, writes, sig=True):
        return self.emit("tensor", lambda e: e.matmul(out, lhsT=lhsT, rhs=rhs, start=start, stop=stop),
                         reads, writes, sig=sig)

    def tr(self, out, in_, ident, reads, writes, sig=True):
        return self.emit("tensor", lambda e: e.transpose(out=out, in_=in_, identity=ident), reads, writes, sig=sig)

    def act(self, out, in_, func, reads, writes, bias=None, scale=None):
        kw = {}
        if bias is not None:
            kw["bias"] = bias
        if scale is not None:
            kw["scale"] = scale
        return self.emit("scalar", lambda e: e.activation(out=out, in_=in_, func=func, **kw), reads, writes)

    def ts(self, eng, out, in0, s1, s2, op0, op1, reads, writes, accum=None):
        kw = {}
        if op1 is not None:
            kw["op1"] = op1
        if accum is not None:
            kw["accum_out"] = accum
        return self.emit(eng, lambda e: e.tensor_scalar(out=out, in0=in0, scalar1=s1, scalar2=s2, op0=op0, **kw),
                         reads, writes)

    def tt(self, eng, out, in0, in1, op, reads, writes):
        return self.emit(eng, lambda e: e.tensor_tensor(out=out, in0=in0, in1=in1, op=op), reads, writes)

    def stt(self, eng, out, in0, scalar, in1, op0, op1, reads, writes):
        return self.emit(eng, lambda e: e.scalar_tensor_tensor(out=out, in0=in0, scalar=scalar, in1=in1,
                                                               op0=op0, op1=op1), reads, writes)

    def cp(self, eng, out, in_, reads, writes):
        if eng == "scalar":
            return self.emit(eng, lambda e: e.copy(out=out, in_=in_), reads, writes)
        return self.emit(eng, lambda e: e.tensor_copy(out=out, in_=in_), reads, writes)

    def memset(self, eng, ap, val, writes):
        return self.emit(eng, lambda e: e.memset(ap, val), (), writes)

    def dma(self, eng, out, in_, reads, writes, slow=False):
        if slow:
            return self.emit(eng, lambda e: e.dma_start(out=out, in_=in_, allow_slow_non_contiguous=True),
                             reads, writes, dma=True)
        return self.emit(eng, lambda e: e.dma_start(out=out, in_=in_), reads, writes, dma=True)


class SB:
    def __init__(self, nc, base, size):
        self.nc = nc
        self.base = base
        self.cur = base
        self.end = base + size
        self.k = 0
        self.peak = base

    def alloc(self, shape, dtype, name="t"):
        isz = 4 if dtype == F32 else 2
        n = 1
        for s in shape[1:]:
            n *= s
        off = (self.cur + 63) // 64 * 64
        self.k += 1
        h = self.nc.alloc_sbuf_tensor_at(f"{name}{self.k}", list(shape), dtype, offset=off)
        self.cur = off + n * isz
        assert self.cur <= self.end, f"SBUF overflow allocating {name}: {self.cur - self.base} > {self.end - self.base}"
        self.peak = max(self.peak, self.cur)
        return h

    def mark(self):
        return self.cur

    def reset(self, m):
        self.cur = m


class Ring:
    def __init__(self, items):
        self.items = items
        self.i = 0

    def next(self):
        it = self.items[self.i % len(self.items)]
        self.i += 1
        return it


def build_program(NB, S, KSEL, NITER):
    NT = S // 128
    nc = bass.Bass("TRN2", target_bir_lowering=False)

    def din(name, shape):
        return nc.dram_tensor(name, list(shape), F32, kind="ExternalInput").ap()

    x_d = din("x", [NB, S, D])
    c_d = din("c", [NB, D])
    wmod_d = din("w_mod", [D, NMOD * D])
    bmod_d = din("b_mod", [1, NMOD * D])
    win_d = din("w_in", [D, WIN])
    wpool_d = din("w_pool", [4, 128, 128])
    pscale_d = din("pool_scale", [512])
    wo_d = din("w_o", [D, D])
    ln1g_d = din("ln1_g", [1, D])
    ln1b_d = din("ln1_b", [1, D])
    wg_d = din("w_gate", [D, DFF])
    wu_d = din("w_up", [D, DFF])
    wd_d = din("w_down", [DFF, D])
    ln2g_d = din("ln2_g", [1, D])
    ln2b_d = din("ln2_b", [1, D])
    cos_d = din("k_cos", [S, 32])
    sin_d = din("k_sin", [S, 32])
    nsin_d = din("k_nsin", [S, 32])
    ident_d = din("k_ident", [128, 128])
    cb_d = din("k_cb", [128, 128])
    bands_d = din("k_bands", [12, 128, 128])
    pw_d = din("k_pw", [128, 2 * NITER])
    y_d = nc.dram_tensor("y", [NB, S, D], F32, kind="ExternalOutput").ap()
    if DEBUG:
        dbg_score = nc.dram_tensor("dbg_score", [128, S], F32, kind="ExternalOutput").ap()
        dbg_mb = nc.dram_tensor("dbg_mb", [128, S], F32, kind="ExternalOutput").ap()
        dbg_bis = nc.dram_tensor("dbg_bis", [128, 2 * NITER + 8], F32, kind="ExternalOutput").ap()
        dbg_w = nc.dram_tensor("dbg_w", [128, 8], F32, kind="ExternalOutput").ap()
    modscr_d = nc.dram_tensor("mod_scr", [NB, NMOD * D], F32).ap()
    x1scr_d = nc.dram_tensor("x1_scr", [NB * S, D], F32).ap()

    ARENA = 207 * 1024
    arena = nc.alloc_sbuf_tensor("arena", [128, ARENA], U8)
    base = nc.lookup_mloc(arena).addr
    sb = SB(nc, base, ARENA)
    banks = [nc.alloc_psum_tensor(f"bank{i}", [128, 512], F32) for i in range(8)]
    bankb = [Buf(f"bank{i}") for i in range(8)]

    sem_ctx = [nc.semaphore(f"s{i}") for i in range(5 + 40 + 16)]
    sems = [c.__enter__() for c in sem_ctx]
    P = Prog(nc, sems)

    try:
        ident_f = sb.alloc([128, 128], F32, "identf")
        ident_b = sb.alloc([128, 128], BF16, "identb")
        identx4 = sb.alloc([128, 512], BF16, "identx4")
        cb_f = sb.alloc([128, 128], F32, "cbf")
        cb_b = sb.alloc([128, 128], BF16, "cbb")
        mhalf = sb.alloc([128, 1], F32, "mhalf")
        consts = Buf("consts")
        P.dma("sync", ident_f[:], ident_d, (), [consts])
        P.dma("sync", cb_f[:], cb_d, (), [consts])
        P.cp("vector", ident_b[:], ident_f[:], [consts], [consts])
        for r in range(4):
            P.cp("vector", identx4[:, r * 128:(r + 1) * 128], ident_f[:], [consts], [consts])
        P.cp("vector", cb_b[:], cb_f[:], [consts], [consts])
        P.memset("vector", mhalf[:], -0.5, [consts])

        stop_at(0)
        rr = {"i": 0}

        def bank(lo=0, hi=5):
            k = lo + rr["i"] % (hi - lo)
            rr["i"] += 1
            return banks[k], bankb[k]

        m0 = sb.mark()
        cT = sb.alloc([128, KC, NB], F32, "cT")
        siluT = sb.alloc([128, KC, NB], BF16, "siluT")
        bmod_b = sb.alloc([128, NMOD * D], F32, "bmodb")
        modrow = sb.alloc([128, NMOD * D], F32, "modrow")
        wm = [sb.alloc([128, KC, 512], BF16, "wm") for _ in range(2)]
        cTb, bmb, mrb = Buf(), Buf(), Buf()
        wmb = [[Buf() for _ in range(KC)] for _ in range(2)]
        for b in range(NB):
            P.dma("sync", cT[:, :, b], c_d[b].rearrange("(k p) -> p k", p=128), (), [cTb], slow=True)
        P.dma("sync", bmod_b[0:NB, :], bmod_d[0:1, :].partition_broadcast(NB), (), [bmb])
        P.act(siluT[:], cT[:], AF.Silu, [cTb], [cTb])
        wmod_v = wmod_d.rearrange("(k p) c -> p k c", p=128)
        for blk in range(NMOD * 2):
            w = wm[blk % 2]
            wb = wmb[blk % 2]
            for kc in range(KC):
                P.dma("gpsimd", w[:, kc, :], wmod_v[:, kc, blk * 512:(blk + 1) * 512], (), [wb[kc]])
            ps, psb = bank()
            for kc in range(KC):
                P.mm(ps[0:NB, :], siluT[:, kc, :], w[:, kc, :], kc == 0, kc == KC - 1, [cTb, wb[kc]], [psb], sig=(kc == KC - 1))
            P.tt("vector", modrow[0:NB, blk * 512:(blk + 1) * 512], ps[0:NB, :], bmod_b[0:NB, blk * 512:(blk + 1) * 512],
                 ALU.add, [psb, bmb], [mrb])
        stop_at(1)
        modscr = Buf("modscr")
        P.dma("sync", modscr_d, modrow[0:NB, :], [mrb], [modscr])
        stop_at(2)
        P.barrier()
        sb.reset(m0)

        mA = sb.mark()
        Win = sb.alloc([128, KC, WIN], BF16, "Win")
        Wo = sb.alloc([128, KC, D], BF16, "Wo")
        Wpool = sb.alloc([128, 4, 128], BF16, "Wpool")
        bands = sb.alloc([128, 12, 128], BF16, "bands")
        cos_t = sb.alloc([128, NT, 32], F32, "cos")
        sin_t = sb.alloc([128, NT, 32], F32, "sin")
        nsin_t = sb.alloc([128, NT, 32], F32, "nsin")
        pw_t = sb.alloc([128, 2 * NITER], F32, "pw")
        ln1g = sb.alloc([128, D], F32, "ln1g")
        ln1b = sb.alloc([128, D], F32, "ln1b")
        gate1 = [sb.alloc([128, D], F32, "gate1") for _ in range(2)]
        sc1 = [sb.alloc([128, KC], F32, "sc1") for _ in range(2)]
        sh1 = [sb.alloc([128, KC], F32, "sh1") for _ in range(2)]
        pe = [sb.alloc([128, 512], F32, "pe") for _ in range(3)]
        kT = sb.alloc([128, 4, S], BF16, "kT")
        ikzA = sb.alloc([128, S], BF16, "ikzA")
        ikzB = sb.alloc([128, S], BF16, "ikzB")
        V = sb.alloc([128, NT, 8, 65], BF16, "V")
        xs = [sb.alloc([128, D], F32, "xs") for _ in range(3)]
        hT = [sb.alloc([128, KC, 128], BF16, "hT") for _ in range(2)]
        qtok = sb.alloc([128, 4, 192], F32, "qtok")
        ktok = sb.alloc([128, 512], F32, "ktok")
        iqtok = sb.alloc([128, 512], F32, "iqtok")
        iktok = sb.alloc([128, 192], F32, "iktok")
        rt1 = [sb.alloc([128, 512], F32, "rt1") for _ in range(1)]
        rt2 = [sb.alloc([128, 512], F32, "rt2") for _ in range(1)]
        qz = [sb.alloc([128, 8, 128], BF16, "qz") for _ in range(2)]
        iqT = [sb.alloc([128, 4, 128], BF16, "iqT") for _ in range(2)]
        w_t = [sb.alloc([128, 8], F32, "wt") for _ in range(2)]
        ubuf = [sb.alloc([128, 512], BF16, "u") for _ in range(2)]
        pooledT = sb.alloc([128, 512], BF16, "pooledT")
        catT = [sb.alloc([128, KC, 128], BF16, "catT") for _ in range(2)]
        obf = sb.alloc([128, 512], F32, "obf")
        rs = sb.alloc([128, 8], F32, "rs")
        score = sb.alloc([128, S], F32, "score")
        MB = sb.alloc([128, S], BF16, "MB")
        Rr = [sb.alloc([128, 512], BF16, "R") for _ in range(4)]
        dg = sb.alloc([128, 8, 128], BF16, "dg")
        PT = [sb.alloc([128, 512], BF16, "PT") for _ in range(4)]
        bis = sb.alloc([128, 2 * NITER + 8], F32, "bis")
        ybuf = [sb.alloc([128, D], F32, "y") for _ in range(2)]
        lnst = sb.alloc([128, 24], F32, "lnst")

        WinB = [Buf() for _ in range(KC)]
        WoB = [Buf() for _ in range(KC)]
        WpoolB, cA = Buf("Wpool"), Buf("constsA")
        win_v = win_d.rearrange("(k p) c -> p k c", p=128)
        wo_v = wo_d.rearrange("(k p) c -> p k c", p=128)
        for kc in range(KC):
            P.dma("gpsimd", Win[:, kc, :], win_v[:, kc, :], (), [WinB[kc]])
        for kc in range(KC):
            P.dma("gpsimd", Wo[:, kc, :], wo_v[:, kc, :], (), [WoB[kc]])
        P.dma("gpsimd", Wpool[:], wpool_d.rearrange("g c d -> c g d"), (), [WpoolB])
        P.dma("gpsimd", bands[:], bands_d.rearrange("n a b -> a n b"), (), [cA])
        P.dma("sync", cos_t[:], cos_d.rearrange("(i p) d -> p i d", p=128), (), [cA])
        P.dma("sync", sin_t[:], sin_d.rearrange("(i p) d -> p i d", p=128), (), [cA])
        P.dma("sync", nsin_t[:], nsin_d.rearrange("(i p) d -> p i d", p=128), (), [cA])
        P.dma("sync", pw_t[:], pw_d, (), [cA])
        P.dma("sync", ln1g[:], ln1g_d[0:1, :].partition_broadcast(128), (), [cA])
        P.dma("sync", ln1b[:], ln1b_d[0:1, :].partition_broadcast(128), (), [cA])
        qtokB, ktokB, iqtokB, iktokB = Buf("qtok"), Buf("ktok"), Buf("iqtok"), Buf("iktok")
        P.memset("vector", qtok[:], 0.0, [qtokB])
        P.memset("vector", iktok[:], 0.0, [iktokB])
        VB = [Buf(f"V{i}") for i in range(NT)]
        kTB = [Buf(f"kT{i}") for i in range(NT)]
        ikzBf = [Buf(f"ikz{i}") for i in range(NT)]
        for i in range(NT):
            P.memset("gpsimd", V[:, i, :, 64:65], 1.0, [VB[i]])

        stop_at(3)
        pse, pseB = rt1[0], Buf()
        P.dma("sync", pse[:], pscale_d.rearrange("(o n) -> o n", o=1).partition_broadcast(128), (), [pseB])
        P.tt("vector", Wpool[:].rearrange("c g d -> c (g d)"), Wpool[:].rearrange("c g d -> c (g d)"), pse[:], ALU.mult,
             [pseB, WpoolB], [WpoolB])
        stop_at(31)
        xsB = [Buf("xs0"), Buf("xs1"), Buf("xs2")]
        hTB = [[Buf() for _ in range(KC)] for _ in range(2)]
        rt1B = [Buf()]
        rt2B = [Buf()]
        rt1B[0] = pseB
        peB = [Buf() for _ in range(3)]
        rt_i = {"i": 0}
        pe_i = {"i": 0}
        qzB, iqTB, wtB = [Buf(), Buf()], [Buf(), Buf()], [Buf(), Buf()]
        uB = [Buf(), Buf()]
        pooledB = Buf()
        catB = [Buf(), Buf()]
        obfB, rsB, scoreB, MBB, dgB, bisB, lnB = Buf(), Buf(), Buf(), Buf(), Buf(), Buf(), Buf()
        RB = [Buf() for _ in range(4)]
        PTB = [Buf() for _ in range(4)]
        R_i = {"i": 0}
        PT_i = {"i": 0}
        yB = [Buf(), Buf()]
        modB = [Buf("modA0"), Buf("modA1")]
        x1B = [[Buf() for _ in range(NT)] for _ in range(NB)]

        steps = bis[:, 0:NITER]
        steps2 = bis[:, NITER:2 * NITER]
        A_ap = bis[:, 2 * NITER:2 * NITER + 1]
        tau = bis[:, 2 * NITER + 1:2 * NITER + 2]
        cnt = bis[:, 2 * NITER + 2:2 * NITER + 3]
        dcol = bis[:, 2 * NITER + 3:2 * NITER + 4]

        def rope(ps, psb, H, i, dests):
            pk = pe_i["i"] % 3
            pe_i["i"] += 1
            P.act(pe[pk][:, 0:H * 64], ps, AF.Copy, [psb], [peB[pk]])
            k = 0
            t1, t2 = rt1[k], rt2[k]
            X = pe[pk][:, 0:H * 64].rearrange("p (h two d) -> p h two d", two=2, d=32)
            t1v = t1[:, 0:H * 64].rearrange("p (h two d) -> p h two d", two=2, d=32)
            t2v = t2[:, 0:H * 64].rearrange("p (h two d) -> p h two d", two=2, d=32)
            cosb = cos_t[:, i, :].unsqueeze(1).unsqueeze(1).broadcast_to([128, H, 2, 32])
            sinb = sin_t[:, i, :].unsqueeze(1).broadcast_to([128, H, 32])
            nsinb = nsin_t[:, i, :].unsqueeze(1).broadcast_to([128, H, 32])
            P.tt("gpsimd", t1v, X, cosb, ALU.mult, [peB[pk], cA], [rt1B[k]])
            P.tt("gpsimd", t2v[:, :, 0, :], X[:, :, 1, :], nsinb, ALU.mult, [peB[pk], cA], [rt2B[k]])
            P.tt("gpsimd", t2v[:, :, 1, :], X[:, :, 0, :], sinb, ALU.mult, [peB[pk], cA], [rt2B[k]])
            for sl, dest, destB in dests:
                P.tt("gpsimd", dest, sl(t1), sl(t2), ALU.add, [rt1B[k], rt2B[k]], [destB])

        def layer_norm(src_y, yb, gam, bet, gB):
            st = lnst[:, 0:12]
            mv = lnst[:, 12:14]
            veps = lnst[:, 14:15]
            rstd = lnst[:, 15:16]
            nmr = lnst[:, 16:17]
            P.emit("vector", lambda e: e.bn_stats(out=lnst[:, 0:6], in_=src_y[:, 0:512]), [yb], [lnB])
            P.emit("vector", lambda e: e.bn_stats(out=lnst[:, 6:12], in_=src_y[:, 512:1024]), [yb], [lnB])
            P.emit("vector", lambda e: e.bn_aggr(out=mv, in_=st), [lnB], [lnB])
            P.ts("vector", veps, mv[:, 1:2], EPS, None, ALU.add, None, [lnB], [lnB])
            P.tt("gpsimd", rstd, veps, mhalf[:], ALU.pow, [lnB, consts], [lnB])
            P.stt("vector", nmr, mv[:, 0:1], -1.0, rstd, ALU.mult, ALU.mult, [lnB], [lnB])
            P.act(src_y[:], src_y[:], AF.Identity, [yb, lnB], [yb], bias=nmr, scale=rstd)
            P.tt("gpsimd", src_y[:], src_y[:], gam, ALU.mult, [yb, gB], [yb])
            P.tt("gpsimd", src_y[:], src_y[:], bet, ALU.add, [yb, gB], [yb])

        full = lambda H: (lambda t: t[:, 0:H * 64].rearrange("p (h two d) -> p h two d", two=2, d=32))

        def S1a(b, i):
            k2 = i % 2
            k3 = (b * NT + i) % 3
            mb = b % 2
            if i == 0:
                mrow = modscr_d[b]
                P.dma("sync", sc1[mb][:], mrow[1 * D:2 * D].rearrange("(k p) -> p k", p=128), [modscr], [modB[mb]],
                      slow=True)
                P.dma("sync", sh1[mb][:], mrow[0 * D:1 * D].rearrange("(k p) -> p k", p=128), [modscr], [modB[mb]],
                      slow=True)
                P.dma("sync", gate1[mb][:], modscr_d[b:b + 1, 2 * D:3 * D].partition_broadcast(128), [modscr],
                      [modB[mb]])
                P.ts("vector", sc1[mb][:], sc1[mb][:], 1.0, None, ALU.add, None, [modB[mb]], [modB[mb]])
            P.dma("sync", xs[k3][:], x_d[b, i * 128:(i + 1) * 128, :], (), [xsB[k3]])
            for half in range(2):
                ps, psb = bank()
                for cc in range(4):
                    kc = half * 4 + cc
                    P.tr(ps[:, cc * 128:(cc + 1) * 128], xs[k3][:, kc * 128:(kc + 1) * 128], ident_f[:],
                         [xsB[k3], consts], [psb], sig=(cc == 3))
                for cc in range(4):
                    kc = half * 4 + cc
                    P.act(hT[k2][:, kc, :], ps[:, cc * 128:(cc + 1) * 128], AF.Identity, [psb, modB[mb]], [hTB[k2][kc]],
                          bias=sh1[mb][:, kc:kc + 1], scale=sc1[mb][:, kc:kc + 1])

            def proj(c0, c1):
                ps, psb = bank()
                for kc in range(KC):
                    P.mm(ps[:, 0:c1 - c0], hT[k2][:, kc, :], Win[:, kc, c0:c1], kc == 0, kc == KC - 1,
                         [hTB[k2][kc], WinB[kc]], [psb], sig=(kc == KC - 1))
                return ps, psb

            ps, psb = proj(512, 1024)
            rope(ps[:, 0:512], psb, 8, i, [(full(8), ktok[:].rearrange("p (h two d) -> p h two d", two=2, d=32), ktokB)])
            ps, psb = proj(0, 512)
            qd = []
            for par in range(2):
                qd.append(((lambda par: (lambda t: t[:].rearrange("p (c b d) -> p c b d", c=4, b=2)[:, :, par, :]))(par),
                           qtok[:, :, par * 128:par * 128 + 64], qtokB))
            rope(ps[:, 0:512], psb, 8, i, qd)
            ps, psb = proj(2048, 2560)
            rope(ps[:, 0:512], psb, 8, i, [(full(8), iqtok[:].rearrange("p (h two d) -> p h two d", two=2, d=32), iqtokB)])
            ps, psb = proj(2560, 2632)
            P.cp("vector", w_t[k2][:], ps[:, 64:72], [psb], [wtB[k2]])
            if 128 * (i + 1) > KSEL:
                for h in range(8):
                    P.ts("vector", dg[:, h, :], ident_b[:], w_t[k2][:, h:h + 1], None, ALU.mult, None,
                         [consts, wtB[k2]], [dgB])
            rope(ps[:, 0:64], psb, 1, i,
                 [(full(1), iktok[:, 64:128].rearrange("p (h two d) -> p h two d", two=2, d=32), iktokB)])
            ps, psb = proj(1024, 1536)
            P.act(V[:, i, :, 0:64], ps[:, 0:512].rearrange("p (h d) -> p h d", d=64), AF.Copy, [psb], [VB[i]])
            ps, psb = proj(1536, 2048)
            P.act(ubuf[k2][:], ps[:, 0:512], AF.Copy, [psb], [uB[k2]])

        def S1b(b, i):
            k2 = i % 2
            n = 128 * (i + 1)
            ps, psb = bank()
            for c in range(4):
                P.tr(ps[:, c * 128:(c + 1) * 128], iqtok[:, c * 128:(c + 1) * 128], ident_f[:], [iqtokB, consts], [psb],
                     sig=(c == 3))
            P.cp("vector", iqT[k2][:], ps[:, 0:512].rearrange("p (c t) -> p c t", c=4), [psb], [iqTB[k2]])
            ps, psb = bank()
            P.tr(ps[:, 0:128], iktok[:, 64:192], ident_f[:], [iktokB, consts], [psb], sig=False)
            P.tr(ps[:, 128:256], iktok[:, 0:128], ident_f[:], [iktokB, consts], [psb], sig=True)
            P.cp("vector", ikzA[:, i * 128:(i + 1) * 128], ps[:, 0:128], [psb], [ikzBf[i]])
            P.cp("vector", ikzB[:, i * 128:(i + 1) * 128], ps[:, 128:256], [psb], [ikzBf[i]])
            if n > KSEL:
                for blk in range((n + 511) // 512):
                    c0 = blk * 512
                    c1 = min(n, c0 + 512)
                    wd = c1 - c0
                    jt = list(range(c0 // 128, c1 // 128))
                    psS, psSb = banks[5], bankb[5]
                    def logit(h):
                        ikz = ikzA if h % 2 == 0 else ikzB
                        psL, psLb = bank()
                        P.mm(psL[:, 0:wd], iqT[k2][:, h // 2, :], ikz[:, c0:c1], True, True,
                             [iqTB[k2]] + [ikzBf[j] for j in jt], [psLb])
                        r = R_i["i"] % 4
                        R_i["i"] += 1
                        P.act(Rr[r][:, 0:wd], psL[:, 0:wd], AF.Relu, [psLb], [RB[r]])
                        return r

                    rl = [logit(0), logit(1)]
                    for h in range(8):
                        if h + 2 < 8:
                            rl.append(logit(h + 2))
                        r = rl[h]
                        P.mm(psS[:, 0:wd], dg[:, h, :], Rr[r][:, 0:wd], h == 0, h == 7, [dgB, RB[r]], [psSb],
                             sig=(h == 7))
                    P.cp("scalar", score[:, c0:c1], psS[:, 0:wd], [psSb], [scoreB])


        def S1c(b, i):
            k2 = i % 2
            n = 128 * (i + 1)
            ps, psb = bank()
            for c in range(4):
                P.tr(ps[:, c * 128:(c + 1) * 128], ktok[:, c * 128:(c + 1) * 128], ident_f[:], [ktokB, consts], [psb],
                     sig=(c == 3))
            P.cp("scalar", kT[:, :, i * 128:(i + 1) * 128], ps[:, 0:512].rearrange("p (c t) -> p c t", c=4),
                 [psb], [kTB[i]])
            for hp in range(2):
                ps, psb = bank()
                for cc in range(2):
                    c = hp * 2 + cc
                    P.tr(ps[:, (2 * cc) * 128:(2 * cc + 1) * 128], qtok[:, c, 0:128], ident_f[:], [qtokB, consts], [psb],
                         sig=False)
                    P.tr(ps[:, (2 * cc + 1) * 128:(2 * cc + 2) * 128], qtok[:, c, 64:192], ident_f[:], [qtokB, consts],
                         [psb], sig=(cc == 1))
                P.cp("scalar", qz[k2][:, hp * 4:(hp + 1) * 4, :], ps[:, 0:512].rearrange("p (h t) -> p h t", h=4),
                     [psb], [qzB[k2]])
            ucur, ucurB = ubuf[k2], uB[k2]
            ps, psb = bank()
            for g in range(4):
                if i == 0:
                    P.mm(ps[:, g * 128:(g + 1) * 128], ucur[:, g * 128:(g + 1) * 128], bands[:, g * 3 + 2, :], True, True,
                         [ucurB, cA], [psb], sig=(g == 3))
                else:
                    uprev, uprevB = ubuf[1 - k2], uB[1 - k2]
                    P.mm(ps[:, g * 128:(g + 1) * 128], ucur[:, g * 128:(g + 1) * 128], bands[:, g * 3 + 0, :], True, False,
                         [ucurB, cA], [psb], sig=False)
                    P.mm(ps[:, g * 128:(g + 1) * 128], uprev[:, g * 128:(g + 1) * 128], bands[:, g * 3 + 1, :], False, True,
                         [uprevB, cA], [psb], sig=(g == 3))
            P.cp("scalar", pooledT[:], ps[:, 0:512], [psb], [pooledB])
            ps, psb = bank()
            for g in range(4):
                P.mm(ps[:, g * 128:(g + 1) * 128], Wpool[:, g, :], pooledT[:, g * 128:(g + 1) * 128], True, True,
                     [WpoolB, pooledB], [psb], sig=(g == 3))
            P.cp("scalar", catT[k2][:, 4:8, :], ps[:, 0:512].rearrange("p (g t) -> p g t", g=4), [psb], [catB[k2]])
        def S2(b, i):
            n = 128 * (i + 1)
            if n > KSEL:
                P.emit("vector", (lambda n=n: (lambda e: e.reduce_max(out=A_ap, in_=score[:, 0:n],
                                                                      axis=mybir.AxisListType.X,
                                                                      apply_absolute_value=True)))(),
                       [scoreB], [bisB])
                P.tt("vector", score[:, i * 128:n], score[:, i * 128:n], cb_f[:], ALU.add, [scoreB, consts], [scoreB])
                P.ts("vector", bis[:, 0:2 * NITER], pw_t[:], A_ap, None, ALU.mult, None, [bisB, cA], [bisB])
                P.memset("vector", tau, 0.0, [bisB])
                for k in range(NITER):
                    P.ts("vector", MB[:, 0:n], score[:, 0:n], tau, None, ALU.is_ge, ALU.add, [scoreB, bisB], [MBB, bisB],
                         accum=cnt)
                    P.stt("vector", dcol, cnt, float(KSEL) - 0.5, steps2[:, k:k + 1], ALU.is_ge, ALU.mult, [bisB], [bisB])
                    P.stt("vector", tau, dcol, steps[:, k:k + 1], tau, ALU.subtract, ALU.add, [bisB], [bisB])
                P.ts("vector", MB[:, 0:n], score[:, 0:n], tau, -BIG, ALU.is_lt, ALU.mult, [scoreB, bisB], [MBB])
            else:
                if i > 0:
                    P.memset("vector", MB[:, 0:i * 128], 0.0, [MBB])
                P.cp("vector", MB[:, i * 128:n], cb_b[:], [consts], [MBB])

        def S3a(b, i):
            k2 = i % 2
            psO = [banks[6], banks[7]]
            psOb = [bankb[6], bankb[7]]
            units = [(j, half) for j in range(i + 1) for half in range(2)]

            def scores(j, half):
                psS, psSb = bank()
                P.mm(psS[:, 0:512], MB[:, j * 128:(j + 1) * 128], identx4[:], True, False, [MBB, consts], [psSb],
                     sig=False)
                for hh in range(4):
                    h = half * 4 + hh
                    P.mm(psS[:, hh * 128:(hh + 1) * 128], kT[:, h // 2, j * 128:(j + 1) * 128], qz[k2][:, h, :],
                         False, True, [kTB[j], qzB[k2]], [psSb], sig=(hh == 3))
                r = PT_i["i"] % 4
                PT_i["i"] += 1
                P.act(PT[r][:], psS[:, 0:512], AF.Exp, [psSb], [PTB[r]], scale=0.125)
                return r

            def pv(j, half, r):
                for hh in range(4):
                    h = half * 4 + hh
                    P.mm(psO[half][:, hh * 128:hh * 128 + 65], PT[r][:, hh * 128:(hh + 1) * 128], V[:, j, h, :],
                         (j == 0 and hh == 0), j == i, [PTB[r], VB[j]], [psOb[half]], sig=(hh == 3))

            LOOK = 2
            rq = []
            for u in range(min(LOOK, len(units))):
                rq.append(scores(*units[u]))
            for u in range(len(units)):
                if u + LOOK < len(units):
                    rq.append(scores(*units[u + LOOK]))
                pv(units[u][0], units[u][1], rq[u])

        def S3b1(b, i):
            k2 = i % 2
            psO = [banks[6], banks[7]]
            psOb = [bankb[6], bankb[7]]
            for half in range(2):
                pv = psO[half][:, 0:512].rearrange("p (h t) -> p h t", h=4)
                P.emit("vector", (lambda pv=pv, half=half: (lambda e: e.reciprocal(out=rs[:, half * 4:(half + 1) * 4],
                                                                                   in_=pv[:, :, 64])))(),
                       [psOb[half]], [rsB])
                P.tt("vector", obf[:].rearrange("p (h d) -> p h d", d=64)[:, half * 4:(half + 1) * 4, :],
                     pv[:, :, 0:64], rs[:, half * 4:(half + 1) * 4].unsqueeze(2).broadcast_to([128, 4, 64]),
                     ALU.mult, [psOb[half], rsB], [obfB])

        def S3b1b(b, i):
            k2 = i % 2
            ps, psb = bank()
            for c in range(4):
                P.tr(ps[:, c * 128:(c + 1) * 128], obf[:, c * 128:(c + 1) * 128], ident_f[:], [obfB, consts], [psb],
                     sig=(c == 3))
            P.cp("scalar", catT[k2][:, 0:4, :], ps[:, 0:512].rearrange("p (c t) -> p c t", c=4), [psb], [catB[k2]])
            yk, ykB = ybuf[k2], yB[k2]
            for half in range(2):
                ps, psb = bank()
                for kc in range(KC):
                    P.mm(ps[:, 0:512], catT[k2][:, kc, :], Wo[:, kc, half * 512:(half + 1) * 512], kc == 0, kc == KC - 1,
                         [catB[k2], WoB[kc]], [psb], sig=(kc == KC - 1))
                P.cp("scalar", yk[:, half * 512:(half + 1) * 512], ps[:, 0:512], [psb], [ykB])

        def S3b2(b, i):
            k2 = i % 2
            k3 = (b * NT + i) % 3
            mb = b % 2
            yk, ykB = ybuf[k2], yB[k2]
            P.tt("gpsimd", yk[:], yk[:], gate1[mb][:], ALU.mult, [ykB, modB[mb]], [ykB])
            P.stt("vector", yk[:], xs[k3][:], ALPHA, yk[:], ALU.mult, ALU.add, [xsB[k3], ykB], [ykB])
            layer_norm(yk, ykB, ln1g[:], ln1b[:], cA)
            P.dma("sync", x1scr_d[(b * NT + i) * 128:(b * NT + i + 1) * 128, :], yk[:], [ykB], [x1B[b][i]])

        tiles = [(b, i) for b in range(NB) for i in range(NT)]
        S1a(*tiles[0])
        stop_at(32)
        S1b(*tiles[0])
        stop_at(33)
        S2(*tiles[0])
        S1c(*tiles[0])
        stop_at(34)
        for g in range(len(tiles)):
            nxt = tiles[g + 1] if g + 1 < len(tiles) else None
            if nxt and nxt[1] == 0:
                if g > 0:
                    S3b2(*tiles[g - 1])
                S3a(*tiles[g])
                S3b1(*tiles[g])
                S1a(*nxt)
                S1b(*nxt)
                S2(*nxt)
                S1c(*nxt)
                S3b1b(*tiles[g])
                continue
            if nxt:
                S1a(*nxt)
            S3a(*tiles[g])
            S3b1(*tiles[g])
            if g > 0:
                S3b2(*tiles[g - 1])
            if nxt:
                S1b(*nxt)
                S2(*nxt)
                S1c(*nxt)
            S3b1b(*tiles[g])
        S3b2(*tiles[-1])

        stop_at(20)
        P.barrier()
        sb.reset(mA)

        Wg = sb.alloc([128, KC, DFF], BF16, "Wg")
        Wu = sb.alloc([128, KC, DFF], BF16, "Wu")
        Wd = sb.alloc([128, FC, D], BF16, "Wd")
        ln2g = sb.alloc([128, D], F32, "ln2g")
        ln2b = sb.alloc([128, D], F32, "ln2b")
        gate2 = [sb.alloc([128, D], F32, "gate2") for _ in range(2)]
        sc2 = [sb.alloc([128, KC], F32, "sc2") for _ in range(2)]
        sh2 = [sb.alloc([128, KC], F32, "sh2") for _ in range(2)]
        x1s = [sb.alloc([128, D], F32, "x1s") for _ in range(4)]
        h2T = [sb.alloc([128, KC, 256], BF16, "h2T") for _ in range(2)]
        aT = sb.alloc([128, FC, 256], BF16, "aT")
        sgt = [sb.alloc([128, 256], F32, "sgt") for _ in range(3)]
        y2 = [sb.alloc([128, D], F32, "y2") for _ in range(2)]
        lnst2 = sb.alloc([128, 24], F32, "lnst2")
        WgB = [Buf() for _ in range(KC)]
        WuB = [Buf() for _ in range(KC)]
        WdB = [Buf() for _ in range(FC)]
        cB, modB2 = Buf(), [Buf(), Buf()]
        x1sB = [Buf() for _ in range(4)]
        h2TB = [Buf(), Buf()]
        aTB = Buf()
        sgtB = [Buf() for _ in range(3)]
        y2B = [Buf(), Buf()]
        lnB2 = Buf()
        wg_v = wg_d.rearrange("(k p) c -> p k c", p=128)
        wu_v = wu_d.rearrange("(k p) c -> p k c", p=128)
        wd_v = wd_d.rearrange("(f p) c -> p f c", p=128)
        for kc in range(KC):
            P.dma("gpsimd", Wg[:, kc, :], wg_v[:, kc, :], (), [WgB[kc]])
            P.dma("gpsimd", Wu[:, kc, :], wu_v[:, kc, :], (), [WuB[kc]])
        for fc in range(FC):
            P.dma("gpsimd", Wd[:, fc, :], wd_v[:, fc, :], (), [WdB[fc]])
        P.dma("sync", ln2g[:], ln2g_d[0:1, :].partition_broadcast(128), (), [cB])
        P.dma("sync", ln2b[:], ln2b_d[0:1, :].partition_broadcast(128), (), [cB])

        def layer_norm2(src_y, yb, gam, bet, gB):
            st = lnst2[:, 0:12]
            mv = lnst2[:, 12:14]
            veps = lnst2[:, 14:15]
            rstd = lnst2[:, 15:16]
            nmr = lnst2[:, 16:17]
            P.emit("vector", lambda e: e.bn_stats(out=lnst2[:, 0:6], in_=src_y[:, 0:512]), [yb], [lnB2])
            P.emit("vector", lambda e: e.bn_stats(out=lnst2[:, 6:12], in_=src_y[:, 512:1024]), [yb], [lnB2])
            P.emit("vector", lambda e: e.bn_aggr(out=mv, in_=st), [lnB2], [lnB2])
            P.ts("vector", veps, mv[:, 1:2], EPS, None, ALU.add, None, [lnB2], [lnB2])
            P.tt("gpsimd", rstd, veps, mhalf[:], ALU.pow, [lnB2, consts], [lnB2])
            P.stt("vector", nmr, mv[:, 0:1], -1.0, rstd, ALU.mult, ALU.mult, [lnB2], [lnB2])
            P.act(src_y[:], src_y[:], AF.Identity, [yb, lnB2], [yb], bias=nmr, scale=rstd)
            P.tt("gpsimd", src_y[:], src_y[:], gam, ALU.mult, [yb, gB], [yb])
            P.tt("gpsimd", src_y[:], src_y[:], bet, ALU.add, [yb, gB], [yb])

        x1_i = {"i": 0}
        y2_i = {"i": 0}
        sg_i = {"i": 0}
        NG = NT // 2
        groups = [(b, g) for b in range(NB) for g in range(NG)]
        xts = {}

        def prep(gi):
            b, g = groups[gi]
            hk = gi % 2
            mb = b % 2
            if g == 0:
                mrow = modscr_d[b]
                P.dma("sync", sc2[mb][:], mrow[4 * D:5 * D].rearrange("(k p) -> p k", p=128), [modscr], [modB2[mb]],
                      slow=True)
                P.dma("sync", sh2[mb][:], mrow[3 * D:4 * D].rearrange("(k p) -> p k", p=128), [modscr], [modB2[mb]],
                      slow=True)
                P.dma("sync", gate2[mb][:], modscr_d[b:b + 1, 5 * D:6 * D].partition_broadcast(128), [modscr],
                      [modB2[mb]])
                P.ts("vector", sc2[mb][:], sc2[mb][:], 1.0, None, ALU.add, None, [modB2[mb]], [modB2[mb]])
            xt = []
            for tl in range(2):
                i = 2 * g + tl
                xk = x1_i["i"] % 4
                x1_i["i"] += 1
                xt.append(xk)
                P.dma("sync", x1s[xk][:], x1scr_d[(b * NT + i) * 128:(b * NT + i + 1) * 128, :], [x1B[b][i]], [x1sB[xk]])
                for half in range(2):
                    ps, psb = bank(0, 8)
                    for cc in range(4):
                        kc = half * 4 + cc
                        P.tr(ps[:, cc * 128:(cc + 1) * 128], x1s[xk][:, kc * 128:(kc + 1) * 128], ident_f[:],
                             [x1sB[xk], consts], [psb], sig=(cc == 3))
                    for cc in range(4):
                        kc = half * 4 + cc
                        P.act(h2T[hk][:, kc, tl * 128:(tl + 1) * 128], ps[:, cc * 128:(cc + 1) * 128], AF.Identity,
                              [psb, modB2[mb]], [h2TB[hk]], bias=sh2[mb][:, kc:kc + 1], scale=sc2[mb][:, kc:kc + 1])
            xts[gi] = xt

        def gateup(gi):
            hk = gi % 2
            for fc in range(FC):
                ps, psb = bank(0, 8)
                for kc in range(KC):
                    P.mm(ps[:, 0:256], Wg[:, kc, fc * 128:(fc + 1) * 128], h2T[hk][:, kc, :], kc == 0, kc == KC - 1,
                         [WgB[kc], h2TB[hk]], [psb], sig=False)
                for kc in range(KC):
                    P.mm(ps[:, 256:512], Wu[:, kc, fc * 128:(fc + 1) * 128], h2T[hk][:, kc, :], kc == 0, kc == KC - 1,
                         [WuB[kc], h2TB[hk]], [psb], sig=(kc == KC - 1))
                sk = sg_i["i"] % 3
                sg_i["i"] += 1
                P.act(sgt[sk][:], ps[:, 0:256], AF.Silu, [psb], [sgtB[sk]])
                P.tt("vector", aT[:, fc, :], sgt[sk][:], ps[:, 256:512], ALU.mult, [sgtB[sk], psb], [aTB])

        def down(gi):
            b, g = groups[gi]
            mb = b % 2
            xt = xts[gi]
            for tl in range(2):
                i = 2 * g + tl
                yk = y2_i["i"] % 2
                y2_i["i"] += 1
                for half in range(2):
                    ps, psb = bank(0, 8)
                    for fc in range(FC):
                        P.mm(ps[:, 0:512], aT[:, fc, tl * 128:(tl + 1) * 128], Wd[:, fc, half * 512:(half + 1) * 512],
                             fc == 0, fc == FC - 1, [aTB, WdB[fc]], [psb], sig=(fc == FC - 1))
                    P.tt("vector", y2[yk][:, half * 512:(half + 1) * 512], ps[:, 0:512],
                         gate2[mb][:, half * 512:(half + 1) * 512], ALU.mult, [psb, modB2[mb]], [y2B[yk]])
                P.stt("vector", y2[yk][:], x1s[xt[tl]][:], ALPHA, y2[yk][:], ALU.mult, ALU.add,
                      [x1sB[xt[tl]], y2B[yk]], [y2B[yk]])
                layer_norm2(y2[yk], y2B[yk], ln2g[:], ln2b[:], cB)
                P.dma("sync", y_d[b, i * 128:(i + 1) * 128, :], y2[yk][:], [y2B[yk]], [Buf()])

        prep(0)
        for gi in range(len(groups)):
            gateup(gi)
            if gi + 1 < len(groups):
                prep(gi + 1)
            down(gi)

    except StopBuild:
        pass
    P.finish()
    with nc.Block() as block:
        P.replay(block)
    for c in reversed(sem_ctx):
        c.__exit__(None, None, None)
    return nc, P, sb


def host_consts(S, NITER):
    inv_freq = (10000.0 ** (-np.arange(0, 64, 2, dtype=np.float32) / np.float32(64))).astype(np.float32)
    ang = (np.arange(S, dtype=np.float32)[:, None] * inv_freq[None, :]).astype(np.float32)
    cos = np.cos(ang).astype(np.float32)
    sin = np.sin(ang).astype(np.float32)
    ident = np.eye(128, dtype=np.float32)
    t = np.arange(128)
    cb = np.where(t[None, :] <= t[:, None], 0.0, -BIG).astype(np.float32)
    bands = np.zeros((12, 128, 128), np.float32)
    tp = t[:, None]
    tq = t[None, :]
    for g, w in enumerate(POOL_WINDOWS):
        bd = ((tp >= tq - w + 1) & (tp <= tq)).astype(np.float32) / w - (tp == tq)
        bp = (tp >= tq + 129 - w).astype(np.float32) / w
        cntf = np.minimum(tq + 1, w).astype(np.float32)
        bdf = ((tp >= np.maximum(tq - w + 1, 0)) & (tp <= tq)).astype(np.float32) / cntf - (tp == tq)
        bands[g * 3 + 0] = bd
        bands[g * 3 + 1] = bp
        bands[g * 3 + 2] = bdf
    pw = np.zeros((128, 2 * NITER), np.float32)
    for k in range(NITER):
        pw[:, k] = 2.0 ** -(k + 1)
        pw[:, NITER + k] = 2.0 ** -k
    return {"k_cos": cos, "k_sin": sin, "k_nsin": -sin, "k_ident": ident, "k_cb": cb, "k_bands": bands, "k_pw": pw}


_CACHE = {}


def run(inputs, n_cores, NB, S, KSEL, NITER):
    key = (NB, S, KSEL, NITER)
    if key not in _CACHE:
        _CACHE[key] = build_program(NB, S, KSEL, NITER)[0]
    nc = _CACHE[key]
    f = lambda a: np.ascontiguousarray(np.asarray(a, dtype=np.float32))
    shared = {
        "w_mod": f(inputs["w_mod"][0]), "b_mod": f(inputs["b_mod"][0]).reshape(1, -1), "w_in": f(inputs["w_in"][0]),
        "w_pool": f(inputs["w_pool"][0]), "pool_scale": f(inputs["pool_scale"][0]), "w_o": f(inputs["w_o"][0]),
        "ln1_g": f(inputs["ln1_g"][0]).reshape(1, -1), "ln1_b": f(inputs["ln1_b"][0]).reshape(1, -1),
        "w_gate": f(inputs["w_gate"][0]), "w_up": f(inputs["w_up"][0]), "w_down": f(inputs["w_down"][0]),
        "ln2_g": f(inputs["ln2_g"][0]).reshape(1, -1), "ln2_b": f(inputs["ln2_b"][0]).reshape(1, -1),
    }
    shared.update(host_consts(S, NITER))
    x = f(inputs["x"])
    c = f(inputs["c"])
    in_maps = []
    for k in range(n_cores):
        m = dict(shared)
        m["x"] = np.ascontiguousarray(x[k * NB:(k + 1) * NB])
        m["c"] = np.ascontiguousarray(c[k * NB:(k + 1) * NB])
        in_maps.append(m)
    res = run_bass_kernel_spmd(nc, in_maps, core_ids=list(range(n_cores)))
    return np.concatenate([np.asarray(r["y"]) for r in res.results], axis=0).astype(np.float32)


def kernel(x, c, w_mod, b_mod, w_in, w_pool, pool_scale, w_o, ln1_g, ln1_b, w_gate, w_up, w_down, ln2_g, ln2_b):
    inputs = dict(x=x, c=c, w_mod=w_mod, b_mod=b_mod, w_in=w_in, w_pool=w_pool, pool_scale=pool_scale, w_o=w_o,
                  ln1_g=ln1_g, ln1_b=ln1_b, w_gate=w_gate, w_up=w_up, w_down=w_down, ln2_g=ln2_g, ln2_b=ln2_b)
    return run(inputs, N_CORES, 4, 2048, 256, 10)
```

```python
import numpy as np
import concourse.bass as bass
import concourse.mybir as mybir
from concourse.bass_utils import run_bass_kernel_spmd

F32 = mybir.dt.float32
BF16 = mybir.dt.bfloat16
U8 = mybir.dt.uint8
AF = mybir.ActivationFunctionType
ALU = mybir.AluOpType

D = 1024
KC = 8
DFF = 2816
FC = 22
WIN = 2632
NMOD = 6
ALPHA = float(2.0 ** 0.25)
EPS = 1e-5
BIG = 32768.0
N_CORES = 8
POOL_WINDOWS = (2, 4, 8, 16)
SAME_WAIT = True
NO_SELF_WAIT = ("tensor",)
STOP = 99
DEBUG = 0


class StopBuild(Exception):
    pass


def stop_at(k):
    if STOP == k:
        raise StopBuild()


class Tok:
    __slots__ = ("sem", "val")

    def __init__(self, sem, val):
        self.sem = sem
        self.val = val


class Sem:
    def __init__(self, h):
        self.h = h
        self.count = 0
        self.last = None


class Buf:
    __slots__ = ("name", "w", "r")

    def __init__(self, name=""):
        self.name = name
        self.w = {}
        self.r = {}


class Eng:
    def __init__(self, name, sem):
        self.name = name
        self.sem = sem
        self.ops = []
        self.seen = {}
        self.pending = []
        self.dma = []
        self.dma_i = 0


class Prog:
    def __init__(self, nc, sems):
        self.nc = nc
        it = iter(sems)
        self.eng = {}
        for n in ("tensor", "vector", "scalar", "gpsimd", "sync"):
            self.eng[n] = Eng(n, Sem(next(it)))
        self.eng["sync"].dma = [Sem(next(it)) for _ in range(40)]
        self.eng["gpsimd"].dma = [Sem(next(it)) for _ in range(16)]
        self.n_ins = 0

    def emit(self, eng, fn, reads=(), writes=(), sig=True, dma=False):
        E = self.eng[eng]
        need = {}

        def add(tok):
            if tok.sem is E.sem:
                if eng in NO_SELF_WAIT or not SAME_WAIT:
                    return
            assert tok.val is not None, f"unresolved token needed by {eng}"
            cur = need.get(tok.sem)
            if cur is None or tok.val > cur:
                need[tok.sem] = tok.val

        for b in reads:
            for t in b.w.values():
                add(t)
        for b in writes:
            for t in b.w.values():
                add(t)
            for t in b.r.values():
                add(t)
        if dma:
            s = E.dma[E.dma_i % len(E.dma)]
            E.dma_i += 1
            if s.last is not None:
                add(s.last)
        for sem, val in need.items():
            if E.seen.get(sem, 0) < val:
                E.ops.append(("wait", sem.h, val))
                E.seen[sem] = val
        if dma:
            s.count += 16
            tok = Tok(s, s.count)
            s.last = tok
            E.ops.append(("ins", fn, s.h, 16))
        elif sig:
            E.sem.count += 1
            tok = Tok(E.sem, E.sem.count)
            for p in E.pending:
                p.val = tok.val
            E.pending = []
            E.ops.append(("ins", fn, E.sem.h, 1))
        else:
            tok = Tok(E.sem, None)
            E.pending.append(tok)
            E.ops.append(("ins", fn, None, 0))
        for b in writes:
            b.w = {tok.sem: tok}
            b.r = {}
        for b in reads:
            if b not in writes:
                b.r[tok.sem] = tok
        self.n_ins += 1
        return tok

    def barrier(self):
        toks = []
        for E in self.eng.values():
            assert not E.pending
            if E.sem.count:
                toks.append((E.sem, E.sem.count))
            for s in E.dma:
                if s.last is not None:
                    toks.append((s, s.last.val))
        for E in self.eng.values():
            for sem, val in toks:
                if sem is E.sem:
                    continue
                if E.seen.get(sem, 0) < val:
                    E.ops.append(("wait", sem.h, val))
                    E.seen[sem] = val

    def finish(self):
        E = self.eng["sync"]
        for En in self.eng.values():
            for s in En.dma:
                if s.last is not None and E.seen.get(s, 0) < s.last.val:
                    E.ops.append(("wait", s.h, s.last.val))
                    E.seen[s] = s.last.val

    def replay(self, block):
        def mk(name):
            E = self.eng[name]

            def body(e):
                for op in E.ops:
                    if op[0] == "wait":
                        e.wait_ge(op[1], op[2])
                    else:
                        ins = op[1](e)
                        if op[2] is not None:
                            ins.then_inc(op[2], op[3])
            return body
        block.sync(mk("sync"))
        block.tensor(mk("tensor"))
        block.vector(mk("vector"))
        block.scalar(mk("scalar"))
        block.gpsimd(mk("gpsimd"))

    def mm(self, out, lhsT, rhs, start, stop, reads, writes, sig=True):
        return self.emit("tensor", lambda e: e.matmul(out, lhsT=lhsT, rhs=rhs, start=start, stop=stop),
                         reads, writes, sig=sig)

    def tr(self, out, in_, ident, reads, writes, sig=True):
        return self.emit("tensor", lambda e: e.transpose(out=out, in_=in_, identity=ident), reads, writes, sig=sig)

    def act(self, out, in_, func, reads, writes, bias=None, scale=None):
        kw = {}
        if bias is not None:
            kw["bias"] = bias
        if scale is not None:
            kw["scale"] = scale
        return self.emit("scalar", lambda e: e.activation(out=out, in_=in_, func=func, **kw), reads, writes)

    def ts(self, eng, out, in0, s1, s2, op0, op1, reads, writes, accum=None):
        kw = {}
        if op1 is not None:
            kw["op1"] = op1
        if accum is not None:
            kw["accum_out"] = accum
        return self.emit(eng, lambda e: e.tensor_scalar(out=out, in0=in0, scalar1=s1, scalar2=s2, op0=op0, **kw),
                         reads, writes)

    def tt(self, eng, out, in0, in1, op, reads, writes):
        return self.emit(eng, lambda e: e.tensor_tensor(out=out, in0=in0, in1=in1, op=op), reads, writes)

    def stt(self, eng, out, in0, scalar, in1, op0, op1, reads, writes):
        return self.emit(eng, lambda e: e.scalar_tensor_tensor(out=out, in0=in0, scalar=scalar, in1=in1,
                                                               op0=op0, op1=op1), reads, writes)

    def cp(self, eng, out, in_, reads, writes):
        if eng == "scalar":
            return self.emit(eng, lambda e: e.copy(out=out, in_=in_), reads, writes)
        return self.emit(eng, lambda e: e.tensor_copy(out=out, in_=in_), reads, writes)

    def memset(self, eng, ap, val, writes):
        return self.emit(eng, lambda e: e.memset(ap, val), (), writes)

    def dma(self, eng, out, in_, reads, writes, slow=False):
        if slow:
            return self.emit(eng, lambda e: e.dma_start(out=out, in_=in_, allow_slow_non_contiguous=True),
                             reads, writes, dma=True)
        return self.emit(eng, lambda e: e.dma_start(out=out, in_=in_), reads, writes, dma=True)


class SB:
    def __init__(self, nc, base, size):
        self.nc = nc
        self.base = base
        self.cur = base
        self.end = base + size
        self.k = 0
        self.peak = base

    def alloc(self, shape, dtype, name="t"):
        isz = 4 if dtype == F32 else 2
        n = 1
        for s in shape[1:]:
            n *= s
        off = (self.cur + 63) // 64 * 64
        self.k += 1
        h = self.nc.alloc_sbuf_tensor_at(f"{name}{self.k}", list(shape), dtype, offset=off)
        self.cur = off + n * isz
        assert self.cur <= self.end, f"SBUF overflow allocating {name}: {self.cur - self.base} > {self.end - self.base}"
        self.peak = max(self.peak, self.cur)
        return h

    def mark(self):
        return self.cur

    def reset(self, m):
        self.cur = m


class Ring:
    def __init__(self, items):
        self.items = items
        self.i = 0

    def next(self):
        it = self.items[self.i % len(self.items)]
        self.i += 1
        return it


def build_program(NB, S, KSEL, NITER):
    NT = S // 128
    nc = bass.Bass("TRN2", target_bir_lowering=False)

    def din(name, shape):
        return nc.dram_tensor(name, list(shape), F32, kind="ExternalInput").ap()

    x_d = din("x", [NB, S, D])
    c_d = din("c", [NB, D])
    wmod_d = din("w_mod", [D, NMOD * D])
    bmod_d = din("b_mod", [1, NMOD * D])
    win_d = din("w_in", [D, WIN])
    wpool_d = din("w_pool", [4, 128, 128])
    pscale_d = din("pool_scale", [512])
    wo_d = din("w_o", [D, D])
    ln1g_d = din("ln1_g", [1, D])
    ln1b_d = din("ln1_b", [1, D])
    wg_d = din("w_gate", [D, DFF])
    wu_d = din("w_up", [D, DFF])
    wd_d = din("w_down", [DFF, D])
    ln2g_d = din("ln2_g", [1, D])
    ln2b_d = din("ln2_b", [1, D])
    cos_d = din("k_cos", [S, 32])
    sin_d = din("k_sin", [S, 32])
    nsin_d = din("k_nsin", [S, 32])
    ident_d = din("k_ident", [128, 128])
    cb_d = din("k_cb", [128, 128])
    bands_d = din("k_bands", [12, 128, 128])
    pw_d = din("k_pw", [128, 2 * NITER])
    y_d = nc.dram_tensor("y", [NB, S, D], F32, kind="ExternalOutput").ap()
    if DEBUG:
        dbg_score = nc.dram_tensor("dbg_score", [128, S], F32, kind="ExternalOutput").ap()
        dbg_mb = nc.dram_tensor("dbg_mb", [128, S], F32, kind="ExternalOutput").ap()
        dbg_bis = nc.dram_tensor("dbg_bis", [128, 2 * NITER + 8], F32, kind="ExternalOutput").ap()
        dbg_w = nc.dram_tensor("dbg_w", [128, 8], F32, kind="ExternalOutput").ap()
    modscr_d = nc.dram_tensor("mod_scr", [NB, NMOD * D], F32).ap()
    x1scr_d = nc.dram_tensor("x1_scr", [NB * S, D], F32).ap()

    ARENA = 207 * 1024
    arena = nc.alloc_sbuf_tensor("arena", [128, ARENA], U8)
    base = nc.lookup_mloc(arena).addr
    sb = SB(nc, base, ARENA)
    banks = [nc.alloc_psum_tensor(f"bank{i}", [128, 512], F32) for i in range(8)]
    bankb = [Buf(f"bank{i}") for i in range(8)]

    sem_ctx = [nc.semaphore(f"s{i}") for i in range(5 + 40 + 16)]
    sems = [c.__enter__() for c in sem_ctx]
    P = Prog(nc, sems)

    try:
        ident_f = sb.alloc([128, 128], F32, "identf")
        ident_b = sb.alloc([128, 128], BF16, "identb")
        identx4 = sb.alloc([128, 512], BF16, "identx4")
        cb_f = sb.alloc([128, 128], F32, "cbf")
        cb_b = sb.alloc([128, 128], BF16, "cbb")
        mhalf = sb.alloc([128, 1], F32, "mhalf")
        consts = Buf("consts")
        P.dma("sync", ident_f[:], ident_d, (), [consts])
        P.dma("sync", cb_f[:], cb_d, (), [consts])
        P.cp("vector", ident_b[:], ident_f[:], [consts], [consts])
        for r in range(4):
            P.cp("vector", identx4[:, r * 128:(r + 1) * 128], ident_f[:], [consts], [consts])
        P.cp("vector", cb_b[:], cb_f[:], [consts], [consts])
        P.memset("vector", mhalf[:], -0.5, [consts])

        stop_at(0)
        rr = {"i": 0}

        def bank(lo=0, hi=5):
            k = lo + rr["i"] % (hi - lo)
            rr["i"] += 1
            return banks[k], bankb[k]

        m0 = sb.mark()
        cT = sb.alloc([128, KC, NB], F32, "cT")
        siluT = sb.alloc([128, KC, NB], BF16, "siluT")
        bmod_b = sb.alloc([128, NMOD * D], F32, "bmodb")
        modrow = sb.alloc([128, NMOD * D], F32, "modrow")
        wm = [sb.alloc([128, KC, 512], BF16, "wm") for _ in range(2)]
        cTb, bmb, mrb = Buf(), Buf(), Buf()
        wmb = [[Buf() for _ in range(KC)] for _ in range(2)]
        for b in range(NB):
            P.dma("sync", cT[:, :, b], c_d[b].rearrange("(k p) -> p k", p=128), (), [cTb], slow=True)
        P.dma("sync", bmod_b[0:NB, :], bmod_d[0:1, :].partition_broadcast(NB), (), [bmb])
        P.act(siluT[:], cT[:], AF.Silu, [cTb], [cTb])
        wmod_v = wmod_d.rearrange("(k p) c -> p k c", p=128)
        for blk in range(NMOD * 2):
            w = wm[blk % 2]
            wb = wmb[blk % 2]
            for kc in range(KC):
                P.dma("gpsimd", w[:, kc, :], wmod_v[:, kc, blk * 512:(blk + 1) * 512], (), [wb[kc]])
            ps, psb = bank()
            for kc in range(KC):
                P.mm(ps[0:NB, :], siluT[:, kc, :], w[:, kc, :], kc == 0, kc == KC - 1, [cTb, wb[kc]], [psb], sig=(kc == KC - 1))
            P.tt("vector", modrow[0:NB, blk * 512:(blk + 1) * 512], ps[0:NB, :], bmod_b[0:NB, blk * 512:(blk + 1) * 512],
                 ALU.add, [psb, bmb], [mrb])
        stop_at(1)
        modscr = Buf("modscr")
        P.dma("sync", modscr_d, modrow[0:NB, :], [mrb], [modscr])
        stop_at(2)
        P.barrier()
        sb.reset(m0)

        mA = sb.mark()
        Win = sb.alloc([128, KC, WIN], BF16, "Win")
        Wo = sb.alloc([128, KC, D], BF16, "Wo")
        Wpool = sb.alloc([128, 4, 128], BF16, "Wpool")
        bands = sb.alloc([128, 12, 128], BF16, "bands")
        cos_t = sb.alloc([128, NT, 32], F32, "cos")
        sin_t = sb.alloc([128, NT, 32], F32, "sin")
        nsin_t = sb.alloc([128, NT, 32], F32, "nsin")
        pw_t = sb.alloc([128, 2 * NITER], F32, "pw")
        ln1g = sb.alloc([128, D], F32, "ln1g")
        ln1b = sb.alloc([128, D], F32, "ln1b")
        gate1 = [sb.alloc([128, D], F32, "gate1") for _ in range(2)]
        sc1 = [sb.alloc([128, KC], F32, "sc1") for _ in range(2)]
        sh1 = [sb.alloc([128, KC], F32, "sh1") for _ in range(2)]
        pe = [sb.alloc([128, 512], F32, "pe") for _ in range(3)]
        kT = sb.alloc([128, 4, S], BF16, "kT")
        ikzA = sb.alloc([128, S], BF16, "ikzA")
        ikzB = sb.alloc([128, S], BF16, "ikzB")
        V = sb.alloc([128, NT, 8, 65], BF16, "V")
        xs = [sb.alloc([128, D], F32, "xs") for _ in range(3)]
        hT = [sb.alloc([128, KC, 128], BF16, "hT") for _ in range(2)]
        qtok = sb.alloc([128, 4, 192], F32, "qtok")
        ktok = sb.alloc([128, 512], F32, "ktok")
        iqtok = sb.alloc([128, 512], F32, "iqtok")
        iktok = sb.alloc([128, 192], F32, "iktok")
        rt1 = [sb.alloc([128, 512], F32, "rt1") for _ in range(1)]
        rt2 = [sb.alloc([128, 512], F32, "rt2") for _ in range(1)]
        qz = [sb.alloc([128, 8, 128], BF16, "qz") for _ in range(2)]
        iqT = [sb.alloc([128, 4, 128], BF16, "iqT") for _ in range(2)]
        w_t = [sb.alloc([128, 8], F32, "wt") for _ in range(2)]
        ubuf = [sb.alloc([128, 512], BF16, "u") for _ in range(2)]
        pooledT = sb.alloc([128, 512], BF16, "pooledT")
        catT = [sb.alloc([128, KC, 128], BF16, "catT") for _ in range(2)]
        obf = sb.alloc([128, 512], F32, "obf")
        rs = sb.alloc([128, 8], F32, "rs")
        score = sb.alloc([128, S], F32, "score")
        MB = sb.alloc([128, S], BF16, "MB")
        Rr = [sb.alloc([128, 512], BF16, "R") for _ in range(4)]
        dg = sb.alloc([128, 8, 128], BF16, "dg")
        PT = [sb.alloc([128, 512], BF16, "PT") for _ in range(4)]
        bis = sb.alloc([128, 2 * NITER + 8], F32, "bis")
        ybuf = [sb.alloc([128, D], F32, "y") for _ in range(2)]
        lnst = sb.alloc([128, 24], F32, "lnst")

        WinB = [Buf() for _ in range(KC)]
        WoB = [Buf() for _ in range(KC)]
        WpoolB, cA = Buf("Wpool"), Buf("constsA")
        win_v = win_d.rearrange("(k p) c -> p k c", p=128)
        wo_v = wo_d.rearrange("(k p) c -> p k c", p=128)
        for kc in range(KC):
            P.dma("gpsimd", Win[:, kc, :], win_v[:, kc, :], (), [WinB[kc]])
        for kc in range(KC):
            P.dma("gpsimd", Wo[:, kc, :], wo_v[:, kc, :], (), [WoB[kc]])
        P.dma("gpsimd", Wpool[:], wpool_d.rearrange("g c d -> c g d"), (), [WpoolB])
        P.dma("gpsimd", bands[:], bands_d.rearrange("n a b -> a n b"), (), [cA])
        P.dma("sync", cos_t[:], cos_d.rearrange("(i p) d -> p i d", p=128), (), [cA])
        P.dma("sync", sin_t[:], sin_d.rearrange("(i p) d -> p i d", p=128), (), [cA])
        P.dma("sync", nsin_t[:], nsin_d.rearrange("(i p) d -> p i d", p=128), (), [cA])
        P.dma("sync", pw_t[:], pw_d, (), [cA])
        P.dma("sync", ln1g[:], ln1g_d[0:1, :].partition_broadcast(128), (), [cA])
        P.dma("sync", ln1b[:], ln1b_d[0:1, :].partition_broadcast(128), (), [cA])
        qtokB, ktokB, iqtokB, iktokB = Buf("qtok"), Buf("ktok"), Buf("iqtok"), Buf("iktok")
        P.memset("vector", qtok[:], 0.0, [qtokB])
        P.memset("vector", iktok[:], 0.0, [iktokB])
        VB = [Buf(f"V{i}") for i in range(NT)]
        kTB = [Buf(f"kT{i}") for i in range(NT)]
        ikzBf = [Buf(f"ikz{i}") for i in range(NT)]
        for i in range(NT):
            P.memset("gpsimd", V[:, i, :, 64:65], 1.0, [VB[i]])

        stop_at(3)
        pse, pseB = rt1[0], Buf()
        P.dma("sync", pse[:], pscale_d.rearrange("(o n) -> o n", o=1).partition_broadcast(128), (), [pseB])
        P.tt("vector", Wpool[:].rearrange("c g d -> c (g d)"), Wpool[:].rearrange("c g d -> c (g d)"), pse[:], ALU.mult,
             [pseB, WpoolB], [WpoolB])
        stop_at(31)
        xsB = [Buf("xs0"), Buf("xs1"), Buf("xs2")]
        hTB = [[Buf() for _ in range(KC)] for _ in range(2)]
        rt1B = [Buf()]
        rt2B = [Buf()]
        rt1B[0] = pseB
        peB = [Buf() for _ in range(3)]
        rt_i = {"i": 0}
        pe_i = {"i": 0}
        qzB, iqTB, wtB = [Buf(), Buf()], [Buf(), Buf()], [Buf(), Buf()]
        uB = [Buf(), Buf()]
        pooledB = Buf()
        catB = [Buf(), Buf()]
        obfB, rsB, scoreB, MBB, dgB, bisB, lnB = Buf(), Buf(), Buf(), Buf(), Buf(), Buf(), Buf()
        RB = [Buf() for _ in range(4)]
        PTB = [Buf() for _ in range(4)]
        R_i = {"i": 0}
        PT_i = {"i": 0}
        yB = [Buf(), Buf()]
        modB = [Buf("modA0"), Buf("modA1")]
        x1B = [[Buf() for _ in range(NT)] for _ in range(NB)]

        steps = bis[:, 0:NITER]
        steps2 = bis[:, NITER:2 * NITER]
        A_ap = bis[:, 2 * NITER:2 * NITER + 1]
        tau = bis[:, 2 * NITER + 1:2 * NITER + 2]
        cnt = bis[:, 2 * NITER + 2:2 * NITER + 3]
        dcol = bis[:, 2 * NITER + 3:2 * NITER + 4]

        def rope(ps, psb, H, i, dests):
            pk = pe_i["i"] % 3
            pe_i["i"] += 1
            P.act(pe[pk][:, 0:H * 64], ps, AF.Copy, [psb], [peB[pk]])
            k = 0
            t1, t2 = rt1[k], rt2[k]
            X = pe[pk][:, 0:H * 64].rearrange("p (h two d) -> p h two d", two=2, d=32)
            t1v = t1[:, 0:H * 64].rearrange("p (h two d) -> p h two d", two=2, d=32)
            t2v = t2[:, 0:H * 64].rearrange("p (h two d) -> p h two d", two=2, d=32)
            cosb = cos_t[:, i, :].unsqueeze(1).unsqueeze(1).broadcast_to([128, H, 2, 32])
            sinb = sin_t[:, i, :].unsqueeze(1).broadcast_to([128, H, 32])
            nsinb = nsin_t[:, i, :].unsqueeze(1).broadcast_to([128, H, 32])
            P.tt("gpsimd", t1v, X, cosb, ALU.mult, [peB[pk], cA], [rt1B[k]])
            P.tt("gpsimd", t2v[:, :, 0, :], X[:, :, 1, :], nsinb, ALU.mult, [peB[pk], cA], [rt2B[k]])
            P.tt("gpsimd", t2v[:, :, 1, :], X[:, :, 0, :], sinb, ALU.mult, [peB[pk], cA], [rt2B[k]])
            for sl, dest, destB in dests:
                P.tt("gpsimd", dest, sl(t1), sl(t2), ALU.add, [rt1B[k], rt2B[k]], [destB])

        def layer_norm(src_y, yb, gam, bet, gB):
            st = lnst[:, 0:12]
            mv = lnst[:, 12:14]
            veps = lnst[:, 14:15]
            rstd = lnst[:, 15:16]
            nmr = lnst[:, 16:17]
            P.emit("vector", lambda e: e.bn_stats(out=lnst[:, 0:6], in_=src_y[:, 0:512]), [yb], [lnB])
            P.emit("vector", lambda e: e.bn_stats(out=lnst[:, 6:12], in_=src_y[:, 512:1024]), [yb], [lnB])
            P.emit("vector", lambda e: e.bn_aggr(out=mv, in_=st), [lnB], [lnB])
            P.ts("vector", veps, mv[:, 1:2], EPS, None, ALU.add, None, [lnB], [lnB])
            P.tt("gpsimd", rstd, veps, mhalf[:], ALU.pow, [lnB, consts], [lnB])
            P.stt("vector", nmr, mv[:, 0:1], -1.0, rstd, ALU.mult, ALU.mult, [lnB], [lnB])
            P.act(src_y[:], src_y[:], AF.Identity, [yb, lnB], [yb], bias=nmr, scale=rstd)
            P.tt("gpsimd", src_y[:], src_y[:], gam, ALU.mult, [yb, gB], [yb])
            P.tt("gpsimd", src_y[:], src_y[:], bet, ALU.add, [yb, gB], [yb])

        full = lambda H: (lambda t: t[:, 0:H * 64].rearrange("p (h two d) -> p h two d", two=2, d=32))

        def S1a(b, i):
            k2 = i % 2
            k3 = (b * NT + i) % 3
            mb = b % 2
            if i == 0:
                mrow = modscr_d[b]
                P.dma("sync", sc1[mb][:], mrow[1 * D:2 * D].rearrange("(k p) -> p k", p=128), [modscr], [modB[mb]],
                      slow=True)
                P.dma("sync", sh1[mb][:], mrow[0 * D:1 * D].rearrange("(k p) -> p k", p=128), [modscr], [modB[mb]],
                      slow=True)
                P.dma("sync", gate1[mb][:], modscr_d[b:b + 1, 2 * D:3 * D].partition_broadcast(128), [modscr],
                      [modB[mb]])
                P.ts("vector", sc1[mb][:], sc1[mb][:], 1.0, None, ALU.add, None, [modB[mb]], [modB[mb]])
            P.dma("sync", xs[k3][:], x_d[b, i * 128:(i + 1) * 128, :], (), [xsB[k3]])
            for half in range(2):
                ps, psb = bank()
                for cc in range(4):
                    kc = half * 4 + cc
                    P.tr(ps[:, cc * 128:(cc + 1) * 128], xs[k3][:, kc * 128:(kc + 1) * 128], ident_f[:],
                         [xsB[k3], consts], [psb], sig=(cc == 3))
                for cc in range(4):
                    kc = half * 4 + cc
                    P.act(hT[k2][:, kc, :], ps[:, cc * 128:(cc + 1) * 128], AF.Identity, [psb, modB[mb]], [hTB[k2][kc]],
                          bias=sh1[mb][:, kc:kc + 1], scale=sc1[mb][:, kc:kc + 1])

            def proj(c0, c1):
                ps, psb = bank()
                for kc in range(KC):
                    P.mm(ps[:, 0:c1 - c0], hT[k2][:, kc, :], Win[:, kc, c0:c1], kc == 0, kc == KC - 1,
                         [hTB[k2][kc], WinB[kc]], [psb], sig=(kc == KC - 1))
                return ps, psb

            ps, psb = proj(512, 1024)
            rope(ps[:, 0:512], psb, 8, i, [(full(8), ktok[:].rearrange("p (h two d) -> p h two d", two=2, d=32), ktokB)])
            ps, psb = proj(0, 512)
            qd = []
            for par in range(2):
                qd.append(((lambda par: (lambda t: t[:].rearrange("p (c b d) -> p c b d", c=4, b=2)[:, :, par, :]))(par),
                           qtok[:, :, par * 128:par * 128 + 64], qtokB))
            rope(ps[:, 0:512], psb, 8, i, qd)
            ps, psb = proj(2048, 2560)
            rope(ps[:, 0:512], psb, 8, i, [(full(8), iqtok[:].rearrange("p (h two d) -> p h two d", two=2, d=32), iqtokB)])
            ps, psb = proj(2560, 2632)
            P.cp("vector", w_t[k2][:], ps[:, 64:72], [psb], [wtB[k2]])
            if 128 * (i + 1) > KSEL:
                for h in range(8):
                    P.ts("vector", dg[:, h, :], ident_b[:], w_t[k2][:, h:h + 1], None, ALU.mult, None,
                         [consts, wtB[k2]], [dgB])
            rope(ps[:, 0:64], psb, 1, i,
                 [(full(1), iktok[:, 64:128].rearrange("p (h two d) -> p h two d", two=2, d=32), iktokB)])
            ps, psb = proj(1024, 1536)
            P.act(V[:, i, :, 0:64], ps[:, 0:512].rearrange("p (h d) -> p h d", d=64), AF.Copy, [psb], [VB[i]])
            ps, psb = proj(1536, 2048)
            P.act(ubuf[k2][:], ps[:, 0:512], AF.Copy, [psb], [uB[k2]])

        def S1b(b, i):
            k2 = i % 2
            n = 128 * (i + 1)
            ps, psb = bank()
            for c in range(4):
                P.tr(ps[:, c * 128:(c + 1) * 128], iqtok[:, c * 128:(c + 1) * 128], ident_f[:], [iqtokB, consts], [psb],
                     sig=(c == 3))
            P.cp("vector", iqT[k2][:], ps[:, 0:512].rearrange("p (c t) -> p c t", c=4), [psb], [iqTB[k2]])
            ps, psb = bank()
            P.tr(ps[:, 0:128], iktok[:, 64:192], ident_f[:], [iktokB, consts], [psb], sig=False)
            P.tr(ps[:, 128:256], iktok[:, 0:128], ident_f[:], [iktokB, consts], [psb], sig=True)
            P.cp("vector", ikzA[:, i * 128:(i + 1) * 128], ps[:, 0:128], [psb], [ikzBf[i]])
            P.cp("vector", ikzB[:, i * 128:(i + 1) * 128], ps[:, 128:256], [psb], [ikzBf[i]])
            if n > KSEL:
                for blk in range((n + 511) // 512):
                    c0 = blk * 512
                    c1 = min(n, c0 + 512)
                    wd = c1 - c0
                    jt = list(range(c0 // 128, c1 // 128))
                    psS, psSb = banks[5], bankb[5]
                    def logit(h):
                        ikz = ikzA if h % 2 == 0 else ikzB
                        psL, psLb = bank()
                        P.mm(psL[:, 0:wd], iqT[k2][:, h // 2, :], ikz[:, c0:c1], True, True,
                             [iqTB[k2]] + [ikzBf[j] for j in jt], [psLb])
                        r = R_i["i"] % 4
                        R_i["i"] += 1
                        if h % 2 == 0:
                            P.act(Rr[r][:, 0:wd], psL[:, 0:wd], AF.Relu, [psLb], [RB[r]])
                        else:
                            P.ts("vector", Rr[r][:, 0:wd], psL[:, 0:wd], 0.0, None, ALU.max, None, [psLb], [RB[r]])
                        return r

                    rl = [logit(0), logit(1)]
                    for h in range(8):
                        if h + 2 < 8:
                            rl.append(logit(h + 2))
                        r = rl[h]
                        P.mm(psS[:, 0:wd], dg[:, h, :], Rr[r][:, 0:wd], h == 0, h == 7, [dgB, RB[r]], [psSb],
                             sig=(h == 7))
                    P.cp("scalar", score[:, c0:c1], psS[:, 0:wd], [psSb], [scoreB])


        def S1c(b, i):
            k2 = i % 2
            n = 128 * (i + 1)
            ps, psb = bank()
            for c in range(4):
                P.tr(ps[:, c * 128:(c + 1) * 128], ktok[:, c * 128:(c + 1) * 128], ident_f[:], [ktokB, consts], [psb],
                     sig=(c == 3))
            P.cp("scalar", kT[:, :, i * 128:(i + 1) * 128], ps[:, 0:512].rearrange("p (c t) -> p c t", c=4),
                 [psb], [kTB[i]])
            for hp in range(2):
                ps, psb = bank()
                for cc in range(2):
                    c = hp * 2 + cc
                    P.tr(ps[:, (2 * cc) * 128:(2 * cc + 1) * 128], qtok[:, c, 0:128], ident_f[:], [qtokB, consts], [psb],
                         sig=False)
                    P.tr(ps[:, (2 * cc + 1) * 128:(2 * cc + 2) * 128], qtok[:, c, 64:192], ident_f[:], [qtokB, consts],
                         [psb], sig=(cc == 1))
                P.cp("scalar", qz[k2][:, hp * 4:(hp + 1) * 4, :], ps[:, 0:512].rearrange("p (h t) -> p h t", h=4),
                     [psb], [qzB[k2]])
            ucur, ucurB = ubuf[k2], uB[k2]
            ps, psb = bank()
            for g in range(4):
                if i == 0:
                    P.mm(ps[:, g * 128:(g + 1) * 128], ucur[:, g * 128:(g + 1) * 128], bands[:, g * 3 + 2, :], True, True,
                         [ucurB, cA], [psb], sig=(g == 3))
                else:
                    uprev, uprevB = ubuf[1 - k2], uB[1 - k2]
                    P.mm(ps[:, g * 128:(g + 1) * 128], ucur[:, g * 128:(g + 1) * 128], bands[:, g * 3 + 0, :], True, False,
                         [ucurB, cA], [psb], sig=False)
                    P.mm(ps[:, g * 128:(g + 1) * 128], uprev[:, g * 128:(g + 1) * 128], bands[:, g * 3 + 1, :], False, True,
                         [uprevB, cA], [psb], sig=(g == 3))
            P.cp("scalar", pooledT[:], ps[:, 0:512], [psb], [pooledB])
            ps, psb = bank()
            for g in range(4):
                P.mm(ps[:, g * 128:(g + 1) * 128], Wpool[:, g, :], pooledT[:, g * 128:(g + 1) * 128], True, True,
                     [WpoolB, pooledB], [psb], sig=(g == 3))
            P.cp("scalar", catT[k2][:, 4:8, :], ps[:, 0:512].rearrange("p (g t) -> p g t", g=4), [psb], [catB[k2]])
        def S2(b, i):
            n = 128 * (i + 1)
            if n > KSEL:
                P.emit("vector", (lambda n=n: (lambda e: e.reduce_max(out=A_ap, in_=score[:, 0:n],
                                                                      axis=mybir.AxisListType.X,
                                                                      apply_absolute_value=True)))(),
                       [scoreB], [bisB])
                P.tt("vector", score[:, i * 128:n], score[:, i * 128:n], cb_f[:], ALU.add, [scoreB, consts], [scoreB])
                P.ts("vector", bis[:, 0:2 * NITER], pw_t[:], A_ap, None, ALU.mult, None, [bisB, cA], [bisB])
                P.memset("vector", tau, 0.0, [bisB])
                for k in range(NITER):
                    P.ts("vector", MB[:, 0:n], score[:, 0:n], tau, None, ALU.is_ge, ALU.add, [scoreB, bisB], [MBB, bisB],
                         accum=cnt)
                    P.stt("vector", dcol, cnt, float(KSEL) - 0.5, steps2[:, k:k + 1], ALU.is_ge, ALU.mult, [bisB], [bisB])
                    P.stt("vector", tau, dcol, steps[:, k:k + 1], tau, ALU.subtract, ALU.add, [bisB], [bisB])
                P.ts("vector", MB[:, 0:n], score[:, 0:n], tau, -BIG, ALU.is_lt, ALU.mult, [scoreB, bisB], [MBB])
            else:
                if i > 0:
                    P.memset("vector", MB[:, 0:i * 128], 0.0, [MBB])
                P.cp("vector", MB[:, i * 128:n], cb_b[:], [consts], [MBB])

        def S3a(b, i):
            k2 = i % 2
            psO = [banks[6], banks[7]]
            psOb = [bankb[6], bankb[7]]
            units = [(j, half) for j in range(i + 1) for half in range(2)]

            def scores(j, half):
                psS, psSb = bank()
                P.mm(psS[:, 0:512], MB[:, j * 128:(j + 1) * 128], identx4[:], True, False, [MBB, consts], [psSb],
                     sig=False)
                for hh in range(4):
                    h = half * 4 + hh
                    P.mm(psS[:, hh * 128:(hh + 1) * 128], kT[:, h // 2, j * 128:(j + 1) * 128], qz[k2][:, h, :],
                         False, True, [kTB[j], qzB[k2]], [psSb], sig=(hh == 3))
                r = PT_i["i"] % 4
                PT_i["i"] += 1
                P.act(PT[r][:], psS[:, 0:512], AF.Exp, [psSb], [PTB[r]], scale=0.125)
                return r

            def pv(j, half, r):
                for hh in range(4):
                    h = half * 4 + hh
                    P.mm(psO[half][:, hh * 128:hh * 128 + 65], PT[r][:, hh * 128:(hh + 1) * 128], V[:, j, h, :],
                         (j == 0 and hh == 0), j == i, [PTB[r], VB[j]], [psOb[half]], sig=(hh == 3))

            LOOK = 2
            rq = []
            for u in range(min(LOOK, len(units))):
                rq.append(scores(*units[u]))
            for u in range(len(units)):
                if u + LOOK < len(units):
                    rq.append(scores(*units[u + LOOK]))
                pv(units[u][0], units[u][1], rq[u])

        def S3b1(b, i):
            k2 = i % 2
            psO = [banks[6], banks[7]]
            psOb = [bankb[6], bankb[7]]
            for half in range(2):
                pv = psO[half][:, 0:512].rearrange("p (h t) -> p h t", h=4)
                P.emit("vector", (lambda pv=pv, half=half: (lambda e: e.reciprocal(out=rs[:, half * 4:(half + 1) * 4],
                                                                                   in_=pv[:, :, 64])))(),
                       [psOb[half]], [rsB])
                P.tt("vector", obf[:].rearrange("p (h d) -> p h d", d=64)[:, half * 4:(half + 1) * 4, :],
                     pv[:, :, 0:64], rs[:, half * 4:(half + 1) * 4].unsqueeze(2).broadcast_to([128, 4, 64]),
                     ALU.mult, [psOb[half], rsB], [obfB])

        def S3b1b(b, i):
            k2 = i % 2
            ps, psb = bank()
            for c in range(4):
                P.tr(ps[:, c * 128:(c + 1) * 128], obf[:, c * 128:(c + 1) * 128], ident_f[:], [obfB, consts], [psb],
                     sig=(c == 3))
            P.cp("scalar", catT[k2][:, 0:4, :], ps[:, 0:512].rearrange("p (c t) -> p c t", c=4), [psb], [catB[k2]])
            yk, ykB = ybuf[k2], yB[k2]
            for half in range(2):
                ps, psb = bank()
                for kc in range(KC):
                    P.mm(ps[:, 0:512], catT[k2][:, kc, :], Wo[:, kc, half * 512:(half + 1) * 512], kc == 0, kc == KC - 1,
                         [catB[k2], WoB[kc]], [psb], sig=(kc == KC - 1))
                P.cp("scalar", yk[:, half * 512:(half + 1) * 512], ps[:, 0:512], [psb], [ykB])

        def S3b2(b, i):
            k2 = i % 2
            k3 = (b * NT + i) % 3
            mb = b % 2
            yk, ykB = ybuf[k2], yB[k2]
            P.tt("gpsimd", yk[:], yk[:], gate1[mb][:], ALU.mult, [ykB, modB[mb]], [ykB])
            P.stt("vector", yk[:], xs[k3][:], ALPHA, yk[:], ALU.mult, ALU.add, [xsB[k3], ykB], [ykB])
            layer_norm(yk, ykB, ln1g[:], ln1b[:], cA)
            P.dma("sync", x1scr_d[(b * NT + i) * 128:(b * NT + i + 1) * 128, :], yk[:], [ykB], [x1B[b][i]])

        tiles = [(b, i) for b in range(NB) for i in range(NT)]
        S1a(*tiles[0])
        stop_at(32)
        S1b(*tiles[0])
        stop_at(33)
        S2(*tiles[0])
        S1c(*tiles[0])
        stop_at(34)
        for g in range(len(tiles)):
            nxt = tiles[g + 1] if g + 1 < len(tiles) else None
            if nxt and nxt[1] == 0:
                if g > 0:
                    S3b2(*tiles[g - 1])
                S3a(*tiles[g])
                S3b1(*tiles[g])
                S1a(*nxt)
                S1b(*nxt)
                S2(*nxt)
                S1c(*nxt)
                S3b1b(*tiles[g])
                continue
            if nxt:
                S1a(*nxt)
            S3a(*tiles[g])
            S3b1(*tiles[g])
            if g > 0:
                S3b2(*tiles[g - 1])
            if nxt:
                S1b(*nxt)
                S2(*nxt)
                S1c(*nxt)
            S3b1b(*tiles[g])
        S3b2(*tiles[-1])

        stop_at(20)
        P.barrier()
        sb.reset(mA)

        Wg = sb.alloc([128, KC, DFF], BF16, "Wg")
        Wu = sb.alloc([128, KC, DFF], BF16, "Wu")
        Wd = sb.alloc([128, FC, D], BF16, "Wd")
        ln2g = sb.alloc([128, D], F32, "ln2g")
        ln2b = sb.alloc([128, D], F32, "ln2b")
        gate2 = [sb.alloc([128, D], F32, "gate2") for _ in range(2)]
        sc2 = [sb.alloc([128, KC], F32, "sc2") for _ in range(2)]
        sh2 = [sb.alloc([128, KC], F32, "sh2") for _ in range(2)]
        x1s = [sb.alloc([128, D], F32, "x1s") for _ in range(4)]
        h2T = [sb.alloc([128, KC, 256], BF16, "h2T") for _ in range(2)]
        aT = sb.alloc([128, FC, 256], BF16, "aT")
        sgt = [sb.alloc([128, 256], F32, "sgt") for _ in range(3)]
        y2 = [sb.alloc([128, D], F32, "y2") for _ in range(2)]
        lnst2 = sb.alloc([128, 24], F32, "lnst2")
        WgB = [Buf() for _ in range(KC)]
        WuB = [Buf() for _ in range(KC)]
        WdB = [Buf() for _ in range(FC)]
        cB, modB2 = Buf(), [Buf(), Buf()]
        x1sB = [Buf() for _ in range(4)]
        h2TB = [Buf(), Buf()]
        aTB = Buf()
        sgtB = [Buf() for _ in range(3)]
        y2B = [Buf(), Buf()]
        lnB2 = Buf()
        wg_v = wg_d.rearrange("(k p) c -> p k c", p=128)
        wu_v = wu_d.rearrange("(k p) c -> p k c", p=128)
        wd_v = wd_d.rearrange("(f p) c -> p f c", p=128)
        for kc in range(KC):
            P.dma("gpsimd", Wg[:, kc, :], wg_v[:, kc, :], (), [WgB[kc]])
            P.dma("gpsimd", Wu[:, kc, :], wu_v[:, kc, :], (), [WuB[kc]])
        for fc in range(FC):
            P.dma("gpsimd", Wd[:, fc, :], wd_v[:, fc, :], (), [WdB[fc]])
        P.dma("sync", ln2g[:], ln2g_d[0:1, :].partition_broadcast(128), (), [cB])
        P.dma("sync", ln2b[:], ln2b_d[0:1, :].partition_broadcast(128), (), [cB])

        def layer_norm2(src_y, yb, gam, bet, gB):
            st = lnst2[:, 0:12]
            mv = lnst2[:, 12:14]
            veps = lnst2[:, 14:15]
            rstd = lnst2[:, 15:16]
            nmr = lnst2[:, 16:17]
            P.emit("vector", lambda e: e.bn_stats(out=lnst2[:, 0:6], in_=src_y[:, 0:512]), [yb], [lnB2])
            P.emit("vector", lambda e: e.bn_stats(out=lnst2[:, 6:12], in_=src_y[:, 512:1024]), [yb], [lnB2])
            P.emit("vector", lambda e: e.bn_aggr(out=mv, in_=st), [lnB2], [lnB2])
            P.ts("vector", veps, mv[:, 1:2], EPS, None, ALU.add, None, [lnB2], [lnB2])
            P.tt("gpsimd", rstd, veps, mhalf[:], ALU.pow, [lnB2, consts], [lnB2])
            P.stt("vector", nmr, mv[:, 0:1], -1.0, rstd, ALU.mult, ALU.mult, [lnB2], [lnB2])
            P.act(src_y[:], src_y[:], AF.Identity, [yb, lnB2], [yb], bias=nmr, scale=rstd)
            P.tt("gpsimd", src_y[:], src_y[:], gam, ALU.mult, [yb, gB], [yb])
            P.tt("gpsimd", src_y[:], src_y[:], bet, ALU.add, [yb, gB], [yb])

        x1_i = {"i": 0}
        y2_i = {"i": 0}
        sg_i = {"i": 0}
        NG = NT // 2
        groups = [(b, g) for b in range(NB) for g in range(NG)]
        xts = {}

        def prep(gi):
            b, g = groups[gi]
            hk = gi % 2
            mb = b % 2
            if g == 0:
                mrow = modscr_d[b]
                P.dma("sync", sc2[mb][:], mrow[4 * D:5 * D].rearrange("(k p) -> p k", p=128), [modscr], [modB2[mb]],
                      slow=True)
                P.dma("sync", sh2[mb][:], mrow[3 * D:4 * D].rearrange("(k p) -> p k", p=128), [modscr], [modB2[mb]],
                      slow=True)
                P.dma("sync", gate2[mb][:], modscr_d[b:b + 1, 5 * D:6 * D].partition_broadcast(128), [modscr],
                      [modB2[mb]])
                P.ts("vector", sc2[mb][:], sc2[mb][:], 1.0, None, ALU.add, None, [modB2[mb]], [modB2[mb]])
            xt = []
            for tl in range(2):
                i = 2 * g + tl
                xk = x1_i["i"] % 4
                x1_i["i"] += 1
                xt.append(xk)
                P.dma("sync", x1s[xk][:], x1scr_d[(b * NT + i) * 128:(b * NT + i + 1) * 128, :], [x1B[b][i]], [x1sB[xk]])
                for half in range(2):
                    ps, psb = bank(0, 8)
                    for cc in range(4):
                        kc = half * 4 + cc
                        P.tr(ps[:, cc * 128:(cc + 1) * 128], x1s[xk][:, kc * 128:(kc + 1) * 128], ident_f[:],
                             [x1sB[xk], consts], [psb], sig=(cc == 3))
                    for cc in range(4):
                        kc = half * 4 + cc
                        P.act(h2T[hk][:, kc, tl * 128:(tl + 1) * 128], ps[:, cc * 128:(cc + 1) * 128], AF.Identity,
                              [psb, modB2[mb]], [h2TB[hk]], bias=sh2[mb][:, kc:kc + 1], scale=sc2[mb][:, kc:kc + 1])
            xts[gi] = xt

        def gateup(gi):
            hk = gi % 2
            for fc in range(FC):
                ps, psb = bank(0, 8)
                for kc in range(KC):
                    P.mm(ps[:, 0:256], Wg[:, kc, fc * 128:(fc + 1) * 128], h2T[hk][:, kc, :], kc == 0, kc == KC - 1,
                         [WgB[kc], h2TB[hk]], [psb], sig=False)
                for kc in range(KC):
                    P.mm(ps[:, 256:512], Wu[:, kc, fc * 128:(fc + 1) * 128], h2T[hk][:, kc, :], kc == 0, kc == KC - 1,
                         [WuB[kc], h2TB[hk]], [psb], sig=(kc == KC - 1))
                sk = sg_i["i"] % 3
                sg_i["i"] += 1
                P.act(sgt[sk][:], ps[:, 0:256], AF.Silu, [psb], [sgtB[sk]])
                P.tt("vector", aT[:, fc, :], sgt[sk][:], ps[:, 256:512], ALU.mult, [sgtB[sk], psb], [aTB])

        def down(gi):
            b, g = groups[gi]
            mb = b % 2
            xt = xts[gi]
            for tl in range(2):
                i = 2 * g + tl
                yk = y2_i["i"] % 2
                y2_i["i"] += 1
                for half in range(2):
                    ps, psb = bank(0, 8)
                    for fc in range(FC):
                        P.mm(ps[:, 0:512], aT[:, fc, tl * 128:(tl + 1) * 128], Wd[:, fc, half * 512:(half + 1) * 512],
                             fc == 0, fc == FC - 1, [aTB, WdB[fc]], [psb], sig=(fc == FC - 1))
                    P.tt("vector", y2[yk][:, half * 512:(half + 1) * 512], ps[:, 0:512],
                         gate2[mb][:, half * 512:(half + 1) * 512], ALU.mult, [psb, modB2[mb]], [y2B[yk]])
                P.stt("vector", y2[yk][:], x1s[xt[tl]][:], ALPHA, y2[yk][:], ALU.mult, ALU.add,
                      [x1sB[xt[tl]], y2B[yk]], [y2B[yk]])
                layer_norm2(y2[yk], y2B[yk], ln2g[:], ln2b[:], cB)
                P.dma("sync", y_d[b, i * 128:(i + 1) * 128, :], y2[yk][:], [y2B[yk]], [Buf()])

        prep(0)
        for gi in range(len(groups)):
            gateup(gi)
            if gi + 1 < len(groups):
                prep(gi + 1)
            down(gi)

    except StopBuild:
        pass
    P.finish()
    with nc.Block() as block:
        P.replay(block)
    for c in reversed(sem_ctx):
        c.__exit__(None, None, None)
    return nc, P, sb


def host_consts(S, NITER):
    inv_freq = (10000.0 ** (-np.arange(0, 64, 2, dtype=np.float32) / np.float32(64))).astype(np.float32)
    ang = (np.arange(S, dtype=np.float32)[:, None] * inv_freq[None, :]).astype(np.float32)
    cos = np.cos(ang).astype(np.float32)
    sin = np.sin(ang).astype(np.float32)
    ident = np.eye(128, dtype=np.float32)
    t = np.arange(128)
    cb = np.where(t[None, :] <= t[:, None], 0.0, -BIG).astype(np.float32)
    bands = np.zeros((12, 128, 128), np.float32)
    tp = t[:, None]
    tq = t[None, :]
    for g, w in enumerate(POOL_WINDOWS):
        bd = ((tp >= tq - w + 1) & (tp <= tq)).astype(np.float32) / w - (tp == tq)
        bp = (tp >= tq + 129 - w).astype(np.float32) / w
        cntf = np.minimum(tq + 1, w).astype(np.float32)
        bdf = ((tp >= np.maximum(tq - w + 1, 0)) & (tp <= tq)).astype(np.float32) / cntf - (tp == tq)
        bands[g * 3 + 0] = bd
        bands[g * 3 + 1] = bp
        bands[g * 3 + 2] = bdf
    pw = np.zeros((128, 2 * NITER), np.float32)
    for k in range(NITER):
        pw[:, k] = 2.0 ** -(k + 1)
        pw[:, NITER + k] = 2.0 ** -k
    return {"k_cos": cos, "k_sin": sin, "k_nsin": -sin, "k_ident": ident, "k_cb": cb, "k_bands": bands, "k_pw": pw}


_CACHE = {}


def run(inputs, n_cores, NB, S, KSEL, NITER):
    key = (NB, S, KSEL, NITER)
    if key not in _CACHE:
        _CACHE[key] = build_program(NB, S, KSEL, NITER)[0]
    nc = _CACHE[key]
    f = lambda a: np.ascontiguousarray(np.asarray(a, dtype=np.float32))
    shared = {
        "w_mod": f(inputs["w_mod"][0]), "b_mod": f(inputs["b_mod"][0]).reshape(1, -1), "w_in": f(inputs["w_in"][0]),
        "w_pool": f(inputs["w_pool"][0]), "pool_scale": f(inputs["pool_scale"][0]), "w_o": f(inputs["w_o"][0]),
        "ln1_g": f(inputs["ln1_g"][0]).reshape(1, -1), "ln1_b": f(inputs["ln1_b"][0]).reshape(1, -1),
        "w_gate": f(inputs["w_gate"][0]), "w_up": f(inputs["w_up"][0]), "w_down": f(inputs["w_down"][0]),
        "ln2_g": f(inputs["ln2_g"][0]).reshape(1, -1), "ln2_b": f(inputs["ln2_b"][0]).reshape(1, -1),
    }
    shared.update(host_consts(S, NITER))
    x = f(inputs["x"])
    c = f(inputs["c"])
    in_maps = []
    for k in range(n_cores):
        m = dict(shared)
        m["x"] = np.ascontiguousarray(x[k * NB:(k + 1) * NB])
        m["c"] = np.ascontiguousarray(c[k * NB:(k + 1) * NB])
        in_maps.append(m)
    res = run_bass_kernel_spmd(nc, in_maps, core_ids=list(range(n_cores)))
    return np.concatenate([np.asarray(r["y"]) for r in res.results], axis=0).astype(np.float32)


def kernel(x, c, w_mod, b_mod, w_in, w_pool, pool_scale, w_o, ln1_g, ln1_b, w_gate, w_up, w_down, ln2_g, ln2_b):
    inputs = dict(x=x, c=c, w_mod=w_mod, b_mod=b_mod, w_in=w_in, w_pool=w_pool, pool_scale=pool_scale, w_o=w_o,
                  ln1_g=ln1_g, ln1_b=ln1_b, w_gate=w_gate, w_up=w_up, w_down=w_down, ln2_g=ln2_g, ln2_b=ln2_b)
    return run(inputs, N_CORES, 4, 2048, 256, 9)
```
